# Optimizing a Trainium2 kernel written in Bass

```python
import math
import jax
import jax.numpy as jnp
from jax import lax
import numpy as np

D_MODEL = 2048
BATCH = 8
SEQ = 2048
DEPTH = 4

CTX_LEN = 256
GRID_W = 64
N_MIXERS = 4
MIXER_RET = 0
MIXER_WIN = 1
MIXER_GDN = 2
MIXER_SGU = 3
NORM_EPS = 1e-6
ROPE_BASE = 10000.0
MLP_HIDDEN = 4 * D_MODEL

RET_HEADS = 8
RET_DK = D_MODEL // RET_HEADS
RET_DV = 2 * RET_DK
RET_CHUNK = 128

ATT_HEADS = 16
ATT_KV_HEADS = 4
ATT_HD = D_MODEL // ATT_HEADS
ATT_GROUP = ATT_HEADS // ATT_KV_HEADS
WINDOW = 128
ATT_BLOCK = 128

GDN_K_HEADS = 16
GDN_V_HEADS = 32
GDN_DK = 128
GDN_DV = 128
GDN_CONV = 4
GDN_CONV_LEFT = GDN_CONV // 2
GDN_CONV_RIGHT = GDN_CONV - 1 - GDN_CONV_LEFT
GDN_CHUNK = 64

SGU_CHUNK = 128
SGU_GROUPS = 8
SGU_WIDTH = 2 * D_MODEL

kernel_name = 'hybrid_interleaved_diffusion_trunk'


def rms_norm(x, g):
    xf = x.astype(jnp.float32)
    y = xf * lax.rsqrt(jnp.mean(xf * xf, axis=-1, keepdims=True) + NORM_EPS)
    return (y * g.astype(jnp.float32)).astype(x.dtype)


def head_rms(x):
    xf = x.astype(jnp.float32)
    return xf * lax.rsqrt(jnp.mean(xf * xf, axis=-1, keepdims=True) + NORM_EPS)


def layer_norm(x, g, b):
    xf = x.astype(jnp.float32)
    mu = jnp.mean(xf, axis=-1, keepdims=True)
    var = jnp.mean(jnp.square(xf - mu), axis=-1, keepdims=True)
    y = (xf - mu) * lax.rsqrt(var + NORM_EPS) * g.astype(jnp.float32) + b.astype(jnp.float32)
    return y.astype(x.dtype)


def l2_normalize(x):
    xf = x.astype(jnp.float32)
    return (xf * lax.rsqrt(jnp.sum(xf * xf, axis=-1, keepdims=True) + 1e-6)).astype(x.dtype)


def flip_seq(t):
    return jnp.flip(t, axis=2)


def axial_rope_tables(rows, head_dim):
    row = jnp.repeat(jnp.arange(rows, dtype=jnp.float32), GRID_W)
    col = jnp.tile(jnp.arange(GRID_W, dtype=jnp.float32), rows)
    axis_dim = head_dim // 2
    inv_freq = jnp.exp(-math.log(ROPE_BASE) * jnp.arange(0, axis_dim, 2, dtype=jnp.float32) / axis_dim)
    ang = jnp.concatenate([row[:, None] * inv_freq, col[:, None] * inv_freq], axis=-1)
    return jnp.cos(ang), jnp.sin(ang)


def apply_rope(x, cos, sin):
    xf = x.astype(jnp.float32)
    x1 = xf[..., 0::2]
    x2 = xf[..., 1::2]
    cs = cos[None, :, None, :]
    sn = sin[None, :, None, :]
    y = jnp.stack([x1 * cs - x2 * sn, x1 * sn + x2 * cs], axis=-1).reshape(x.shape)
    return y.astype(x.dtype)


def ada_params(cond, w, b):
    m = jax.nn.silu(cond) @ w + b
    return jnp.split(m[:, None, :], 6, axis=-1)


def modulate(h, shift, scale):
    return h * (1.0 + scale) + shift


def sq_relu_mlp(h, w_up, w_down):
    a = jax.nn.relu(h @ w_up)
    return (a * a) @ w_down


def ctx_feeds_later(i):
    return any(k % N_MIXERS != MIXER_SGU for k in range(i + 1, DEPTH))


def retention_chunked(q, k, v, s0):
    b, h, s, _ = q.shape
    dv = v.shape[-1]
    n = s // RET_CHUNK
    lg = jnp.log1p(-jnp.exp2(-5.0 - jnp.arange(RET_HEADS, dtype=jnp.float32)))[:, None]
    pos = jnp.arange(RET_CHUNK, dtype=jnp.float32)
    diff = pos[:, None] - pos[None, :]
    intra = jnp.where(diff >= 0, jnp.exp(lg[:, :, None] * jnp.maximum(diff, 0.0)), 0.0)
    q_decay = jnp.exp(lg * (pos + 1.0))[:, :, None]
    k_decay = jnp.exp(lg * (RET_CHUNK - 1.0 - pos))[:, :, None]
    c_decay = jnp.exp(lg * RET_CHUNK)[:, :, None]

    def chunks(t):
        return jnp.moveaxis(t.astype(jnp.float32).reshape(b, h, n, RET_CHUNK, t.shape[-1]), 2, 0)

    def step(state, inp):
        qi, ki, vi = inp
        scores = jnp.einsum('bhcd,bhmd->bhcm', qi, ki) * intra
        out = (jnp.einsum('bhcm,bhmv->bhcv', scores, vi)
               + jnp.einsum('bhcd,bhdv->bhcv', qi * q_decay, state))
        state = state * c_decay + jnp.einsum('bhcd,bhcv->bhdv', ki * k_decay, vi)
        return state, out

    state, out = lax.scan(step, s0, (chunks(q), chunks(k), chunks(v)))
    return jnp.moveaxis(out, 0, 2).reshape(b, h, s, dv), state


def retention_mixer(h_lat, h_ctx, w_in, gn_g, w_out, cos, sin, want_ctx):
    qk_w = RET_HEADS * RET_DK
    v_w = RET_HEADS * RET_DV

    def project(h, rotate):
        bsz, n, _ = h.shape
        q, k, v, gate = jnp.split(h @ w_in, [qk_w, 2 * qk_w, 2 * qk_w + v_w], axis=-1)
        q = q.reshape(bsz, n, RET_HEADS, RET_DK)
        k = k.reshape(bsz, n, RET_HEADS, RET_DK) * RET_DK ** -0.5
        if rotate:
            q = apply_rope(q, cos, sin)
            k = apply_rope(k, cos, sin)
        v = v.reshape(bsz, n, RET_HEADS, RET_DV)
        return q.transpose(0, 2, 1, 3), k.transpose(0, 2, 1, 3), v.transpose(0, 2, 1, 3), gate

    def finish(o, gate):
        bsz, _, n, _ = o.shape
        o = head_rms(o.transpose(0, 2, 1, 3)).reshape(bsz, n, v_w) * gn_g.astype(jnp.float32)
        return (o.astype(gate.dtype) * jax.nn.silu(gate)) @ w_out

    qc, kc, vc, gc = project(h_ctx, False)
    zero = jnp.zeros((qc.shape[0], RET_HEADS, RET_DK, RET_DV), jnp.float32)
    oc_f, sc_f = retention_chunked(qc, kc, vc, zero)
    oc_b, sc_b = retention_chunked(flip_seq(qc), flip_seq(kc), flip_seq(vc), zero)
    ql, kl, vl, gl = project(h_lat, True)
    ol_f, _ = retention_chunked(ql, kl, vl, sc_f)
    ol_b, _ = retention_chunked(flip_seq(ql), flip_seq(kl), flip_seq(vl), sc_b)
    out_lat = finish(ol_f + flip_seq(ol_b), gl)
    out_ctx = finish(oc_f + flip_seq(oc_b), gc) if want_ctx else None
    return out_lat, out_ctx


def window_attention_mixer(h_lat, h_ctx, w_in, sink, w_out, cos, sin, want_ctx):
    q_w = ATT_HEADS * ATT_HD
    kv_w = ATT_KV_HEADS * ATT_HD
    scale = ATT_HD ** -0.5
    sink_l = sink.astype(jnp.float32).reshape(ATT_KV_HEADS, ATT_GROUP)

    def project(h, rotate):
        bsz, n, _ = h.shape
        q, k, v = jnp.split(h @ w_in, [q_w, q_w + kv_w], axis=-1)
        q = q.reshape(bsz, n, ATT_HEADS, ATT_HD)
        k = k.reshape(bsz, n, ATT_KV_HEADS, ATT_HD)
        v = v.reshape(bsz, n, ATT_KV_HEADS, ATT_HD)
        if rotate:
            q = apply_rope(q, cos, sin)
            k = apply_rope(k, cos, sin)
        return q.reshape(bsz, n, ATT_KV_HEADS, ATT_GROUP, ATT_HD) * scale, k, v

    def softmax_with_sink(logits):
        sink_col = jnp.broadcast_to(sink_l[None, :, :, None, None], logits.shape[:-1] + (1,))
        p = jax.nn.softmax(jnp.concatenate([logits, sink_col], axis=-1), axis=-1)
        return p[..., :-1]

    qc, kc, vc = project(h_ctx, False)
    ql, kl, vl = project(h_lat, True)
    bsz, n = ql.shape[0], ql.shape[1]
    n_blocks = n // ATT_BLOCK
    span = ATT_BLOCK + 2 * WINDOW
    pad = ((0, 0), (WINDOW, WINDOW), (0, 0), (0, 0))
    kp = jnp.pad(kl, pad)
    vp = jnp.pad(vl, pad)

    def block(bi):
        start = bi * ATT_BLOCK
        qb = lax.dynamic_slice_in_dim(ql, start, ATT_BLOCK, axis=1)
        kb = lax.dynamic_slice_in_dim(kp, start, span, axis=1)
        vb = lax.dynamic_slice_in_dim(vp, start, span, axis=1)
        qpos = start + jnp.arange(ATT_BLOCK)
        kpos = start - WINDOW + jnp.arange(span)
        valid = ((jnp.abs(qpos[:, None] - kpos[None, :]) <= WINDOW)
                 & (kpos >= 0)[None, :] & (kpos < n)[None, :])
        s_lat = jnp.einsum('bqkgd,bmkd->bkgqm', qb, kb).astype(jnp.float32)
        s_lat = jnp.where(valid, s_lat, -jnp.inf)
        s_ctx = jnp.einsum('bqkgd,bmkd->bkgqm', qb, kc).astype(jnp.float32)
        p = softmax_with_sink(jnp.concatenate([s_lat, s_ctx], axis=-1)).astype(vb.dtype)
        return (jnp.einsum('bkgqm,bmkd->bqkgd', p[..., :span], vb)
                + jnp.einsum('bkgqm,bmkd->bqkgd', p[..., span:], vc))

    o = lax.map(block, jnp.arange(n_blocks))
    o = jnp.moveaxis(o, 0, 1).reshape(bsz, n, q_w)
    out_lat = o @ w_out
    out_ctx = None
    if want_ctx:
        s = jnp.einsum('bqkgd,bmkd->bkgqm', qc, kc).astype(jnp.float32)
        p = softmax_with_sink(s).astype(vc.dtype)
        oc = jnp.einsum('bkgqm,bmkd->bqkgd', p, vc).reshape(bsz, qc.shape[1], q_w)
        out_ctx = oc @ w_out
    return out_lat, out_ctx


def short_conv(x, w):
    return lax.conv_general_dilated(
        x, w[:, None, :].astype(x.dtype), window_strides=(1,),
        padding=[(GDN_CONV_LEFT, GDN_CONV_RIGHT)],
        dimension_numbers=('NWC', 'WIO', 'NWC'), feature_group_count=x.shape[-1])


def gated_delta_chunked(q, k, v, g, beta, s0):
    b, h, s, _ = q.shape
    dv = v.shape[-1]
    C = GDN_CHUNK
    n = s // C
    f32 = jnp.float32
    q, k, v = [t.astype(f32).reshape(b, h, n, C, t.shape[-1]) for t in (q, k, v)]
    beta = beta.astype(f32).reshape(b, h, n, C, 1)
    gc = jnp.cumsum(g.astype(f32).reshape(b, h, n, C), axis=-1)
    idx = jnp.arange(C)
    lower = idx[:, None] >= idx[None, :]
    strict = idx[:, None] > idx[None, :]
    decay = jnp.exp(jnp.where(lower, gc[..., :, None] - gc[..., None, :], -jnp.inf))
    kb = k * beta
    lmat = jnp.where(strict, jnp.einsum('bhncd,bhnmd->bhncm', kb, k) * decay, 0.0)
    eye = jnp.eye(C, dtype=f32)
    tmat = lax.linalg.triangular_solve(lmat + eye, jnp.broadcast_to(eye, lmat.shape),
                                       left_side=True, lower=True, unit_diagonal=True)
    u = jnp.einsum('bhncm,bhnmv->bhncv', tmat, v * beta)
    w = jnp.einsum('bhncm,bhnmd->bhncd', tmat, kb * jnp.exp(gc)[..., None])

    def step(state, inp):
        qi, ki, ui, wi, gi, di = inp
        v_new = ui - jnp.einsum('bhcd,bhdv->bhcv', wi, state)
        attn = jnp.einsum('bhcd,bhmd->bhcm', qi, ki) * di
        out = (jnp.einsum('bhcd,bhdv->bhcv', qi * jnp.exp(gi)[..., None], state)
               + jnp.einsum('bhcm,bhmv->bhcv', attn, v_new))
        g_last = gi[..., -1:]
        state = (state * jnp.exp(g_last)[..., None]
                 + jnp.einsum('bhcd,bhcv->bhdv', ki * jnp.exp(g_last - gi)[..., None], v_new))
        return state, out

    xs = tuple(jnp.moveaxis(t, 2, 0) for t in (q, k, u, w, gc, decay))
    state, out = lax.scan(step, s0.astype(f32), xs)
    return jnp.moveaxis(out, 0, 2).reshape(b, h, s, dv), state


def gdn_mixer(h_lat, h_ctx, w_in, conv_w, a_log, dt_bias, norm_g, w_out, want_ctx):
    qk_w = GDN_K_HEADS * GDN_DK
    v_w = GDN_V_HEADS * GDN_DV
    conv_ch = 2 * qk_w + v_w
    rep = GDN_V_HEADS // GDN_K_HEADS
    f32 = jnp.float32

    def project(h):
        bsz, n, _ = h.shape
        mixed, z, b_raw, a_raw = jnp.split(
            h @ w_in, [conv_ch, conv_ch + v_w, conv_ch + v_w + 2 * GDN_V_HEADS], axis=-1)
        mixed = jax.nn.silu(short_conv(mixed, conv_w))
        q, k, v = jnp.split(mixed, [qk_w, 2 * qk_w], axis=-1)

        def heads(t, nh, d):
            return t.reshape(bsz, n, nh, d).transpose(0, 2, 1, 3)

        q = jnp.repeat(l2_normalize(heads(q, GDN_K_HEADS, GDN_DK)), rep, axis=1) * GDN_DK ** -0.5
        k = jnp.repeat(l2_normalize(heads(k, GDN_K_HEADS, GDN_DK)), rep, axis=1)
        v = heads(v, GDN_V_HEADS, GDN_DV)
        beta = jax.nn.sigmoid(b_raw.astype(f32)).reshape(bsz, n, 2, GDN_V_HEADS).transpose(2, 0, 3, 1)
        a_dir = a_raw.astype(f32).reshape(bsz, n, 2, GDN_V_HEADS).transpose(2, 0, 3, 1)
        g = -jnp.exp(a_log.astype(f32))[:, None, :, None] * jax.nn.softplus(
            a_dir + dt_bias.astype(f32)[:, None, :, None])
        return q, k, v, beta, g, z

    def bidir(q, k, v, beta, g, s_f, s_b):
        o_f, s_f = gated_delta_chunked(q, k, v, g[0], beta[0], s_f)
        o_b, s_b = gated_delta_chunked(flip_seq(q), flip_seq(k), flip_seq(v),
                                       flip_seq(g[1]), flip_seq(beta[1]), s_b)
        return o_f + flip_seq(o_b), s_f, s_b

    def finish(o, z):
        bsz, _, n, _ = o.shape
        o = head_rms(o.transpose(0, 2, 1, 3)) * norm_g.astype(f32)
        o = o.astype(z.dtype) * jax.nn.silu(z.reshape(bsz, n, GDN_V_HEADS, GDN_DV))
        return o.reshape(bsz, n, v_w) @ w_out

    qc, kc, vc, bc, gcx, zc = project(h_ctx)
    zero = jnp.zeros((qc.shape[0], GDN_V_HEADS, GDN_DK, GDN_DV), f32)
    oc, sc_f, sc_b = bidir(qc, kc, vc, bc, gcx, zero, zero)
    ql, kl, vl, bl, gl, zl = project(h_lat)
    ol, _, _ = bidir(ql, kl, vl, bl, gl, sc_f, sc_b)
    out_lat = finish(ol, zl)
    out_ctx = finish(oc, zc) if want_ctx else None
    return out_lat, out_ctx


def sgu_mixer(h, w_in, ln_g, ln_b, w_s, b_s, w_out):
    bsz, n, _ = h.shape
    nch = n // SGU_CHUNK
    z = jax.nn.gelu(h @ w_in, approximate=False)
    u, v = jnp.split(z, 2, axis=-1)
    v = layer_norm(v, ln_g, ln_b)
    v = v.reshape(bsz, nch, SGU_CHUNK, SGU_GROUPS, SGU_WIDTH // SGU_GROUPS)
    v = jnp.einsum('gpq,bnqgc->bnpgc', w_s, v) + b_s.T[None, None, :, :, None]
    return (u * v.reshape(bsz, n, SGU_WIDTH)) @ w_out


def setup_inputs(seed: int = 0) -> dict:
    key = jax.random.key(seed)
    ks = iter(jax.random.split(key, 40))
    f32 = jnp.float32
    D = D_MODEL

    def normal(shape, scale):
        return jax.random.normal(next(ks), shape, f32) * scale

    def gain(shape):
        return 1.0 + normal(shape, 0.1)

    n_a, n_b, n_c, n_d = [len(range(m, DEPTH, N_MIXERS)) for m in range(N_MIXERS)]
    ret_in_w = 2 * RET_HEADS * RET_DK + 2 * RET_HEADS * RET_DV
    att_in_w = (ATT_HEADS + 2 * ATT_KV_HEADS) * ATT_HD
    gdn_conv_ch = 2 * GDN_K_HEADS * GDN_DK + GDN_V_HEADS * GDN_DV
    gdn_in_w = gdn_conv_ch + GDN_V_HEADS * GDN_DV + 4 * GDN_V_HEADS
    dt = jnp.exp(jax.random.uniform(next(ks), (n_c, 2, GDN_V_HEADS), f32,
                                    math.log(1e-3), math.log(1e-1)))
    a_init = jax.random.uniform(next(ks), (n_c, 2, GDN_V_HEADS), f32, 1.0, 16.0)
    return {
        'x': normal((BATCH, SEQ, D), 1.0),
        'c': normal((BATCH, D), 1.0),
        'ctx': normal((BATCH, CTX_LEN, D), 1.0),
        'c_ctx': normal((D,), 1.0),
        'mod_w': normal((DEPTH, D, 6 * D), 0.5 * D ** -0.5),
        'mod_b': normal((DEPTH, 6 * D), 0.02),
        'norm1_g': gain((DEPTH, D)),
        'norm2_g': gain((DEPTH, D)),
        'mlp_up': normal((DEPTH, D, MLP_HIDDEN), D ** -0.5),
        'mlp_down': normal((DEPTH, MLP_HIDDEN, D), MLP_HIDDEN ** -0.5),
        'final_g': gain((D,)),
        'ret_w_in': normal((n_a, D, ret_in_w), D ** -0.5),
        'ret_gn_g': gain((n_a, RET_HEADS * RET_DV)),
        'ret_w_out': normal((n_a, RET_HEADS * RET_DV, D), (RET_HEADS * RET_DV) ** -0.5),
        'att_w_in': normal((n_b, D, att_in_w), D ** -0.5),
        'att_sink': normal((n_b, ATT_HEADS), 1.0),
        'att_w_out': normal((n_b, ATT_HEADS * ATT_HD, D), (ATT_HEADS * ATT_HD) ** -0.5),
        'gdn_w_in': normal((n_c, D, gdn_in_w), D ** -0.5),
        'gdn_conv_w': normal((n_c, GDN_CONV, gdn_conv_ch), GDN_CONV ** -0.5),
        'gdn_a_log': jnp.log(a_init),
        'gdn_dt_bias': dt + jnp.log(-jnp.expm1(-dt)),
        'gdn_norm_g': gain((n_c, GDN_DV)),
        'gdn_w_out': normal((n_c, GDN_V_HEADS * GDN_DV, D), (GDN_V_HEADS * GDN_DV) ** -0.5),
        'sgu_w_in': normal((n_d, D, 2 * SGU_WIDTH), D ** -0.5),
        'sgu_ln_g': gain((n_d, SGU_WIDTH)),
        'sgu_ln_b': normal((n_d, SGU_WIDTH), 0.02),
        'sgu_w_s': normal((n_d, SGU_GROUPS, SGU_CHUNK, SGU_CHUNK), SGU_CHUNK ** -0.5),
        'sgu_b_s': gain((n_d, SGU_GROUPS, SGU_CHUNK)),
        'sgu_w_out': normal((n_d, SGU_WIDTH, D), SGU_WIDTH ** -0.5),
    }


def reference(x, c, ctx, c_ctx, mod_w, mod_b, norm1_g, norm2_g, mlp_up, mlp_down, final_g,
              ret_w_in, ret_gn_g, ret_w_out, att_w_in, att_sink, att_w_out,
              gdn_w_in, gdn_conv_w, gdn_a_log, gdn_dt_bias, gdn_norm_g, gdn_w_out,
              sgu_w_in, sgu_ln_g, sgu_ln_b, sgu_w_s, sgu_b_s, sgu_w_out):
    ROWS = x.shape[1] // GRID_W
    cos_r, sin_r = axial_rope_tables(ROWS, RET_DK)
    cos_a, sin_a = axial_rope_tables(ROWS, ATT_HD)
    for i in range(DEPTH):
        kind = i % N_MIXERS
        j = i // N_MIXERS
        want_ctx = ctx_feeds_later(i)
        reads_ctx = kind != MIXER_SGU or want_ctx
        sh1, sc1, g1, sh2, sc2, g2 = ada_params(c, mod_w[i], mod_b[i])
        h_lat = modulate(rms_norm(x, norm1_g[i]), sh1, sc1)
        h_ctx = None
        if reads_ctx:
            csh1, csc1, cg1, csh2, csc2, cg2 = ada_params(c_ctx[None, :], mod_w[i], mod_b[i])
            h_ctx = modulate(rms_norm(ctx, norm1_g[i]), csh1, csc1)
        if kind == MIXER_RET:
            o_lat, o_ctx = retention_mixer(h_lat, h_ctx, ret_w_in[j], ret_gn_g[j], ret_w_out[j],
                                           cos_r, sin_r, want_ctx)
        elif kind == MIXER_WIN:
            o_lat, o_ctx = window_attention_mixer(h_lat, h_ctx, att_w_in[j], att_sink[j], att_w_out[j],
                                                  cos_a, sin_a, want_ctx)
        elif kind == MIXER_GDN:
            o_lat, o_ctx = gdn_mixer(h_lat, h_ctx, gdn_w_in[j], gdn_conv_w[j], gdn_a_log[j],
                                     gdn_dt_bias[j], gdn_norm_g[j], gdn_w_out[j], want_ctx)
        else:
            o_lat = sgu_mixer(h_lat, sgu_w_in[j], sgu_ln_g[j], sgu_ln_b[j], sgu_w_s[j], sgu_b_s[j],
                              sgu_w_out[j])
            o_ctx = None
            if want_ctx:
                o_ctx = sgu_mixer(h_ctx, sgu_w_in[j], sgu_ln_g[j], sgu_ln_b[j], sgu_w_s[j],
                                  sgu_b_s[j], sgu_w_out[j])
        x = x + g1 * o_lat
        x = x + g2 * sq_relu_mlp(modulate(rms_norm(x, norm2_g[i]), sh2, sc2), mlp_up[i], mlp_down[i])
        if want_ctx:
            ctx = ctx + cg1 * o_ctx
            ctx = ctx + cg2 * sq_relu_mlp(modulate(rms_norm(ctx, norm2_g[i]), csh2, csc2),
                                          mlp_up[i], mlp_down[i])
    return rms_norm(x, final_g)
```

```python
import contextlib
import math
import os
import numpy as np
import concourse.bass as bass
import concourse.mybir as mybir
from concourse.bass_utils import run_bass_kernel_spmd

F32 = mybir.dt.float32
BF16 = mybir.dt.bfloat16
AF = mybir.ActivationFunctionType
ALU = mybir.AluOpType
AX = mybir.AxisListType

D = 2048
KC = 16
NT = 18
NCTX = 2
EPS = 1e-6
HID = 8192


class View:
    __slots__ = ("tl", "ap", "key")

    def __init__(s, tl, ap, key="*"):
        s.tl = tl
        s.ap = ap
        s.key = key

    def k(s, key):
        return View(s.tl, s.ap, key)


class Tl:
    def __init__(s, base, name):
        s.base = base
        s.name = name
        s.st = {}
        s.psum = False

    def __getitem__(s, idx):
        return View(s, s.base[idx])

    def v(s, ap, key="*"):
        return View(s, ap, key)


class _Sub:
    def __init__(s, tl, t):
        s.tl = tl
        s.t = t

    def __getitem__(s, idx):
        return View(s.tl, s.tl.base[:, s.t, :][idx])


class Ring:
    def __init__(s, tls):
        s.tls = tls
        s.i = 0

    def next(s):
        t = s.tls[s.i % len(s.tls)]
        s.i += 1
        return t


class Bld:
    def __init__(s):
        s.nc = bass.Bass("TRN2", target_bir_lowering=False)
        nc = s.nc
        s.es = contextlib.ExitStack()
        s.E = {"pe": nc.tensor, "act": nc.scalar, "dve": nc.vector, "pool": nc.gpsimd, "sp": nc.sync}
        s.sem = {}
        s.cnt = {}
        for e in ["pe", "act", "dve", "pool"]:
            s.sem[e] = s.es.enter_context(nc.semaphore("s_" + e))
            s.cnt[e] = 0
        s.dq = {}
        for q, n in [("sp", 10), ("pool", 16), ("act", 2)]:
            names = []
            for i in range(n):
                nm = "d_%s%d" % (q, i)
                s.sem[nm] = s.es.enter_context(nc.semaphore(nm))
                s.cnt[nm] = 0
                names.append(nm)
            s.dq[q] = names
        s.dqi = {"sp": 0, "pool": 0, "act": 0}
        s.known = {e: {} for e in s.E}
        s.ninst = 0
        s.uid = 0

    def sb(s, shape, dt, name=None, stack=None):
        s.uid += 1
        nm = "%s_%d" % (name or "t", s.uid)
        t = (stack or s.es).enter_context(s.nc.sbuf_tensor(nm, list(shape), dt))
        return Tl(t, nm)

    def ps(s, shape, dt, name=None):
        s.uid += 1
        nm = "%s_%d" % (name or "p", s.uid)
        t = s.es.enter_context(s.nc.psum_tensor(nm, list(shape), dt))
        tl = Tl(t, nm)
        tl.psum = True
        return tl

    def dram(s, name, shape, dt, kind="Internal"):
        t = s.nc.dram_tensor(name, list(shape), dt, kind=kind)
        return Tl(t.ap(), name)

    def _states(s, v):
        st = v.tl.st
        if "*" not in st:
            st["*"] = [{}, {}]
        if v.key == "*":
            return list(st.values())
        keys = v.key if isinstance(v.key, list) else [v.key]
        out = [st["*"]]
        for k in keys:
            if k not in st:
                st[k] = [{}, {}]
            out.append(st[k])
        return out

    def _need(s, eng, reads, writes, is_dma):
        need = {}

        def mg(d, skip):
            for src, val in d.items():
                if skip and src == eng:
                    continue
                if need.get(src, 0) < val:
                    need[src] = val

        for v in reads:
            for st in s._states(v):
                mg(st[0], False)
                if v.tl.psum:
                    mg(st[1], True)
        for v in writes:
            for st in s._states(v):
                mg(st[0], not is_dma)
                mg(st[1], not is_dma)
        if eng == "pe":
            need.pop("pe", None)
        return need

    def _wait(s, eng, need):
        kn = s.known[eng]
        for src, val in need.items():
            if kn.get(src, 0) < val:
                s.E[eng].wait_ge(s.sem[src], val)
                kn[src] = val
                s.ninst += 1

    def _upd(s, src, val, reads, writes):
        for v in reads:
            st = v.tl.st
            if v.key == "*":
                st["*"][1][src] = val
            else:
                for k in v.key if isinstance(v.key, list) else [v.key]:
                    st[k][1][src] = val
        for v in writes:
            st = v.tl.st
            if v.key == "*":
                st.clear()
                st["*"] = [{src: val}, {}]
            else:
                for k in v.key if isinstance(v.key, list) else [v.key]:
                    st[k] = [{src: val}, {}]

    def op(s, eng, fn, reads, writes):
        need = s._need(eng, reads, writes, False)
        s._wait(eng, need)
        inst = fn()
        s.cnt[eng] += 1
        inst.then_inc(s.sem[eng], 1)
        s.ninst += 1
        s._upd(eng, s.cnt[eng], reads, writes)
        return inst

    def dma(s, q, out, in_, **kw):
        need = s._need(q, [in_], [out], True)
        ring = s.dq[q]
        nm = ring[s.dqi[q] % len(ring)]
        s.dqi[q] += 1
        if s.cnt[nm] > 0:
            need[nm] = max(need.get(nm, 0), s.cnt[nm])
        s._wait(q, need)
        inst = s.E[q].dma_start(out=out.ap, in_=in_.ap, **kw)
        s.cnt[nm] += 16
        inst.then_inc(s.sem[nm], 16)
        s.ninst += 1
        s._upd(nm, s.cnt[nm], [in_], [out])

    def barrier(s, engines=None):
        for e in engines or list(s.E):
            need = {src: val for src, val in s.cnt.items() if val > 0}
            if e == "pe":
                need.pop("pe", None)
            s._wait(e, need)

    def mm(s, out, lhsT, rhs, start, stop):
        return s.op("pe", lambda: s.nc.tensor.matmul(out.ap, lhsT.ap, rhs.ap, start=start, stop=stop),
                    [lhsT, rhs], [out])

    def tr(s, out, in_, ident):
        return s.op("pe", lambda: s.nc.tensor.transpose(out.ap, in_.ap, ident.ap), [in_, ident], [out])

    def act(s, out, in_, func, bias=None, scale=None, accum=None, eng="act"):
        kw = {}
        reads = [in_]
        writes = [out]
        if bias is not None:
            if isinstance(bias, View):
                kw["bias"] = bias.ap
                reads.append(bias)
            else:
                kw["bias"] = bias
        if scale is not None:
            if isinstance(scale, View):
                kw["scale"] = scale.ap
                reads.append(scale)
            else:
                kw["scale"] = scale
        if accum is not None:
            kw["accum_out"] = accum.ap
            writes.append(accum)
        return s.op("act", lambda: s.nc.scalar.activation(out.ap, in_.ap, func, **kw), reads, writes)

    def tt(s, eng, out, in0, in1, op):
        return s.op(eng, lambda: s.E[eng].tensor_tensor(out.ap, in0.ap, in1.ap, op), [in0, in1], [out])

    def ts(s, eng, out, in0, s1, s2, op0, op1=None):
        reads = [in0]
        a1 = s1
        a2 = s2
        if isinstance(s1, View):
            reads.append(s1)
            a1 = s1.ap
        if isinstance(s2, View):
            reads.append(s2)
            a2 = s2.ap
        if op1 is None:
            return s.op(eng, lambda: s.E[eng].tensor_scalar(out.ap, in0.ap, a1, None, op0), reads, [out])
        return s.op(eng, lambda: s.E[eng].tensor_scalar(out.ap, in0.ap, a1, a2, op0, op1), reads, [out])

    def stt(s, out, in0, scalar, in1, op0, op1, eng="dve"):
        reads = [in0, in1]
        a = scalar
        if isinstance(scalar, View):
            reads.append(scalar)
            a = scalar.ap
        return s.op(eng, lambda: s.E[eng].scalar_tensor_tensor(out.ap, in0.ap, a, in1.ap, op0, op1), reads, [out])

    def copy(s, eng, out, in_):
        if eng == "act":
            return s.op("act", lambda: s.nc.scalar.copy(out.ap, in_.ap), [in_], [out])
        return s.op(eng, lambda: s.E[eng].tensor_copy(out.ap, in_.ap), [in_], [out])

    def memset(s, eng, out, val):
        return s.op(eng, lambda: s.E[eng].memset(out.ap, val), [], [out])

    def recip(s, out, in_):
        return s.op("dve", lambda: s.nc.vector.reciprocal(out.ap, in_.ap), [in_], [out])

    def rmax(s, out, in_):
        return s.op("dve", lambda: s.nc.vector.reduce_max(out.ap, in_.ap, axis=AX.X), [in_], [out])


GROUPS = [[0, 1], [2, 3, 4, 5, 6, 7], [8, 9, 10, 11, 12, 13], [14, 15, 16, 17]]
WSPEC = {
    0: [("ret_w_in", 2048, 12288), ("ret_w_out", 4096, 2048)],
    1: [("att_w_in", 2048, 3072), ("att_w_out", 2048, 2048)],
    2: [("gdn_w_in", 2048, 12416), ("gdn_w_out", 4096, 2048)],
    3: [("sgu_w_in", 2048, 8192), ("sgu_w_out", 4096, 2048)],
}
IN_SHAPES = {
    "mod_w": [4, 2048, 12288], "mod_b": [4, 12288], "norm1_g": [4, 2048], "norm2_g": [4, 2048],
    "mlp_up": [4, 2048, 8192], "mlp_down": [4, 8192, 2048], "final_g": [2048],
    "ret_w_in": [1, 2048, 12288], "ret_gn_g": [1, 4096], "ret_w_out": [1, 4096, 2048],
    "att_w_in": [1, 2048, 3072], "att_sink": [1, 16], "att_w_out": [1, 2048, 2048],
    "gdn_w_in": [1, 2048, 12416], "gdn_conv_w": [1, 4, 8192], "gdn_a_log": [1, 2, 32],
    "gdn_dt_bias": [1, 2, 32], "gdn_norm_g": [1, 128], "gdn_w_out": [1, 4096, 2048],
    "sgu_w_in": [1, 2048, 8192], "sgu_ln_g": [1, 4096], "sgu_ln_b": [1, 4096],
    "sgu_w_s": [1, 8, 128, 128], "sgu_b_s": [1, 8, 128], "sgu_w_out": [1, 4096, 2048],
}


class Model(Bld):
    def __init__(s, layers=(0, 1, 2, 3), final=True, consts=None):
        super().__init__()
        s.layers = list(layers)
        s.I = {}
        s.I["xin"] = s.dram("xin", [NT * 128, D], F32, "ExternalInput")
        s.I["c2"] = s.dram("c2", [2, D], F32, "ExternalInput")
        for k, shp in IN_SHAPES.items():
            s.I[k] = s.dram(k, shp, F32, "ExternalInput")
        for k, arr in (consts or {}).items():
            s.I[k] = s.dram(k, list(arr.shape), F32, "ExternalInput")
        s.out = s.dram("y", [16 * 128, D], F32, "ExternalOutput")
        s.xres = s.dram("xres", [NT * 128, D], F32)
        s.adav = {li: s.dram("adav%d" % li, [2, 6 * D], F32) for li in s.layers}
        s.proj = s.dram("proj", [NT * 128, 12416], BF16)
        s.projf = s.dram("projf", [NT * 128, 128], F32)
        s.omix = s.dram("omix", [NT * 128, 4096], BF16)
        s.mixT = s.dram("mixT", [64, 128, NT * 128], BF16)
        s.wb = {}
        for li in s.layers:
            for nm, k, n in WSPEC[li]:
                s.wb[nm] = s.dram(nm + "_bf", [k, n], BF16)
            s.wb["up%d" % li] = s.dram("up%d_bf" % li, [D, HID], BF16)
            s.wb["down%d" % li] = s.dram("down%d_bf" % li, [HID, D], BF16)
        s.psf = Ring([s.ps([128, 512], F32, "psf") for _ in range(6)])
        s.psb = Ring([s.ps([128, 1024], BF16, "psb") for _ in range(2)])
        s.identf = s.sb([128, 128], F32, "identf")
        s.identb = s.sb([128, 128], BF16, "identb")
        s.dma("sp", s.identf[:, :], s.I["ident"][:, :])
        s.copy("dve", s.identb[:, :], s.identf[:, :])
        s.small = Ring([s.sb([128, 8], F32, "sm") for _ in range(12)])
        s.evi = 0
        s.xsrc = s.I["xin"]
        s.build()

    def ev_eng(s):
        s.evi += 1
        return "act" if s.evi % 2 else "dve"

    def load_bc(s, dst, src_tl, ap1d, q="sp", np_=128):
        s.dma(q, dst, View(src_tl, ap1d.partition_broadcast(np_)))

    def cast_weights(s, li):
        if li not in s.layers:
            return
        lst = [(nm, s.I[nm], s.I[nm].base[0], k, n) for nm, k, n in WSPEC[li]]
        lst.insert(1, ("up%d" % li, s.I["mlp_up"], s.I["mlp_up"].base[li], D, HID))
        lst.append(("down%d" % li, s.I["mlp_down"], s.I["mlp_down"].base[li], HID, D))
        for nm, tl, src, k, n in lst:
            for r in range(0, k, 256):
                s.dma("pool", s.wb[nm][r:r + 256, :].k(r), View(tl, src[r:r + 256, :]))

    def ada_phase(s):
        with contextlib.ExitStack() as st:
            c2t = s.sb([2, D], F32, "c2t", st)
            sc = s.sb([2, D], F32, "sc", st)
            scT = s.sb([128, KC, 2], F32, "scT", st)
            wr = Ring([s.sb([128, KC, 512], F32, "adw", st) for _ in range(2)])
            mbr = Ring([s.sb([2, 512], F32, "mb", st) for _ in range(2)])
            orr = Ring([s.sb([2, 512], F32, "ao", st) for _ in range(2)])
            s.dma("sp", c2t[:, :], s.I["c2"][:, :])
            s.act(sc[:, :], c2t[:, :], AF.Silu)
            p = s.psf.next()
            for kc in range(KC):
                s.tr(p[:, kc * 2:(kc + 1) * 2], sc[0:2, kc * 128:(kc + 1) * 128], s.identf[0:2, 0:2])
            s.copy("dve", scT[:, :, :], View(p, p.base[:, 0:32].rearrange("p (k r) -> p k r", r=2)))
            for li in s.layers:
                wv = s.I["mod_w"].base[li].rearrange("(kc p) n -> p kc n", p=128)
                for c0 in range(0, 6 * D, 512):
                    wt = wr.next()
                    s.dma("sp", wt[:, :, :], View(s.I["mod_w"], wv[:, :, c0:c0 + 512]))
                    mb = mbr.next()
                    s.load_bc(mb[:, :], s.I["mod_b"], s.I["mod_b"].base[li, c0:c0 + 512], np_=2)
                    p = s.psf.next()
                    for kc in range(KC):
                        s.mm(p[0:2, :], scT[:, kc, :], wt[:, kc, :], kc == 0, kc == KC - 1)
                    o = orr.next()
                    s.tt("dve", o[:, :], p[0:2, :], mb[:, :], ALU.add)
                    s.dma("pool", s.adav[li][:, c0:c0 + 512], o[:, :])
        s.barrier()

    def mod_vecs(s, li, r, which, gm, sh, gt, tmp):
        a = s.adav[li]
        o = 0 if which == 1 else 3
        ng = s.I["norm1_g" if which == 1 else "norm2_g"]
        s.load_bc(sh[:, :], a, a.base[r, (o + 0) * D:(o + 1) * D])
        s.load_bc(tmp[:, :], a, a.base[r, (o + 1) * D:(o + 2) * D])
        s.load_bc(gm[:, :], ng, ng.base[li, :])
        s.stt(gm[:, :], tmp[:, :], 1.0, gm[:, :], ALU.add, ALU.mult)
        if gt is not None:
            s.load_bc(gt[:, :], a, a.base[r, (o + 2) * D:(o + 3) * D])

    def norm_tile(s, xt, gm, sh, h32, hb, junk):
        ss = s.small.next()
        s.act(junk[:, :], xt, AF.Square, accum=ss[:, 0:1])
        s.ts("dve", ss[:, 1:2], ss[:, 0:1], 1.0 / D, EPS, ALU.mult, ALU.add)
        s.act(ss[:, 2:3], ss[:, 1:2], AF.Sqrt)
        s.recip(ss[:, 3:4], ss[:, 2:3])
        s.stt(h32[:, :], xt, ss[:, 3:4], gm, ALU.mult, ALU.mult)
        s.tt("pool", hb[:, :], h32[:, :], sh, ALU.add)

    def transpose_tile(s, src, ncol, dst, kc0, tok0):
        nb = ncol // 128
        for b0 in range(0, nb, 8):
            n = min(8, nb - b0)
            p = s.psb.next()
            for j in range(n):
                s.tr(p[:, j * 128:(j + 1) * 128], src[:, (b0 + j) * 128:(b0 + j + 1) * 128], s.identb[:, :])
            s.copy(s.ev_eng(), dst[:, kc0 + b0:kc0 + b0 + n, tok0:tok0 + 128],
                   View(p, p.base[:, 0:n * 128].rearrange("p (j c) -> p j c", c=128)))

    def wview(s, nm):
        return s.wb[nm].base.rearrange("(kc p) n -> p kc n", p=128)

    def wkeys(s, k0, k1):
        return [r for r in range(0, 8192, 256) if r < k1 and r + 256 > k0]

    def gemm_tm(s, lhsT_fn, tiles, wname, K, n0, n1, evac, wring):
        kcn = K // 128
        wv = s.wview(wname)
        for c0 in range(n0, n1, 512):
            cw = min(512, n1 - c0)
            wt = wring.next()
            s.dma("sp", wt[:, 0:kcn, 0:cw], View(s.wb[wname], wv[:, :, c0:c0 + cw], s.wkeys(0, K)))
            for t in tiles:
                p = s.psf.next()
                for kc in range(kcn):
                    s.mm(p[:, 0:cw], lhsT_fn(t, kc), wt[:, kc, 0:cw], kc == 0, kc == kcn - 1)
                evac(t, c0, cw, p)

    def gemm_acc(s, lhsT_fn, tiles, wname, K, evac, wring):
        kcn = K // 128
        wv = s.wview(wname)
        assert len(tiles) <= 6
        for c0 in range(0, D, 512):
            banks = [s.psf.next() for _ in tiles]
            for hb in range(0, kcn, 16):
                wt = wring.next()
                s.dma("sp", wt[:, 0:16, :], View(s.wb[wname], wv[:, hb:hb + 16, c0:c0 + 512],
                                                  s.wkeys(hb * 128, (hb + 16) * 128)))
                for i, t in enumerate(tiles):
                    for kc in range(16):
                        s.mm(banks[i][:, :], lhsT_fn(i, hb + kc), wt[:, kc, :], hb + kc == 0, hb + kc == kcn - 1)
            for i, t in enumerate(tiles):
                evac(t, c0, banks[i])

    def resid_evac_fn(s, gt, xr, tr_):
        def evac(t, c0, p):
            xt = xr.next()
            s.dma("sp", xt[:, :], s.xsrc[t * 128:(t + 1) * 128, c0:c0 + 512].k((t, c0)))
            s.tt("dve", p[:, :], p[:, :], gt[:, c0:c0 + 512], ALU.mult)
            s.tt("dve", xt[:, :], xt[:, :], p[:, :], ALU.add)
            s.dma("pool", s.xres[t * 128:(t + 1) * 128, c0:c0 + 512].k((t, c0)), xt[:, :])
        return evac

    def xkeys(s, t):
        return [(t, c) for c in range(0, D, 512)]

    def outproj_phase(s, li, wname, K, do_ctx):
        kcn = K // 128
        with contextlib.ExitStack() as st:
            oT = s.sb([128, kcn, 6 * 128], BF16, "oT", st)
            gt = s.sb([128, D], F32, "g1", st)
            otr = Ring([s.sb([128, K], BF16, "ot", st) for _ in range(2)])
            wring = Ring([s.sb([128, 16, 512], BF16, "wo", st) for _ in range(2)])
            xr = Ring([s.sb([128, 512], F32, "xr", st) for _ in range(3)])
            tr_ = None
            a = s.adav[li]
            cur_r = None
            for g in GROUPS:
                r = 1 if g[0] < NCTX else 0
                if r == 1 and not do_ctx:
                    continue
                if r != cur_r:
                    s.load_bc(gt[:, :], a, a.base[r, 2 * D:3 * D])
                    cur_r = r
                for i, t in enumerate(g):
                    ot = otr.next()
                    s.dma("sp", ot[:, :], s.omix[t * 128:(t + 1) * 128, 0:K].k(t))
                    s.transpose_tile(ot, K, oT, 0, i * 128)
                s.gemm_acc(lambda i, kc: oT[:, kc, i * 128:(i + 1) * 128], g, wname, K,
                           s.resid_evac_fn(gt, xr, tr_), wring)
        s.xsrc = s.xres if do_ctx or True else s.xsrc
        s.barrier()

    def mlp_phase(s, li, do_ctx):
        with contextlib.ExitStack() as st:
            hT = s.sb([128, KC, 6 * 128], BF16, "hT2", st)
            upT = s.sb([128, 64, 6 * 128], BF16, "upT", st)
            gm = s.sb([128, D], F32, "gm2", st)
            sh = s.sb([128, D], F32, "sh2", st)
            gt = s.sb([128, D], F32, "g2", st)
            xtr = Ring([s.sb([128, D], F32, "xt", st) for _ in range(1)])
            h32 = s.sb([128, D], F32, "h32", st)
            hb = s.sb([128, D], BF16, "hb", st)
            junk = hb
            wring = Ring([s.sb([128, 16, 512], BF16, "wm", st) for _ in range(2)])
            xr = Ring([s.sb([128, 512], F32, "xr", st) for _ in range(3)])
            tr_ = None
            r32 = Ring([s.sb([128, 512], F32, "r32", st) for _ in range(2)])
            cur_r = None
            wv = s.wview("up%d" % li)
            for g in GROUPS:
                r = 1 if g[0] < NCTX else 0
                if r == 1 and not do_ctx:
                    continue
                if r != cur_r:
                    s.mod_vecs(li, r, 2, gm, sh, gt, h32)
                    cur_r = r
                ntok = len(g) * 128
                for i, t in enumerate(g):
                    xt = xtr.next()
                    s.dma("sp", xt[:, :], s.xsrc[t * 128:(t + 1) * 128, :].k(s.xkeys(t)))
                    s.norm_tile(xt[:, :], gm[:, :], sh[:, :], h32, hb, junk)
                    s.transpose_tile(hb, D, hT, 0, i * 128)
                for c0 in range(0, HID, 512):
                    wt = wring.next()
                    s.dma("sp", wt[:, :, :], View(s.wb["up%d" % li], wv[:, :, c0:c0 + 512], s.wkeys(0, D)))
                    for j in range(4):
                        hc = c0 // 128 + j
                        for tb in range(0, ntok, 512):
                            tw = min(512, ntok - tb)
                            p = s.psf.next()
                            for kc in range(KC):
                                s.mm(p[:, 0:tw], wt[:, kc, j * 128:(j + 1) * 128], hT[:, kc, tb:tb + tw],
                                     kc == 0, kc == KC - 1)
                            rr = r32.next()
                            s.act(rr[:, 0:tw], p[:, 0:tw], AF.Relu)
                            s.tt("pool" if (hc + tb // 512) % 2 else "dve", upT[:, hc, tb:tb + tw],
                                 rr[:, 0:tw], rr[:, 0:tw], ALU.mult)
                s.gemm_acc(lambda i, kc: upT[:, kc, i * 128:(i + 1) * 128], g, "down%d" % li, HID,
                           s.resid_evac_fn(gt, xr, tr_), wring)
        s.barrier()

    def n1_phase(s, li, tiles, hT, st):
        with contextlib.ExitStack() as st2:
            gm = s.sb([128, D], F32, "gm1", st2)
            sh = s.sb([128, D], F32, "sh1", st2)
            xtr = Ring([s.sb([128, D], F32, "xt", st2) for _ in range(2)])
            h32 = s.sb([128, D], F32, "h32", st2)
            hb = s.sb([128, D], BF16, "hb", st2)
            junk = hb
            cur_r = None
            for t in tiles:
                r = 1 if t < NCTX else 0
                if r != cur_r:
                    s.mod_vecs(li, r, 1, gm, sh, None, h32)
                    cur_r = r
                xt = xtr.next()
                s.dma("sp", xt[:, :], s.xsrc[t * 128:(t + 1) * 128, :].k(s.xkeys(t)))
                s.norm_tile(xt[:, :], gm[:, :], sh[:, :], h32, hb, junk)
                s.transpose_tile(hb, D, hT, 0, t * 128)
            s.barrier()

    def final_phase(s):
        with contextlib.ExitStack() as st:
            gm = s.sb([128, D], F32, "fg", st)
            xtr = Ring([s.sb([128, D], F32, "xt", st) for _ in range(2)])
            otr = Ring([s.sb([128, D], F32, "ot", st) for _ in range(2)])
            junk = s.sb([128, D], BF16, "junk", st)
            s.load_bc(gm[:, :], s.I["final_g"], s.I["final_g"].base[:])
            off = NCTX if os.environ.get("DBGCTX") != "1" else 0
            for t in range(off, off + 16):
                xt = xtr.next()
                s.dma("sp", xt[:, :], s.xsrc[t * 128:(t + 1) * 128, :].k(s.xkeys(t)))
                ss = s.small.next()
                s.act(junk[:, :], xt[:, :], AF.Square, accum=ss[:, 0:1])
                s.ts("dve", ss[:, 1:2], ss[:, 0:1], 1.0 / D, EPS, ALU.mult, ALU.add)
                s.act(ss[:, 2:3], ss[:, 1:2], AF.Sqrt)
                s.recip(ss[:, 3:4], ss[:, 2:3])
                ot = otr.next()
                s.stt(ot[:, :], xt[:, :], ss[:, 3:4], gm[:, :], ALU.mult, ALU.mult)
                s.dma("pool", s.out[(t - off) * 128:(t - off + 1) * 128, :], ot[:, :])
        s.barrier()

    def sgu_mixer(s, li):
        tiles = list(range(NCTX, NT))
        with contextlib.ExitStack() as st:
            hT = s.sb([128, KC, NT * 128], BF16, "hT", st)
            s.n1_phase(li, tiles, hT, st)
            with contextlib.ExitStack() as st2:
                wring = Ring([s.sb([128, 16, 512], BF16, "wi", st2) for _ in range(2)])
                zr = Ring([s.sb([128, 512], BF16, "z", st2) for _ in range(4)])

                def evac(t, c0, cw, p):
                    z = zr.next()
                    s.act(z[:, 0:cw], p[:, 0:cw], AF.Gelu)
                    s.dma("pool", s.proj[t * 128:(t + 1) * 128, c0:c0 + cw].k((t, c0)), z[:, 0:cw])
                s.gemm_tm(lambda t, kc: hT[:, kc, t * 128:(t + 1) * 128], tiles, "sgu_w_in", D, 0, 8192, evac, wring)
            s.barrier()
        with contextlib.ExitStack() as st:
            lg = s.sb([128, 4096], F32, "lng", st)
            lb = s.sb([128, 4096], F32, "lnb", st)
            s.load_bc(lg[:, :], s.I["sgu_ln_g"], s.I["sgu_ln_g"].base[0, :])
            s.load_bc(lb[:, :], s.I["sgu_ln_b"], s.I["sgu_ln_b"].base[0, :])
            wsT = s.sb([128, 8, 128], BF16, "wsT", st)
            bs = s.sb([128, 8], F32, "bs", st)
            wsf = s.sb([128, 8, 128], F32, "wsf", st)
            wsb = s.sb([128, 8, 128], BF16, "wsb", st)
            bsr = s.sb([8, 128], F32, "bsr", st)
            s.dma("sp", wsf[:, :, :], View(s.I["sgu_w_s"], s.I["sgu_w_s"].base[0].rearrange("g p q -> p g q")))
            s.copy("dve", wsb[:, :, :], wsf[:, :, :])
            for g0 in range(0, 8, 4):
                p = s.psb.next()
                for j in range(4):
                    s.tr(p[:, j * 128:(j + 1) * 128], wsb[:, g0 + j, :], s.identb[:, :])
                s.copy("dve", wsT[:, g0:g0 + 4, :], View(p, p.base[:, 0:512].rearrange("p (j c) -> p j c", c=128)))
            s.dma("sp", bsr[:, :], s.I["sgu_b_s"][0, :, :])
            p = s.psf.next()
            s.tr(p[:, 0:8], bsr[0:8, :], s.identf[0:8, 0:8])
            s.copy("dve", bs[:, :], p[:, 0:8])
            zt = Ring([s.sb([128, 8192], BF16, "zt", st) for _ in range(2)])
            vn32 = s.sb([128, 4096], F32, "vn32", st)
            vnb = s.sb([128, 4096], BF16, "vnb", st)
            junk = s.sb([128, 4096], BF16, "junk", st)
            gor = Ring([s.sb([128, 4096], BF16, "go", st) for _ in range(2)])
            for t in tiles:
                z = zt.next()
                s.dma("sp", z[:, :], s.proj[t * 128:(t + 1) * 128, 0:8192].k([(t, c) for c in range(0, 8192, 512)]))
                ss = s.small.next()
                v = z[:, 4096:8192]
                s.act(junk[:, :], v, AF.Copy, accum=ss[:, 0:1])
                s.act(junk[:, :], v, AF.Square, accum=ss[:, 1:2])
                s.ts("dve", ss[:, 2:3], ss[:, 0:1], 1.0 / 4096, None, ALU.mult)
                s.tt("dve", ss[:, 3:4], ss[:, 2:3], ss[:, 2:3], ALU.mult)
                s.stt(ss[:, 4:5], ss[:, 1:2], 1.0 / 4096, ss[:, 3:4], ALU.mult, ALU.subtract)
                s.ts("dve", ss[:, 5:6], ss[:, 4:5], EPS, None, ALU.add)
                s.act(ss[:, 6:7], ss[:, 5:6], AF.Sqrt)
                s.recip(ss[:, 7:8], ss[:, 6:7])
                s.ts("dve", vn32[:, :], v, ss[:, 2:3], ss[:, 7:8], ALU.subtract, ALU.mult)
                s.tt("pool", vn32[:, :], vn32[:, :], lg[:, :], ALU.mult)
                s.tt("dve", vnb[:, :], vn32[:, :], lb[:, :], ALU.add)
                go = gor.next()
                for g in range(8):
                    p = s.psf.next()
                    s.mm(p[:, :], wsT[:, g, :], vnb[:, g * 512:(g + 1) * 512], True, True)
                    s.stt(go[:, g * 512:(g + 1) * 512], p[:, :], bs[:, g:g + 1], z[:, g * 512:(g + 1) * 512],
                          ALU.add, ALU.mult)
                s.dma("pool", s.omix[t * 128:(t + 1) * 128, :].k(t), go[:, :])
        s.barrier()
        s.outproj_phase(li, "sgu_w_out", 4096, False)


    def rope_evac(s, p, cw, t, scale, rope, cosT, sinT, xs, tmpr, ob):
        if not rope:
            s.act(ob[:, 0:cw], p[:, 0:cw], AF.Copy, scale=scale)
            return
        s.act(xs[:, 0:cw], p[:, 0:cw], AF.Copy, scale=scale)
        h = cw // 2
        x1 = View(xs, xs.base[:, 0:cw].rearrange("p (n two) -> p n two", two=2)[:, :, 0])
        x2 = View(xs, xs.base[:, 0:cw].rearrange("p (n two) -> p n two", two=2)[:, :, 1])
        y1 = View(ob, ob.base[:, 0:cw].rearrange("p (n two) -> p n two", two=2)[:, :, 0])
        y2 = View(ob, ob.base[:, 0:cw].rearrange("p (n two) -> p n two", two=2)[:, :, 1])
        c = cosT[:, t - NCTX, 0:h]
        sn = sinT[:, t - NCTX, 0:h]
        t1, t2, t3, t4 = [tmpr.next() for _ in range(4)]
        s.tt("dve", t1[:, 0:h], x1, c, ALU.mult)
        s.tt("pool", t2[:, 0:h], x2, sn, ALU.mult)
        s.tt("dve", y1, t1[:, 0:h], t2[:, 0:h], ALU.subtract)
        s.tt("pool", t3[:, 0:h], x1, sn, ALU.mult)
        s.tt("dve", t4[:, 0:h], x2, c, ALU.mult)
        s.tt("pool", y2, t3[:, 0:h], t4[:, 0:h], ALU.add)

    def att_mixer(s, li):
        tiles = list(range(NT))
        with contextlib.ExitStack() as st:
            hT = s.sb([128, KC, NT * 128], BF16, "hT", st)
            s.n1_phase(li, tiles, hT, st)
            with contextlib.ExitStack() as st2:
                wring = Ring([s.sb([128, 16, 512], BF16, "wi", st2) for _ in range(2)])
                cosT = s.sb([128, 16, 256], F32, "cos", st2)
                sinT = s.sb([128, 16, 256], F32, "sin", st2)
                s.dma("sp", cosT[:, :, :], View(s.I["cos_a"], s.I["cos_a"].base.rearrange("t p c -> p t c")))
                s.dma("sp", sinT[:, :, :], View(s.I["sin_a"], s.I["sin_a"].base.rearrange("t p c -> p t c")))
                xsr = Ring([s.sb([128, 512], F32, "xs", st2) for _ in range(2)])
                tmpr = Ring([s.sb([128, 256], F32, "rt", st2) for _ in range(8)])
                obr = Ring([s.sb([128, 512], BF16, "ob", st2) for _ in range(4)])

                def evac(t, c0, cw, p):
                    ob = obr.next()
                    isq = c0 < 2048
                    isv = c0 >= 2560
                    s.rope_evac(p, cw, t, (128 ** -0.5) if isq else 1.0, (not isv) and t >= NCTX,
                                cosT, sinT, xsr.next(), tmpr, ob)
                    s.dma("pool", s.proj[t * 128:(t + 1) * 128, c0:c0 + cw].k((t, c0)), ob[:, 0:cw])
                s.gemm_tm(lambda t, kc: hT[:, kc, t * 128:(t + 1) * 128], tiles, "att_w_in", D, 0, 3072, evac, wring)
            s.barrier()
        pv = s.proj.base.rearrange("(t p) c -> p t c", p=128)
        ov = s.omix.base.rearrange("(t p) c -> p t c", p=128)
        allk = lambda c0: [(t, c0) for t in range(NT)]
        with contextlib.ExitStack() as st:
            mask3 = s.sb([128, 384], F32, "mask3", st)
            s.dma("sp", mask3[:, :], s.I["mask3"][:, :])
            sinkb = s.sb([128, 16], F32, "sinkb", st)
            s.load_bc(sinkb[:, :], s.I["att_sink"], s.I["att_sink"].base[0, :])
            ktm = s.sb([128, NT, 128], BF16, "ktm", st)
            kT = s.sb([128, 1, NT * 128], BF16, "kT", st)
            vtm = Ring([s.sb([128, NT, 128], BF16, "vtm", st) for _ in range(2)])
            qtmr = Ring([s.sb([128, NT, 128], BF16, "qtm", st) for _ in range(2)])
            qTr = Ring([s.sb([128, 1, NT * 128], BF16, "qT", st) for _ in range(2)])
            ohr = Ring([s.sb([128, NT, 128], BF16, "oh", st) for _ in range(2)])
            ssb = Ring([s.sb([128, 640], F32, "ssb", st) for _ in range(3)])
            pbr = Ring([s.sb([128, 640], BF16, "pb", st) for _ in range(3)])
            ptr = Ring([s.sb([128, 5, 128], BF16, "pt", st) for _ in range(3)])
            for g in range(4):
                s.dma("sp", ktm[:, :, :], View(s.proj, pv[:, :, 2048 + g * 128:2048 + (g + 1) * 128], allk(2048)))
                vt = vtm.next()
                s.dma("sp", vt[:, :, :], View(s.proj, pv[:, :, 2560 + g * 128:2560 + (g + 1) * 128], allk(2560)))
                for t in range(NT):
                    s.transpose_tile(_Sub(ktm, t), 128, kT, 0, t * 128)
                for hh in range(4):
                    h = g * 4 + hh
                    qtm = qtmr.next()
                    s.dma("sp", qtm[:, :, :], View(s.proj, pv[:, :, h * 128:(h + 1) * 128], allk((h // 4) * 512)))
                    qT = qTr.next()
                    for t in range(NT):
                        s.transpose_tile(_Sub(qtm, t), 128, qT, 0, t * 128)
                    oh = ohr.next()
                    for t in range(NT):
                        qv = qT[:, 0, t * 128:(t + 1) * 128]
                        sS = ssb.next()
                        if t < NCTX:
                            pa = s.psf.next()
                            s.mm(pa[:, 0:256], qv, kT[:, 0, 0:256], True, True)
                            s.copy("act", sS[:, 0:256], pa[:, 0:256])
                            n = 256
                            ktiles = [0, 1]
                        else:
                            lo, hi = max(NCTX, t - 1), min(NT - 1, t + 1)
                            nl = hi - lo + 1
                            pa = s.psf.next()
                            s.mm(pa[:, 0:256], qv, kT[:, 0, 0:256], True, True)
                            pb_ = s.psf.next()
                            s.mm(pb_[:, 0:nl * 128], qv, kT[:, 0, lo * 128:(hi + 1) * 128], True, True)
                            s.copy("act", sS[:, 0:256], pa[:, 0:256])
                            m0 = (lo - (t - 1)) * 128
                            s.tt("dve", sS[:, 256:256 + nl * 128], pb_[:, 0:nl * 128], mask3[:, m0:m0 + nl * 128], ALU.add)
                            n = 256 + nl * 128
                            ktiles = [0, 1] + list(range(lo, hi + 1))
                        sm = s.small.next()
                        s.rmax(sm[:, 0:1], sS[:, 0:n])
                        s.ts("dve", sm[:, 1:2], sm[:, 0:1], sinkb[:, h:h + 1], -1.0, ALU.max, ALU.mult)
                        pb = pbr.next()
                        s.act(pb[:, 0:n], sS[:, 0:n], AF.Exp, bias=sm[:, 1:2], accum=sm[:, 2:3])
                        s.act(sm[:, 3:4], sinkb[:, h:h + 1], AF.Exp, bias=sm[:, 1:2])
                        s.tt("dve", sm[:, 4:5], sm[:, 2:3], sm[:, 3:4], ALU.add)
                        s.recip(sm[:, 5:6], sm[:, 4:5])
                        nk = n // 128
                        pT = ptr.next()
                        pp = s.psb.next()
                        for j in range(nk):
                            s.tr(pp[:, j * 128:(j + 1) * 128], pb[:, j * 128:(j + 1) * 128], s.identb[:, :])
                        s.copy("act", pT[:, 0:nk, :], View(pp, pp.base[:, 0:nk * 128].rearrange("p (j c) -> p j c", c=128)))
                        po = s.psf.next()
                        for j in range(nk):
                            s.mm(po[:, 0:128], pT[:, j, :], vt[:, ktiles[j], :], j == 0, j == nk - 1)
                        s.ts("dve", oh[:, t, :], po[:, 0:128], sm[:, 5:6], None, ALU.mult)
                    s.dma("pool", View(s.omix, ov[:, :, h * 128:(h + 1) * 128], list(range(NT))), oh[:, :, :])
        s.barrier()
        s.outproj_phase(li, "att_w_out", 2048, True)


    def ret_mixer(s, li):
        tiles = list(range(NT))
        with contextlib.ExitStack() as st:
            hT = s.sb([128, KC, NT * 128], BF16, "hT", st)
            s.n1_phase(li, tiles, hT, st)
            with contextlib.ExitStack() as st2:
                wring = Ring([s.sb([128, 16, 512], BF16, "wi", st2) for _ in range(2)])
                cosT = s.sb([128, 16, 256], F32, "cos", st2)
                sinT = s.sb([128, 16, 256], F32, "sin", st2)
                s.dma("sp", cosT[:, :, :], View(s.I["cos_r"], s.I["cos_r"].base.rearrange("t p c -> p t c")))
                s.dma("sp", sinT[:, :, :], View(s.I["sin_r"], s.I["sin_r"].base.rearrange("t p c -> p t c")))
                xsr = Ring([s.sb([128, 512], F32, "xs", st2) for _ in range(2)])
                tmpr = Ring([s.sb([128, 256], F32, "rt", st2) for _ in range(8)])
                obr = Ring([s.sb([128, 512], BF16, "ob", st2) for _ in range(4)])

                def evac(t, c0, cw, p):
                    ob = obr.next()
                    if c0 >= 8192:
                        s.act(ob[:, 0:cw], p[:, 0:cw], AF.Silu)
                    else:
                        isk = 2048 <= c0 < 4096
                        s.rope_evac(p, cw, t, (256 ** -0.5) if isk else 1.0, c0 < 4096 and t >= NCTX,
                                    cosT, sinT, xsr.next(), tmpr, ob)
                    s.dma("pool", s.proj[t * 128:(t + 1) * 128, c0:c0 + cw].k((t, c0)), ob[:, 0:cw])
                s.gemm_tm(lambda t, kc: hT[:, kc, t * 128:(t + 1) * 128], tiles, "ret_w_in", D, 0, 12288, evac, wring)
            s.barrier()
        pv = s.proj.base.rearrange("(t p) c -> p t c", p=128)
        ov = s.omix.base.rearrange("(t p) c -> p t c", p=128)
        allk = lambda c0: [(t, (c0 // 512) * 512) for t in range(NT)]
        lg = np.log1p(-np.exp2(-5.0 - np.arange(8, dtype=np.float32))).astype(np.float32)
        with contextlib.ExitStack() as st:
            qtm = s.sb([128, NT, 256], BF16, "qtm", st)
            ktm = s.sb([128, NT, 256], BF16, "ktm", st)
            vtm = s.sb([128, NT, 512], BF16, "vtm", st)
            gtm = s.sb([128, NT, 512], BF16, "gtm", st)
            oh = s.sb([128, NT, 512], BF16, "oh", st)
            qT = s.sb([128, 2, NT * 128], BF16, "qT", st)
            kT = s.sb([128, 2, NT * 128], BF16, "kT", st)
            sbst = s.sb([128, NT, 2, 512], BF16, "sbst", st)
            S32 = s.sb([128, 2, 512], F32, "S32", st)
            Sfb = Ring([s.sb([128, 2, 512], BF16, "Sfb", st) for _ in range(2)])
            Dm = s.sb([128, 128], F32, "Dm", st)
            qdec = s.sb([128, 2, 128], F32, "qdec", st)
            kdec = s.sb([128, 16], F32, "kdec", st)
            gng = s.sb([128, 512], F32, "gng", st)
            s.dma("sp", kdec[:, :], s.I["ret_kdec"][:, :])
            ksr = Ring([s.sb([128, 256], BF16, "ks", st) for _ in range(3)])
            ptr = Ring([s.sb([128, 128], BF16, "PT", st) for _ in range(2)])
            qfr = Ring([s.sb([128, 2, 128], BF16, "qf", st) for _ in range(4)])
            o32 = Ring([s.sb([128, 512], F32, "o32", st) for _ in range(2)])
            junk = s.sb([128, 512], BF16, "junk", st)
            for h in range(8):
                cd = float(np.exp(lg[h] * np.float32(128.0)))
                s.dma("sp", qtm[:, :, :], View(s.proj, pv[:, :, h * 256:(h + 1) * 256], allk(h * 256)))
                s.dma("sp", ktm[:, :, :], View(s.proj, pv[:, :, 2048 + h * 256:2048 + (h + 1) * 256], allk(2048 + h * 256)))
                s.dma("sp", vtm[:, :, :], View(s.proj, pv[:, :, 4096 + h * 512:4096 + (h + 1) * 512], allk(4096 + h * 512)))
                s.dma("sp", gtm[:, :, :], View(s.proj, pv[:, :, 8192 + h * 512:8192 + (h + 1) * 512], allk(8192 + h * 512)))
                s.dma("sp", Dm[:, :], s.I["ret_D"][h, :, :])
                s.load_bc(qdec[:, 0, :], s.I["ret_qdec"], s.I["ret_qdec"].base[h, 0, :])
                s.load_bc(qdec[:, 1, :], s.I["ret_qdec"], s.I["ret_qdec"].base[h, 1, :])
                s.load_bc(gng[:, :], s.I["ret_gn_g"], s.I["ret_gn_g"].base[0, h * 512:(h + 1) * 512])
                for t in range(NT):
                    s.transpose_tile(_Sub(qtm, t), 256, qT, 0, t * 128)
                    s.transpose_tile(_Sub(ktm, t), 256, kT, 0, t * 128)

                def upd(t, col):
                    ks = ksr.next()
                    s.ts("pool", ks[:, :], ktm[:, t, :], kdec[:, col:col + 1], None, ALU.mult)
                    for dc in range(2):
                        p = s.psf.next()
                        s.mm(p[:, :], ks[:, dc * 128:(dc + 1) * 128], vtm[:, t, :], True, True)
                        s.stt(S32[:, dc, :], S32[:, dc, :], cd, p[:, :], ALU.mult, ALU.add)
                s.memset("dve", S32[:, :, :], 0.0)
                for t in [1, 0] + list(range(NT - 1, NCTX - 1, -1)):
                    s.copy("act", sbst[:, t, :, :], S32[:, :, :])
                    upd(t, h * 2 + 1)
                s.memset("dve", S32[:, :, :], 0.0)
                for t in range(NT):
                    sf = Sfb.next()
                    s.copy("act", sf[:, :, :], S32[:, :, :])
                    tk = slice(t * 128, (t + 1) * 128)
                    p1 = s.psf.next()
                    for dc in range(2):
                        s.mm(p1[:, 0:128], kT[:, dc, tk], qT[:, dc, tk], dc == 0, dc == 1)
                    PT = ptr.next()
                    s.tt("dve", PT[:, :], p1[:, 0:128], Dm[:, :], ALU.mult)
                    qf = qfr.next()
                    qb = qfr.next()
                    for dc in range(2):
                        s.tt("pool", qf[:, dc, :], qT[:, dc, tk], qdec[:, 0, :], ALU.mult)
                        s.tt("pool", qb[:, dc, :], qT[:, dc, tk], qdec[:, 1, :], ALU.mult)
                    po = s.psf.next()
                    s.mm(po[:, :], PT[:, :], vtm[:, t, :], True, False)
                    for dc in range(2):
                        s.mm(po[:, :], qf[:, dc, :], sf[:, dc, :], False, False)
                    for dc in range(2):
                        s.mm(po[:, :], qb[:, dc, :], sbst[:, t, dc, :], False, dc == 1)
                    sm = s.small.next()
                    s.act(junk[:, :], po[:, :], AF.Square, accum=sm[:, 0:1])
                    s.ts("dve", sm[:, 1:2], sm[:, 0:1], 1.0 / 512, EPS, ALU.mult, ALU.add)
                    s.act(sm[:, 2:3], sm[:, 1:2], AF.Sqrt)
                    s.recip(sm[:, 3:4], sm[:, 2:3])
                    o = o32.next()
                    s.stt(o[:, :], po[:, :], sm[:, 3:4], gng[:, :], ALU.mult, ALU.mult)
                    s.tt("pool", oh[:, t, :], o[:, :], gtm[:, t, :], ALU.mult)
                    upd(t, h * 2 + 0)
                s.dma("pool", View(s.omix, ov[:, :, h * 512:(h + 1) * 512], list(range(NT))), oh[:, :, :])
        s.barrier()
        s.outproj_phase(li, "ret_w_out", 4096, True)


    def gdn_mixer(s, li):
        tiles = list(range(NT))
        XW = 2310
        mixT = s.mixT
        with contextlib.ExitStack() as st:
            hT = s.sb([128, KC, NT * 128], BF16, "hT", st)
            s.n1_phase(li, tiles, hT, st)
            with contextlib.ExitStack() as st2:
                wring = Ring([s.sb([128, 16, 512], BF16, "wi", st2) for _ in range(2)])
                X = s.sb([128, XW], F32, "X", st2)
                Y = s.sb([128, NT * 128], F32, "Y", st2)
                Z = s.sb([128, NT * 128], F32, "Z", st2)
                SQ = s.sb([128, NT * 128], F32, "SQ", st2)
                OB = Ring([s.sb([128, NT * 128], BF16, "OB", st2) for _ in range(2)])
                rn = Ring([s.sb([128, 512], F32, "rn", st2) for _ in range(2)])
                onesf = s.sb([128, 128], F32, "onesf", st2)
                s.memset("dve", onesf[:, :], 1.0)
                s.memset("dve", X[:, :], 0.0)
                cw4 = s.sb([4, 8192], F32, "cw4", st2)
                cwT = s.sb([128, 64, 4], F32, "cwT", st2)
                s.dma("sp", cw4[:, :], s.I["gdn_conv_w"][0, :, :])
                p = s.psf.next()
                for j in range(64):
                    s.tr(p[:, j * 4:(j + 1) * 4], cw4[0:4, j * 128:(j + 1) * 128], s.identf[0:4, 0:4])
                s.copy("dve", cwT[:, :, :], View(p, p.base[:, 0:256].rearrange("p (j k) -> p j k", k=4)))
                wv = s.wview("gdn_w_in")
                blocks = [(0, 256)] + [(256 + i * 512, 512) for i in range(4)]
                xcol = lambda tok: tok + 2 if tok < 256 else tok + 5
                for c0 in range(0, 8192, 512):
                    wt = wring.next()
                    s.dma("sp", wt[:, :, :], View(s.wb["gdn_w_in"], wv[:, :, c0:c0 + 512], s.wkeys(0, D)))
                    for j in range(4):
                        ch = c0 // 128 + j
                        for tb, tw in blocks:
                            p = s.psf.next()
                            for kc in range(KC):
                                s.mm(p[:, 0:tw], wt[:, kc, j * 128:(j + 1) * 128], hT[:, kc, tb:tb + tw], kc == 0, kc == KC - 1)
                            s.copy("act", X[:, xcol(tb):xcol(tb) + tw], p[:, 0:tw])
                        for (t0, n) in [(0, 256), (256, 2048)]:
                            x0 = xcol(t0)
                            s.ts("dve", Y[:, t0:t0 + n], X[:, x0 - 2:x0 - 2 + n], cwT[:, ch, 0:1], None, ALU.mult)
                            for k in range(1, 4):
                                s.stt(Y[:, t0:t0 + n], X[:, x0 - 2 + k:x0 - 2 + k + n], cwT[:, ch, k:k + 1], Y[:, t0:t0 + n],
                                      ALU.mult, ALU.add)
                        ob = OB.next()
                        if ch >= 32:
                            s.act(ob[:, :], Y[:, :], AF.Silu)
                        else:
                            s.act(Z[:, :], Y[:, :], AF.Silu)
                            s.tt("pool", SQ[:, :], Z[:, :], Z[:, :], ALU.mult)
                            for tb, tw in blocks:
                                p = s.psf.next()
                                s.mm(p[:, 0:tw], onesf[:, :], SQ[:, tb:tb + tw], True, True)
                                r = rn.next()
                                s.ts("dve", r[:, 0:tw], p[:, 0:tw], 1e-6, None, ALU.add)
                                s.act(r[:, 0:tw], r[:, 0:tw], AF.Sqrt)
                                s.recip(r[:, 0:tw], r[:, 0:tw])
                                s.stt(ob[:, tb:tb + tw], Z[:, tb:tb + tw], (128 ** -0.5) if ch < 16 else 1.0, r[:, 0:tw],
                                      ALU.mult, ALU.mult)
                        s.dma("pool", mixT[ch, :, :].k(ch), ob[:, :])
                zr = Ring([s.sb([128, 512], BF16, "z", st2) for _ in range(3)])
                fr = Ring([s.sb([128, 128], F32, "f", st2) for _ in range(2)])

                def evac(t, c0, cw, p):
                    if c0 < 12288:
                        z = zr.next()
                        s.act(z[:, 0:cw], p[:, 0:cw], AF.Silu)
                        s.dma("pool", s.proj[t * 128:(t + 1) * 128, c0 - 8192:c0 - 8192 + cw].k((t, c0 - 8192)), z[:, 0:cw])
                    else:
                        f = fr.next()
                        s.copy("dve", f[:, 0:cw], p[:, 0:cw])
                        s.dma("pool", s.projf[t * 128:(t + 1) * 128, 0:cw].k(t), f[:, 0:cw])
                s.gemm_tm(lambda t, kc: hT[:, kc, t * 128:(t + 1) * 128], tiles, "gdn_w_in", D, 8192, 12416, evac, wring)
            s.barrier()
        if os.environ.get("GDNDBG") == "1":
            s.outproj_phase(li, "gdn_w_out", 4096, False)
            return
        pv = s.proj.base.rearrange("(t p) c -> p t c", p=128)
        ov = s.omix.base.rearrange("(t p) c -> p t c", p=128)
        with contextlib.ExitStack() as st:
            f32t = lambda nm, shp: s.sb(shp, F32, nm, st)
            raw = f32t("raw", [128, NT, 128])
            s.dma("sp", raw[:, :, :], View(s.projf, s.projf.base.rearrange("(t p) c -> p t c", p=128), list(range(NT))))
            beta = f32t("beta", [128, NT, 64])
            nbeta = f32t("nbeta", [128, NT, 64])
            g = f32t("g", [128, NT, 64])
            t1 = f32t("t1", [128, NT, 64])
            t2 = f32t("t2", [128, NT, 64])
            gc = f32t("gc", [128, NT, 64])
            tot = f32t("tot", [128, NT, 64])
            eg = f32t("eg", [128, NT, 64])
            egl = f32t("egl", [128, NT, 64])
            etot = f32t("etot", [128, NT, 64])
            beg = f32t("beg", [128, NT, 64])
            alb = f32t("alb", [128, 64])
            dtb = f32t("dtb", [128, 64])
            nA = f32t("nA", [128, 64])
            onesf = f32t("onesf", [128, 128])
            masks = f32t("masks", [128, 14, 128])
            s.memset("dve", onesf[:, :], 1.0)
            s.dma("sp", masks[:, :, :], View(s.I["gdn_masks"], s.I["gdn_masks"].base.rearrange("k p c -> p k c")))
            s.load_bc(alb[:, :], s.I["gdn_a_log"], s.I["gdn_a_log"].base[0].rearrange("a b -> (a b)"))
            s.load_bc(dtb[:, :], s.I["gdn_dt_bias"], s.I["gdn_dt_bias"].base[0].rearrange("a b -> (a b)"))
            s.act(nA[:, :], alb[:, :], AF.Exp)
            s.ts("dve", nA[:, :], nA[:, :], -1.0, None, ALU.mult)
            s.act(beta[:, :, :], raw[:, :, 0:64], AF.Sigmoid)
            s.ts("dve", nbeta[:, :, :], beta[:, :, :], -1.0, None, ALU.mult)
            for t in range(NT):
                s.tt("dve", t1[:, t, :], raw[:, t, 64:128], dtb[:, :], ALU.add)
            s.act(t2[:, :, :], t1[:, :, :], AF.Abs)
            s.act(t2[:, :, :], t2[:, :, :], AF.Exp, scale=-1.0)
            s.act(t2[:, :, :], t2[:, :, :], AF.Ln, bias=1.0)
            s.ts("dve", t1[:, :, :], t1[:, :, :], 0.0, None, ALU.max)
            s.tt("dve", t1[:, :, :], t1[:, :, :], t2[:, :, :], ALU.add)
            for t in range(NT):
                s.tt("dve", g[:, t, :], t1[:, t, :], nA[:, :], ALU.mult)
            MU_F, MU_B, M_SL, M_SU, M_UI, M_LI = range(6)
            for t in range(NT):
                p = s.psf.next()
                s.mm(p[:, 0:32], masks[:, MU_F, :], g[:, t, 0:32], True, True)
                s.mm(p[:, 32:64], masks[:, MU_B, :], g[:, t, 32:64], True, True)
                s.mm(p[:, 64:128], onesf[:, :], g[:, t, :], True, True)
                s.copy("act", gc[:, t, :], p[:, 0:64])
                s.copy("dve", tot[:, t, :], p[:, 64:128])
            s.act(eg[:, :, :], gc[:, :, :], AF.Exp)
            s.tt("dve", t1[:, :, :], tot[:, :, :], gc[:, :, :], ALU.subtract)
            s.act(egl[:, :, :], t1[:, :, :], AF.Exp)
            s.act(etot[:, :, :], tot[:, :, :], AF.Exp)
            s.tt("dve", beg[:, :, :], beta[:, :, :], eg[:, :, :], ALU.mult)
            ngb = f32t("ngb", [128, 128])
            s.load_bc(ngb[:, :], s.I["gdn_norm_g"], s.I["gdn_norm_g"].base[0, :])
            kT = s.sb([128, 1, NT * 128], BF16, "kT", st)
            qT = s.sb([128, 1, NT * 128], BF16, "qT", st)
            Ktm = s.sb([128, NT, 128], BF16, "Ktm", st)
            vT = [s.sb([128, 1, NT * 128], BF16, "vT", st) for _ in range(2)]
            Vtm = [s.sb([128, NT, 128], BF16, "Vtm", st) for _ in range(2)]
            ztm = [s.sb([128, NT, 128], BF16, "ztm", st) for _ in range(2)]
            acc = [f32t("acc", [128, NT, 128]) for _ in range(2)]
            oh = [s.sb([128, NT, 128], BF16, "oh", st) for _ in range(2)]
            S32 = [f32t("S32", [128, 128]) for _ in range(2)]
            Sb = [Ring([s.sb([128, 128], BF16, "Sb", st) for _ in range(2)]) for _ in range(2)]
            shr = Ring([f32t("shr", [128, 5, 128]) for _ in range(2)])
            fr = Ring([f32t("fr", [128, 128]) for _ in range(40)])
            br = Ring([s.sb([128, 128], BF16, "br", st) for _ in range(24)])
            junk = s.sb([128, 128], BF16, "junk", st)
            for hk in range(int(os.environ.get("GDNHK", "16"))):
                s.dma("sp", kT[:, 0, :], mixT[16 + hk, :, :].k(16 + hk))
                s.dma("sp", qT[:, 0, :], mixT[hk, :, :].k(hk))
                for t in range(NT):
                    pp = s.psb.next()
                    s.tr(pp[:, 0:128], kT[:, 0, t * 128:(t + 1) * 128], s.identb[:, :])
                    s.copy("act", Ktm[:, t, :], pp[:, 0:128])
                for e in range(2):
                    hv = hk * 2 + e
                    s.dma("sp", vT[e][:, 0, :], mixT[32 + hv, :, :].k(32 + hv))
                    s.dma("sp", ztm[e][:, :, :], View(s.proj, pv[:, :, hv * 128:(hv + 1) * 128],
                                                      [(t, (hv // 4) * 512) for t in range(NT)]))
                    for t in range(NT):
                        pp = s.psb.next()
                        s.tr(pp[:, 0:128], vT[e][:, 0, t * 128:(t + 1) * 128], s.identb[:, :])
                        s.copy("dve", Vtm[e][:, t, :], pp[:, 0:128])
                for d in range(2):
                    order = list(range(NT)) if d == 0 else [1, 0] + list(range(NT - 1, NCTX - 1, -1))
                    for e in range(2):
                        s.memset("dve", S32[e][:, :], 0.0)
                    sbc = [None, None]
                    for e in range(2):
                        sbc[e] = Sb[e].next()
                        s.memset("pool", sbc[e][:, :], 0.0)
                    for t in order:
                        tk = slice(t * 128, (t + 1) * 128)
                        sh = shr.next()
                        pG = s.psf.next()
                        s.mm(pG[:, 0:128], kT[:, 0, tk], kT[:, 0, tk], True, True)
                        for j in range(4):
                            s.tt("dve", sh[:, j, :], pG[:, 0:128], masks[:, (6 if d == 0 else 10) + j, :], ALU.mult)
                        pQ = s.psf.next()
                        s.mm(pQ[:, 0:128], kT[:, 0, tk], qT[:, 0, tk], True, True)
                        s.tt("dve", sh[:, 4, :], pQ[:, 0:128], masks[:, M_UI if d == 0 else M_LI, :], ALU.mult)
                        for e in range(2):
                            hv = hk * 2 + e
                            col = d * 32 + hv
                            gcc = gc[:, t, col:col + 1]
                            M = fr.next()
                            s.ts("pool", M[:, :], onesf[:, :], gcc, None, ALU.mult)
                            pR = s.psf.next()
                            s.tr(pR[:, 0:128], M[:, :], s.identf[:, :])
                            A1 = fr.next()
                            s.ts("dve", A1[:, :], pR[:, 0:128], gcc, 0.0, ALU.subtract, ALU.max)
                            B1 = fr.next()
                            s.ts("dve", B1[:, :], pR[:, 0:128], gcc, 0.0, ALU.subtract, ALU.min)
                            s.act(A1[:, :], A1[:, :], AF.Exp, scale=-1.0)
                            s.act(B1[:, :], B1[:, :], AF.Exp)
                            Pp = []
                            PTp = []
                            for j in range(4):
                                pj = fr.next()
                                s.stt(pj[:, :], A1[:, :], nbeta[:, t, col:col + 1], sh[:, j, :], ALU.mult, ALU.mult)
                                Pp.append(pj)
                            for j in range(4):
                                pt = s.psf.next()
                                s.tr(pt[:, 0:128], Pp[j][:, :], s.identf[:, :])
                                ptj = fr.next()
                                s.copy("act", ptj[:, :], pt[:, 0:128])
                                PTp.append(ptj)
                            AT = br.next()
                            s.tt("pool", AT[:, :], B1[:, :], sh[:, 4, :], ALU.mult)
                            P, PT = Pp[0], PTp[0]
                            X = fr.next()
                            Xt = fr.next()
                            s.tt("dve", X[:, :], P[:, :], s.identf[:, :], ALU.add)
                            s.tt("pool", Xt[:, :], PT[:, :], s.identf[:, :], ALU.add)
                            for lv in range(1, 4):
                                p1 = s.psf.next()
                                s.mm(p1[:, 0:128], PT[:, :], P[:, :], True, True)
                                p2 = s.psf.next()
                                s.mm(p2[:, 0:128], P[:, :], PT[:, :], True, True)
                                Pn = fr.next()
                                PTn = fr.next()
                                s.copy("act", Pn[:, :], p1[:, 0:128])
                                s.copy("dve", PTn[:, :], p2[:, 0:128])
                                p3 = s.psf.next()
                                s.mm(p3[:, 0:128], PTn[:, :], X[:, :], True, True)
                                p4 = s.psf.next()
                                s.mm(p4[:, 0:128], Pn[:, :], Xt[:, :], True, True)
                                s.tt("dve", X[:, :], X[:, :], p3[:, 0:128], ALU.add)
                                s.tt("dve", Xt[:, :], Xt[:, :], p4[:, 0:128], ALU.add)
                                P, PT = Pn, PTn
                            for j in range(1, 4):
                                if j < 3:
                                    pa = s.psf.next()
                                    s.mm(pa[:, 0:128], PTp[j][:, :], X[:, :], True, True)
                                    A1s = fr.next()
                                    s.copy("act", A1s[:, :], pa[:, 0:128])
                                pb_ = s.psf.next()
                                s.mm(pb_[:, 0:128], Pp[j][:, :], Xt[:, :], True, True)
                                B1s = fr.next()
                                s.copy("dve", B1s[:, :], pb_[:, 0:128])
                                if j < 3:
                                    pc = s.psf.next()
                                    s.mm(pc[:, 0:128], Xt[:, :], A1s[:, :], True, True)
                                pd = s.psf.next()
                                s.mm(pd[:, 0:128], X[:, :], B1s[:, :], True, True)
                                if j < 3:
                                    s.tt("dve", X[:, :], X[:, :], pc[:, 0:128], ALU.add)
                                s.tt("dve", Xt[:, :], Xt[:, :], pd[:, 0:128], ALU.add)
                            TT = Xt
                            TTb = br.next()
                            s.copy("act", TTb[:, :], TT[:, :])
                            vb = br.next()
                            s.ts("pool", vb[:, :], Vtm[e][:, t, :], beta[:, t, col:col + 1], None, ALU.mult)
                            kbg = br.next()
                            s.ts("pool", kbg[:, :], Ktm[:, t, :], beg[:, t, col:col + 1], None, ALU.mult)
                            kdl = br.next()
                            s.ts("pool", kdl[:, :], Ktm[:, t, :], egl[:, t, col:col + 1], None, ALU.mult)
                            pw = s.psf.next()
                            s.mm(pw[:, 0:128], kbg[:, :], TTb[:, :], True, True)
                            nw = br.next()
                            s.act(nw[:, :], pw[:, 0:128], AF.Copy, scale=-1.0)
                            pvn = s.psf.next()
                            s.mm(pvn[:, 0:128], TTb[:, :], vb[:, :], True, False)
                            s.mm(pvn[:, 0:128], nw[:, :], sbc[e][:, :], False, True)
                            vn = br.next()
                            s.copy("act", vn[:, :], pvn[:, 0:128])
                            if t >= NCTX:
                                po1 = s.psf.next()
                                s.mm(po1[:, 0:128], qT[:, 0, tk], sbc[e][:, :], True, True)
                                po2 = s.psf.next()
                                s.mm(po2[:, 0:128], AT[:, :], vn[:, :], True, True)
                                o2 = fr.next()
                                s.copy("act", o2[:, :], po2[:, 0:128])
                                if d == 0:
                                    s.stt(acc[e][:, t, :], po1[:, 0:128], eg[:, t, col:col + 1], o2[:, :], ALU.mult, ALU.add)
                                else:
                                    s.stt(o2[:, :], po1[:, 0:128], eg[:, t, col:col + 1], o2[:, :], ALU.mult, ALU.add)
                                    s.tt("pool", acc[e][:, t, :], acc[e][:, t, :], o2[:, :], ALU.add)
                            ps_ = s.psf.next()
                            s.mm(ps_[:, 0:128], kdl[:, :], vn[:, :], True, True)
                            s.stt(S32[e][:, :], S32[e][:, :], etot[:, t, col:col + 1], ps_[:, 0:128], ALU.mult, ALU.add)
                            sbc[e] = Sb[e].next()
                            s.copy("act", sbc[e][:, :], S32[e][:, :])
                for e in range(2):
                    hv = hk * 2 + e
                    for t in range(NCTX, NT):
                        sm = s.small.next()
                        s.act(junk[:, :], acc[e][:, t, :], AF.Square, accum=sm[:, 0:1])
                        s.ts("dve", sm[:, 1:2], sm[:, 0:1], 1.0 / 128, EPS, ALU.mult, ALU.add)
                        s.act(sm[:, 2:3], sm[:, 1:2], AF.Sqrt)
                        s.recip(sm[:, 3:4], sm[:, 2:3])
                        s.stt(acc[e][:, t, :], acc[e][:, t, :], sm[:, 3:4], ngb[:, :], ALU.mult, ALU.mult)
                        s.tt("pool", oh[e][:, t, :], acc[e][:, t, :], ztm[e][:, t, :], ALU.mult)
                    s.dma("pool", View(s.omix, ov[:, NCTX:NT, hv * 128:(hv + 1) * 128], list(range(NCTX, NT))),
                          oh[e][:, NCTX:NT, :])
        s.barrier()
        s.outproj_phase(li, "gdn_w_out", 4096, False)

    def build(s):
        s.cast_weights(s.layers[0])
        s.ada_phase()
        for i, li in enumerate(s.layers):
            kind = li % 4
            want_ctx = li < 2
            if i + 1 < len(s.layers):
                s.cast_weights(s.layers[i + 1])
            if kind == 3:
                s.sgu_mixer(li)
            elif kind == 1:
                s.att_mixer(li)
            elif kind == 0:
                s.ret_mixer(li)
            elif kind == 2:
                s.gdn_mixer(li)
            s.mlp_phase(li, want_ctx)
        s.final_phase()


def make_consts():
    c = {}
    c["ident"] = np.eye(128, dtype=np.float32)
    i = np.arange(128)[:, None]
    j = np.arange(128)[None, :]
    NEG = -30000.0
    mprev = np.where(j >= i, 0.0, NEG)
    mnext = np.where(j <= i, 0.0, NEG)
    c["mask3"] = np.concatenate([mprev, np.zeros((128, 128)), mnext], axis=1).astype(np.float32)

    def rope(hd, rep):
        rows = 2048 // 64
        row = np.repeat(np.arange(rows, dtype=np.float32), 64)
        col = np.tile(np.arange(64, dtype=np.float32), rows)
        ad = hd // 2
        inv = np.exp(np.float32(-math.log(10000.0)) * np.arange(0, ad, 2, dtype=np.float32) / np.float32(ad)).astype(np.float32)
        ang = np.concatenate([row[:, None] * inv, col[:, None] * inv], axis=-1).astype(np.float32)
        cs = np.cos(ang).astype(np.float32)
        sn = np.sin(ang).astype(np.float32)
        cs = np.tile(cs, (1, rep)).reshape(16, 128, -1)
        sn = np.tile(sn, (1, rep)).reshape(16, 128, -1)
        return np.ascontiguousarray(cs), np.ascontiguousarray(sn)
    c["cos_a"], c["sin_a"] = rope(128, 4)
    c["cos_r"], c["sin_r"] = rope(256, 2)
    tt_ = np.arange(128)[:, None]
    cc_ = np.arange(128)[None, :]
    lo = tt_ > cc_
    F = [lo & (tt_ // 16 == cc_ // 16),
         lo & (tt_ // 32 == cc_ // 32) & (tt_ // 16 != cc_ // 16),
         lo & (tt_ // 64 == cc_ // 64) & (tt_ // 32 != cc_ // 32),
         lo & (tt_ // 64 != cc_ // 64)]
    Bm = [f.T for f in F]
    c["gdn_masks"] = np.stack([tt_ <= cc_, tt_ >= cc_, tt_ > cc_, tt_ < cc_, cc_ >= tt_, cc_ <= tt_] + F + Bm).astype(np.float32)
    lg = np.log1p(-np.exp2(-5.0 - np.arange(8, dtype=np.float32))).astype(np.float32)
    pos = np.arange(128, dtype=np.float32)
    diff = np.abs(pos[:, None] - pos[None, :])
    Dm = np.exp(lg[:, None, None] * diff[None]).astype(np.float32)
    Dm[:, np.arange(128), np.arange(128)] = 2.0
    c["ret_D"] = np.ascontiguousarray(Dm)
    qd = np.stack([np.exp(lg[:, None] * (pos + 1.0)[None]), np.exp(lg[:, None] * (128.0 - pos)[None])], axis=1)
    c["ret_qdec"] = np.ascontiguousarray(qd.astype(np.float32))
    kd = np.stack([np.exp(lg[:, None] * (127.0 - pos)[None]), np.exp(lg[:, None] * pos[None])], axis=1)
    c["ret_kdec"] = np.ascontiguousarray(kd.astype(np.float32).transpose(2, 0, 1).reshape(128, 16))
    return c


_CACHE = {}


def kernel(**inputs):
    layers = inputs.pop("_layers", (0, 1, 2, 3))
    ncores = inputs.pop("_ncores", 8)
    consts = make_consts()
    key = (tuple(layers), ncores)
    if key not in _CACHE:
        _CACHE[key] = Model(layers=layers, consts=consts)
    m = _CACHE[key]
    x = np.asarray(inputs["x"], dtype=np.float32)
    ctx = np.asarray(inputs["ctx"], dtype=np.float32)
    c = np.asarray(inputs["c"], dtype=np.float32)
    cc = np.asarray(inputs["c_ctx"], dtype=np.float32)
    shared = {k: np.ascontiguousarray(np.asarray(inputs[k], dtype=np.float32)) for k in IN_SHAPES}
    shared.update(consts)
    in_maps = []
    for b in range(ncores):
        d = dict(shared)
        d["xin"] = np.ascontiguousarray(np.concatenate([ctx[b], x[b]], axis=0))
        d["c2"] = np.ascontiguousarray(np.stack([c[b], cc], axis=0))
        in_maps.append(d)
    res = run_bass_kernel_spmd(m.nc, in_maps, core_ids=list(range(ncores)))
    out = np.stack([np.asarray(r["y"], dtype=np.float32) for r in res.results], axis=0)
    return out
```

```python
import contextlib
import math
import os
import numpy as np
import concourse.bass as bass
import concourse.mybir as mybir
from concourse.bass_utils import run_bass_kernel_spmd

F32 = mybir.dt.float32
BF16 = mybir.dt.bfloat16
AF = mybir.ActivationFunctionType
ALU = mybir.AluOpType
AX = mybir.AxisListType

D = 2048
KC = 16
NT = 18
NCTX = 2
EPS = 1e-6
HID = 8192


class View:
    __slots__ = ("tl", "ap", "key")

    def __init__(s, tl, ap, key="*"):
        s.tl = tl
        s.ap = ap
        s.key = key

    def k(s, key):
        return View(s.tl, s.ap, key)


class Tl:
    def __init__(s, base, name):
        s.base = base
        s.name = name
        s.st = {}
        s.psum = False

    def __getitem__(s, idx):
        return View(s, s.base[idx])

    def v(s, ap, key="*"):
        return View(s, ap, key)


class _Sub:
    def __init__(s, tl, t):
        s.tl = tl
        s.t = t

    def __getitem__(s, idx):
        return View(s.tl, s.tl.base[:, s.t, :][idx])


class Ring:
    def __init__(s, tls):
        s.tls = tls
        s.i = 0

    def next(s):
        t = s.tls[s.i % len(s.tls)]
        s.i += 1
        return t


class Bld:
    def __init__(s):
        s.nc = bass.Bass("TRN2", target_bir_lowering=False)
        nc = s.nc
        s.es = contextlib.ExitStack()
        s.E = {"pe": nc.tensor, "act": nc.scalar, "dve": nc.vector, "pool": nc.gpsimd, "sp": nc.sync}
        s.sem = {}
        s.cnt = {}
        for e in ["pe", "act", "dve", "pool"]:
            s.sem[e] = s.es.enter_context(nc.semaphore("s_" + e))
            s.cnt[e] = 0
        s.dq = {}
        for q, n in [("sp", 10), ("pool", 16), ("act", 2)]:
            names = []
            for i in range(n):
                nm = "d_%s%d" % (q, i)
                s.sem[nm] = s.es.enter_context(nc.semaphore(nm))
                s.cnt[nm] = 0
                names.append(nm)
            s.dq[q] = names
        s.dqi = {"sp": 0, "pool": 0, "act": 0}
        s.known = {e: {} for e in s.E}
        s.ninst = 0
        s.uid = 0

    def sb(s, shape, dt, name=None, stack=None):
        s.uid += 1
        nm = "%s_%d" % (name or "t", s.uid)
        t = (stack or s.es).enter_context(s.nc.sbuf_tensor(nm, list(shape), dt))
        return Tl(t, nm)

    def ps(s, shape, dt, name=None):
        s.uid += 1
        nm = "%s_%d" % (name or "p", s.uid)
        t = s.es.enter_context(s.nc.psum_tensor(nm, list(shape), dt))
        tl = Tl(t, nm)
        tl.psum = True
        return tl

    def dram(s, name, shape, dt, kind="Internal"):
        t = s.nc.dram_tensor(name, list(shape), dt, kind=kind)
        return Tl(t.ap(), name)

    def _states(s, v):
        st = v.tl.st
        if "*" not in st:
            st["*"] = [{}, {}]
        if v.key == "*":
            return list(st.values())
        keys = v.key if isinstance(v.key, list) else [v.key]
        out = [st["*"]]
        for k in keys:
            if k not in st:
                st[k] = [{}, {}]
            out.append(st[k])
        return out

    def _need(s, eng, reads, writes, is_dma):
        need = {}

        def mg(d, skip):
            for src, val in d.items():
                if skip and src == eng:
                    continue
                if need.get(src, 0) < val:
                    need[src] = val

        for v in reads:
            for st in s._states(v):
                mg(st[0], False)
                if v.tl.psum:
                    mg(st[1], True)
        for v in writes:
            for st in s._states(v):
                mg(st[0], not is_dma)
                mg(st[1], not is_dma)
        if eng == "pe":
            need.pop("pe", None)
        return need

    def _wait(s, eng, need):
        kn = s.known[eng]
        for src, val in need.items():
            if kn.get(src, 0) < val:
                s.E[eng].wait_ge(s.sem[src], val)
                kn[src] = val
                s.ninst += 1

    def _upd(s, src, val, reads, writes):
        for v in reads:
            st = v.tl.st
            if v.key == "*":
                st["*"][1][src] = val
            else:
                for k in v.key if isinstance(v.key, list) else [v.key]:
                    st[k][1][src] = val
        for v in writes:
            st = v.tl.st
            if v.key == "*":
                st.clear()
                st["*"] = [{src: val}, {}]
            else:
                for k in v.key if isinstance(v.key, list) else [v.key]:
                    st[k] = [{src: val}, {}]

    def op(s, eng, fn, reads, writes):
        need = s._need(eng, reads, writes, False)
        s._wait(eng, need)
        inst = fn()
        s.cnt[eng] += 1
        inst.then_inc(s.sem[eng], 1)
        s.ninst += 1
        s._upd(eng, s.cnt[eng], reads, writes)
        return inst

    def dma(s, q, out, in_, **kw):
        need = s._need(q, [in_], [out], True)
        ring = s.dq[q]
        nm = ring[s.dqi[q] % len(ring)]
        s.dqi[q] += 1
        if s.cnt[nm] > 0:
            need[nm] = max(need.get(nm, 0), s.cnt[nm])
        s._wait(q, need)
        inst = s.E[q].dma_start(out=out.ap, in_=in_.ap, **kw)
        s.cnt[nm] += 16
        inst.then_inc(s.sem[nm], 16)
        s.ninst += 1
        s._upd(nm, s.cnt[nm], [in_], [out])

    def barrier(s, engines=None):
        for e in engines or list(s.E):
            need = {src: val for src, val in s.cnt.items() if val > 0}
            if e == "pe":
                need.pop("pe", None)
            s._wait(e, need)

    def mm(s, out, lhsT, rhs, start, stop):
        return s.op("pe", lambda: s.nc.tensor.matmul(out.ap, lhsT.ap, rhs.ap, start=start, stop=stop),
                    [lhsT, rhs], [out])

    def tr(s, out, in_, ident):
        return s.op("pe", lambda: s.nc.tensor.transpose(out.ap, in_.ap, ident.ap), [in_, ident], [out])

    def act(s, out, in_, func, bias=None, scale=None, accum=None, eng="act"):
        kw = {}
        reads = [in_]
        writes = [out]
        if bias is not None:
            if isinstance(bias, View):
                kw["bias"] = bias.ap
                reads.append(bias)
            else:
                kw["bias"] = bias
        if scale is not None:
            if isinstance(scale, View):
                kw["scale"] = scale.ap
                reads.append(scale)
            else:
                kw["scale"] = scale
        if accum is not None:
            kw["accum_out"] = accum.ap
            writes.append(accum)
        return s.op("act", lambda: s.nc.scalar.activation(out.ap, in_.ap, func, **kw), reads, writes)

    def tt(s, eng, out, in0, in1, op):
        return s.op(eng, lambda: s.E[eng].tensor_tensor(out.ap, in0.ap, in1.ap, op), [in0, in1], [out])

    def ts(s, eng, out, in0, s1, s2, op0, op1=None):
        reads = [in0]
        a1 = s1
        a2 = s2
        if isinstance(s1, View):
            reads.append(s1)
            a1 = s1.ap
        if isinstance(s2, View):
            reads.append(s2)
            a2 = s2.ap
        if op1 is None:
            return s.op(eng, lambda: s.E[eng].tensor_scalar(out.ap, in0.ap, a1, None, op0), reads, [out])
        return s.op(eng, lambda: s.E[eng].tensor_scalar(out.ap, in0.ap, a1, a2, op0, op1), reads, [out])

    def stt(s, out, in0, scalar, in1, op0, op1, eng="dve"):
        reads = [in0, in1]
        a = scalar
        if isinstance(scalar, View):
            reads.append(scalar)
            a = scalar.ap
        return s.op(eng, lambda: s.E[eng].scalar_tensor_tensor(out.ap, in0.ap, a, in1.ap, op0, op1), reads, [out])

    def copy(s, eng, out, in_):
        if eng == "act":
            return s.op("act", lambda: s.nc.scalar.copy(out.ap, in_.ap), [in_], [out])
        return s.op(eng, lambda: s.E[eng].tensor_copy(out.ap, in_.ap), [in_], [out])

    def memset(s, eng, out, val):
        return s.op(eng, lambda: s.E[eng].memset(out.ap, val), [], [out])

    def recip(s, out, in_):
        return s.op("dve", lambda: s.nc.vector.reciprocal(out.ap, in_.ap), [in_], [out])

    def rmax(s, out, in_):
        return s.op("dve", lambda: s.nc.vector.reduce_max(out.ap, in_.ap, axis=AX.X), [in_], [out])


GROUPS = [[0, 1], [2, 3, 4, 5, 6, 7], [8, 9, 10, 11, 12, 13], [14, 15, 16, 17]]
WSPEC = {
    0: [("ret_w_in", 2048, 12288), ("ret_w_out", 4096, 2048)],
    1: [("att_w_in", 2048, 3072), ("att_w_out", 2048, 2048)],
    2: [("gdn_w_in", 2048, 12416), ("gdn_w_out", 4096, 2048)],
    3: [("sgu_w_in", 2048, 8192), ("sgu_w_out", 4096, 2048)],
}
IN_SHAPES = {
    "mod_w": [4, 2048, 12288], "mod_b": [4, 12288], "norm1_g": [4, 2048], "norm2_g": [4, 2048],
    "mlp_up": [4, 2048, 8192], "mlp_down": [4, 8192, 2048], "final_g": [2048],
    "ret_w_in": [1, 2048, 12288], "ret_gn_g": [1, 4096], "ret_w_out": [1, 4096, 2048],
    "att_w_in": [1, 2048, 3072], "att_sink": [1, 16], "att_w_out": [1, 2048, 2048],
    "gdn_w_in": [1, 2048, 12416], "gdn_conv_w": [1, 4, 8192], "gdn_a_log": [1, 2, 32],
    "gdn_dt_bias": [1, 2, 32], "gdn_norm_g": [1, 128], "gdn_w_out": [1, 4096, 2048],
    "sgu_w_in": [1, 2048, 8192], "sgu_ln_g": [1, 4096], "sgu_ln_b": [1, 4096],
    "sgu_w_s": [1, 8, 128, 128], "sgu_b_s": [1, 8, 128], "sgu_w_out": [1, 4096, 2048],
}


class Model(Bld):
    def __init__(s, layers=(0, 1, 2, 3), final=True, consts=None):
        super().__init__()
        s.layers = list(layers)
        s.I = {}
        s.I["xin"] = s.dram("xin", [NT * 128, D], F32, "ExternalInput")
        s.I["c2"] = s.dram("c2", [2, D], F32, "ExternalInput")
        for k, shp in IN_SHAPES.items():
            s.I[k] = s.dram(k, shp, F32, "ExternalInput")
        for k, arr in (consts or {}).items():
            s.I[k] = s.dram(k, list(arr.shape), F32, "ExternalInput")
        s.out = s.dram("y", [16 * 128, D], F32, "ExternalOutput")
        s.xres = s.dram("xres", [NT * 128, D], F32)
        s.adav = {li: s.dram("adav%d" % li, [2, 6 * D], F32) for li in s.layers}
        s.proj = s.dram("proj", [NT * 128, 12416], BF16)
        s.projf = s.dram("projf", [NT * 128, 128], F32)
        s.omix = s.dram("omix", [NT * 128, 4096], BF16)
        s.mixT = s.dram("mixT", [64, 128, NT * 128], BF16)
        s.wb = {}
        for li in s.layers:
            for nm, k, n in WSPEC[li]:
                s.wb[nm] = s.dram(nm + "_bf", [k, n], BF16)
            s.wb["up%d" % li] = s.dram("up%d_bf" % li, [D, HID], BF16)
            s.wb["down%d" % li] = s.dram("down%d_bf" % li, [HID, D], BF16)
        s.psf = Ring([s.ps([128, 512], F32, "psf") for _ in range(6)])
        s.psb = Ring([s.ps([128, 1024], BF16, "psb") for _ in range(2)])
        s.identf = s.sb([128, 128], F32, "identf")
        s.identb = s.sb([128, 128], BF16, "identb")
        s.dma("sp", s.identf[:, :], s.I["ident"][:, :])
        s.copy("dve", s.identb[:, :], s.identf[:, :])
        s.small = Ring([s.sb([128, 8], F32, "sm") for _ in range(12)])
        s.evi = 0
        s.xsrc = s.I["xin"]
        s.build()

    def ev_eng(s):
        s.evi += 1
        return "act" if s.evi % 2 else "dve"

    def load_bc(s, dst, src_tl, ap1d, q="sp", np_=128):
        s.dma(q, dst, View(src_tl, ap1d.partition_broadcast(np_)))

    def cast_weights(s, li):
        if li not in s.layers:
            return
        lst = [(nm, s.I[nm], s.I[nm].base[0], k, n) for nm, k, n in WSPEC[li]]
        lst.insert(1, ("up%d" % li, s.I["mlp_up"], s.I["mlp_up"].base[li], D, HID))
        lst.append(("down%d" % li, s.I["mlp_down"], s.I["mlp_down"].base[li], HID, D))
        for nm, tl, src, k, n in lst:
            for r in range(0, k, 256):
                s.dma("pool", s.wb[nm][r:r + 256, :].k(r), View(tl, src[r:r + 256, :]))

    def ada_phase(s):
        with contextlib.ExitStack() as st:
            c2t = s.sb([2, D], F32, "c2t", st)
            sc = s.sb([2, D], F32, "sc", st)
            scT = s.sb([128, KC, 2], F32, "scT", st)
            wr = Ring([s.sb([128, KC, 512], F32, "adw", st) for _ in range(2)])
            mbr = Ring([s.sb([2, 512], F32, "mb", st) for _ in range(2)])
            orr = Ring([s.sb([2, 512], F32, "ao", st) for _ in range(2)])
            s.dma("sp", c2t[:, :], s.I["c2"][:, :])
            s.act(sc[:, :], c2t[:, :], AF.Silu)
            p = s.psf.next()
            for kc in range(KC):
                s.tr(p[:, kc * 2:(kc + 1) * 2], sc[0:2, kc * 128:(kc + 1) * 128], s.identf[0:2, 0:2])
            s.copy("dve", scT[:, :, :], View(p, p.base[:, 0:32].rearrange("p (k r) -> p k r", r=2)))
            for li in s.layers:
                wv = s.I["mod_w"].base[li].rearrange("(kc p) n -> p kc n", p=128)
                for c0 in range(0, 6 * D, 512):
                    wt = wr.next()
                    s.dma("sp", wt[:, :, :], View(s.I["mod_w"], wv[:, :, c0:c0 + 512]))
                    mb = mbr.next()
                    s.load_bc(mb[:, :], s.I["mod_b"], s.I["mod_b"].base[li, c0:c0 + 512], np_=2)
                    p = s.psf.next()
                    for kc in range(KC):
                        s.mm(p[0:2, :], scT[:, kc, :], wt[:, kc, :], kc == 0, kc == KC - 1)
                    o = orr.next()
                    s.tt("dve", o[:, :], p[0:2, :], mb[:, :], ALU.add)
                    s.dma("pool", s.adav[li][:, c0:c0 + 512], o[:, :])
        s.barrier()

    def mod_vecs(s, li, r, which, gm, sh, gt, tmp):
        a = s.adav[li]
        o = 0 if which == 1 else 3
        ng = s.I["norm1_g" if which == 1 else "norm2_g"]
        s.load_bc(sh[:, :], a, a.base[r, (o + 0) * D:(o + 1) * D])
        s.load_bc(tmp[:, :], a, a.base[r, (o + 1) * D:(o + 2) * D])
        s.load_bc(gm[:, :], ng, ng.base[li, :])
        s.stt(gm[:, :], tmp[:, :], 1.0, gm[:, :], ALU.add, ALU.mult)
        if gt is not None:
            s.load_bc(gt[:, :], a, a.base[r, (o + 2) * D:(o + 3) * D])

    def norm_tile(s, xt, gm, sh, h32, hb, junk):
        ss = s.small.next()
        s.act(junk[:, :], xt, AF.Square, accum=ss[:, 0:1])
        s.ts("dve", ss[:, 1:2], ss[:, 0:1], 1.0 / D, EPS, ALU.mult, ALU.add)
        s.act(ss[:, 2:3], ss[:, 1:2], AF.Sqrt)
        s.recip(ss[:, 3:4], ss[:, 2:3])
        s.stt(h32[:, :], xt, ss[:, 3:4], gm, ALU.mult, ALU.mult)
        s.tt("pool", hb[:, :], h32[:, :], sh, ALU.add)

    def transpose_tile(s, src, ncol, dst, kc0, tok0):
        nb = ncol // 128
        for b0 in range(0, nb, 8):
            n = min(8, nb - b0)
            p = s.psb.next()
            for j in range(n):
                s.tr(p[:, j * 128:(j + 1) * 128], src[:, (b0 + j) * 128:(b0 + j + 1) * 128], s.identb[:, :])
            s.copy(s.ev_eng(), dst[:, kc0 + b0:kc0 + b0 + n, tok0:tok0 + 128],
                   View(p, p.base[:, 0:n * 128].rearrange("p (j c) -> p j c", c=128)))

    def wview(s, nm):
        return s.wb[nm].base.rearrange("(kc p) n -> p kc n", p=128)

    def wkeys(s, k0, k1):
        return [r for r in range(0, 8192, 256) if r < k1 and r + 256 > k0]

    def gemm_tm(s, lhsT_fn, tiles, wname, K, n0, n1, evac, wring):
        kcn = K // 128
        wv = s.wview(wname)
        for c0 in range(n0, n1, 512):
            cw = min(512, n1 - c0)
            wt = wring.next()
            s.dma("sp", wt[:, 0:kcn, 0:cw], View(s.wb[wname], wv[:, :, c0:c0 + cw], s.wkeys(0, K)))
            for t in tiles:
                p = s.psf.next()
                for kc in range(kcn):
                    s.mm(p[:, 0:cw], lhsT_fn(t, kc), wt[:, kc, 0:cw], kc == 0, kc == kcn - 1)
                evac(t, c0, cw, p)

    def gemm_acc(s, lhsT_fn, tiles, wname, K, evac, wring):
        kcn = K // 128
        wv = s.wview(wname)
        assert len(tiles) <= 6
        for c0 in range(0, D, 512):
            banks = [s.psf.next() for _ in tiles]
            for hb in range(0, kcn, 16):
                wt = wring.next()
                s.dma("sp", wt[:, 0:16, :], View(s.wb[wname], wv[:, hb:hb + 16, c0:c0 + 512],
                                                  s.wkeys(hb * 128, (hb + 16) * 128)))
                for i, t in enumerate(tiles):
                    for kc in range(16):
                        s.mm(banks[i][:, :], lhsT_fn(i, hb + kc), wt[:, kc, :], hb + kc == 0, hb + kc == kcn - 1)
            for i, t in enumerate(tiles):
                evac(t, c0, banks[i])

    def resid_evac_fn(s, gt, xr, tr_):
        def evac(t, c0, p):
            xt = xr.next()
            s.dma("sp", xt[:, :], s.xsrc[t * 128:(t + 1) * 128, c0:c0 + 512].k((t, c0)))
            s.tt("dve", p[:, :], p[:, :], gt[:, c0:c0 + 512], ALU.mult)
            s.tt("dve", xt[:, :], xt[:, :], p[:, :], ALU.add)
            s.dma("pool", s.xres[t * 128:(t + 1) * 128, c0:c0 + 512].k((t, c0)), xt[:, :])
        return evac

    def xkeys(s, t):
        return [(t, c) for c in range(0, D, 512)]

    def outproj_phase(s, li, wname, K, do_ctx):
        kcn = K // 128
        with contextlib.ExitStack() as st:
            oT = s.sb([128, kcn, 6 * 128], BF16, "oT", st)
            gt = s.sb([128, D], F32, "g1", st)
            otr = Ring([s.sb([128, K], BF16, "ot", st) for _ in range(2)])
            wring = Ring([s.sb([128, 16, 512], BF16, "wo", st) for _ in range(2)])
            xr = Ring([s.sb([128, 512], F32, "xr", st) for _ in range(3)])
            tr_ = None
            a = s.adav[li]
            cur_r = None
            for g in GROUPS:
                r = 1 if g[0] < NCTX else 0
                if r == 1 and not do_ctx:
                    continue
                if r != cur_r:
                    s.load_bc(gt[:, :], a, a.base[r, 2 * D:3 * D])
                    cur_r = r
                for i, t in enumerate(g):
                    ot = otr.next()
                    s.dma("sp", ot[:, :], s.omix[t * 128:(t + 1) * 128, 0:K].k(t))
                    s.transpose_tile(ot, K, oT, 0, i * 128)
                s.gemm_acc(lambda i, kc: oT[:, kc, i * 128:(i + 1) * 128], g, wname, K,
                           s.resid_evac_fn(gt, xr, tr_), wring)
        s.xsrc = s.xres if do_ctx or True else s.xsrc
        s.barrier()

    def mlp_phase(s, li, do_ctx):
        with contextlib.ExitStack() as st:
            hT = s.sb([128, KC, 6 * 128], BF16, "hT2", st)
            upT = s.sb([128, 64, 6 * 128], BF16, "upT", st)
            gm = s.sb([128, D], F32, "gm2", st)
            sh = s.sb([128, D], F32, "sh2", st)
            gt = s.sb([128, D], F32, "g2", st)
            xtr = Ring([s.sb([128, D], F32, "xt", st) for _ in range(1)])
            h32 = s.sb([128, D], F32, "h32", st)
            hb = s.sb([128, D], BF16, "hb", st)
            junk = hb
            wring = Ring([s.sb([128, 16, 512], BF16, "wm", st) for _ in range(2)])
            xr = Ring([s.sb([128, 512], F32, "xr", st) for _ in range(3)])
            tr_ = None
            r32 = Ring([s.sb([128, 512], F32, "r32", st) for _ in range(2)])
            cur_r = None
            wv = s.wview("up%d" % li)
            for g in GROUPS:
                r = 1 if g[0] < NCTX else 0
                if r == 1 and not do_ctx:
                    continue
                if r != cur_r:
                    s.mod_vecs(li, r, 2, gm, sh, gt, h32)
                    cur_r = r
                ntok = len(g) * 128
                for i, t in enumerate(g):
                    xt = xtr.next()
                    s.dma("sp", xt[:, :], s.xsrc[t * 128:(t + 1) * 128, :].k(s.xkeys(t)))
                    s.norm_tile(xt[:, :], gm[:, :], sh[:, :], h32, hb, junk)
                    s.transpose_tile(hb, D, hT, 0, i * 128)
                for c0 in range(0, HID, 512):
                    wt = wring.next()
                    s.dma("sp", wt[:, :, :], View(s.wb["up%d" % li], wv[:, :, c0:c0 + 512], s.wkeys(0, D)))
                    for j in range(4):
                        hc = c0 // 128 + j
                        for tb in range(0, ntok, 512):
                            tw = min(512, ntok - tb)
                            p = s.psf.next()
                            for kc in range(KC):
                                s.mm(p[:, 0:tw], wt[:, kc, j * 128:(j + 1) * 128], hT[:, kc, tb:tb + tw],
                                     kc == 0, kc == KC - 1)
                            rr = r32.next()
                            s.act(rr[:, 0:tw], p[:, 0:tw], AF.Relu)
                            s.tt("pool" if (hc + tb // 512) % 2 else "dve", upT[:, hc, tb:tb + tw],
                                 rr[:, 0:tw], rr[:, 0:tw], ALU.mult)
                s.gemm_acc(lambda i, kc: upT[:, kc, i * 128:(i + 1) * 128], g, "down%d" % li, HID,
                           s.resid_evac_fn(gt, xr, tr_), wring)
        s.barrier()

    def n1_phase(s, li, tiles, hT, st):
        with contextlib.ExitStack() as st2:
            gm = s.sb([128, D], F32, "gm1", st2)
            sh = s.sb([128, D], F32, "sh1", st2)
            xtr = Ring([s.sb([128, D], F32, "xt", st2) for _ in range(2)])
            h32 = s.sb([128, D], F32, "h32", st2)
            hb = s.sb([128, D], BF16, "hb", st2)
            junk = hb
            cur_r = None
            for t in tiles:
                r = 1 if t < NCTX else 0
                if r != cur_r:
                    s.mod_vecs(li, r, 1, gm, sh, None, h32)
                    cur_r = r
                xt = xtr.next()
                s.dma("sp", xt[:, :], s.xsrc[t * 128:(t + 1) * 128, :].k(s.xkeys(t)))
                s.norm_tile(xt[:, :], gm[:, :], sh[:, :], h32, hb, junk)
                s.transpose_tile(hb, D, hT, 0, t * 128)
            s.barrier()

    def final_phase(s):
        with contextlib.ExitStack() as st:
            gm = s.sb([128, D], F32, "fg", st)
            xtr = Ring([s.sb([128, D], F32, "xt", st) for _ in range(2)])
            otr = Ring([s.sb([128, D], F32, "ot", st) for _ in range(2)])
            junk = s.sb([128, D], BF16, "junk", st)
            s.load_bc(gm[:, :], s.I["final_g"], s.I["final_g"].base[:])
            off = NCTX if os.environ.get("DBGCTX") != "1" else 0
            for t in range(off, off + 16):
                xt = xtr.next()
                s.dma("sp", xt[:, :], s.xsrc[t * 128:(t + 1) * 128, :].k(s.xkeys(t)))
                ss = s.small.next()
                s.act(junk[:, :], xt[:, :], AF.Square, accum=ss[:, 0:1])
                s.ts("dve", ss[:, 1:2], ss[:, 0:1], 1.0 / D, EPS, ALU.mult, ALU.add)
                s.act(ss[:, 2:3], ss[:, 1:2], AF.Sqrt)
                s.recip(ss[:, 3:4], ss[:, 2:3])
                ot = otr.next()
                s.stt(ot[:, :], xt[:, :], ss[:, 3:4], gm[:, :], ALU.mult, ALU.mult)
                s.dma("pool", s.out[(t - off) * 128:(t - off + 1) * 128, :], ot[:, :])
        s.barrier()

    def sgu_mixer(s, li):
        tiles = list(range(NCTX, NT))
        with contextlib.ExitStack() as st:
            hT = s.sb([128, KC, NT * 128], BF16, "hT", st)
            s.n1_phase(li, tiles, hT, st)
            with contextlib.ExitStack() as st2:
                wring = Ring([s.sb([128, 16, 512], BF16, "wi", st2) for _ in range(2)])
                zr = Ring([s.sb([128, 512], BF16, "z", st2) for _ in range(4)])

                def evac(t, c0, cw, p):
                    z = zr.next()
                    s.act(z[:, 0:cw], p[:, 0:cw], AF.Gelu)
                    s.dma("pool", s.proj[t * 128:(t + 1) * 128, c0:c0 + cw].k((t, c0)), z[:, 0:cw])
                s.gemm_tm(lambda t, kc: hT[:, kc, t * 128:(t + 1) * 128], tiles, "sgu_w_in", D, 0, 8192, evac, wring)
            s.barrier()
        with contextlib.ExitStack() as st:
            lg = s.sb([128, 4096], F32, "lng", st)
            lb = s.sb([128, 4096], F32, "lnb", st)
            s.load_bc(lg[:, :], s.I["sgu_ln_g"], s.I["sgu_ln_g"].base[0, :])
            s.load_bc(lb[:, :], s.I["sgu_ln_b"], s.I["sgu_ln_b"].base[0, :])
            wsT = s.sb([128, 8, 128], BF16, "wsT", st)
            bs = s.sb([128, 8], F32, "bs", st)
            wsf = s.sb([128, 8, 128], F32, "wsf", st)
            wsb = s.sb([128, 8, 128], BF16, "wsb", st)
            bsr = s.sb([8, 128], F32, "bsr", st)
            s.dma("sp", wsf[:, :, :], View(s.I["sgu_w_s"], s.I["sgu_w_s"].base[0].rearrange("g p q -> p g q")))
            s.copy("dve", wsb[:, :, :], wsf[:, :, :])
            for g0 in range(0, 8, 4):
                p = s.psb.next()
                for j in range(4):
                    s.tr(p[:, j * 128:(j + 1) * 128], wsb[:, g0 + j, :], s.identb[:, :])
                s.copy("dve", wsT[:, g0:g0 + 4, :], View(p, p.base[:, 0:512].rearrange("p (j c) -> p j c", c=128)))
            s.dma("sp", bsr[:, :], s.I["sgu_b_s"][0, :, :])
            p = s.psf.next()
            s.tr(p[:, 0:8], bsr[0:8, :], s.identf[0:8, 0:8])
            s.copy("dve", bs[:, :], p[:, 0:8])
            zt = Ring([s.sb([128, 8192], BF16, "zt", st) for _ in range(2)])
            vn32 = s.sb([128, 4096], F32, "vn32", st)
            vnb = s.sb([128, 4096], BF16, "vnb", st)
            junk = s.sb([128, 4096], BF16, "junk", st)
            gor = Ring([s.sb([128, 4096], BF16, "go", st) for _ in range(2)])
            for t in tiles:
                z = zt.next()
                s.dma("sp", z[:, :], s.proj[t * 128:(t + 1) * 128, 0:8192].k([(t, c) for c in range(0, 8192, 512)]))
                ss = s.small.next()
                v = z[:, 4096:8192]
                s.act(junk[:, :], v, AF.Copy, accum=ss[:, 0:1])
                s.act(junk[:, :], v, AF.Square, accum=ss[:, 1:2])
                s.ts("dve", ss[:, 2:3], ss[:, 0:1], 1.0 / 4096, None, ALU.mult)
                s.tt("dve", ss[:, 3:4], ss[:, 2:3], ss[:, 2:3], ALU.mult)
                s.stt(ss[:, 4:5], ss[:, 1:2], 1.0 / 4096, ss[:, 3:4], ALU.mult, ALU.subtract)
                s.ts("dve", ss[:, 5:6], ss[:, 4:5], EPS, None, ALU.add)
                s.act(ss[:, 6:7], ss[:, 5:6], AF.Sqrt)
                s.recip(ss[:, 7:8], ss[:, 6:7])
                s.ts("dve", vn32[:, :], v, ss[:, 2:3], ss[:, 7:8], ALU.subtract, ALU.mult)
                s.tt("pool", vn32[:, :], vn32[:, :], lg[:, :], ALU.mult)
                s.tt("dve", vnb[:, :], vn32[:, :], lb[:, :], ALU.add)
                go = gor.next()
                for g in range(8):
                    p = s.psf.next()
                    s.mm(p[:, :], wsT[:, g, :], vnb[:, g * 512:(g + 1) * 512], True, True)
                    s.stt(go[:, g * 512:(g + 1) * 512], p[:, :], bs[:, g:g + 1], z[:, g * 512:(g + 1) * 512],
                          ALU.add, ALU.mult)
                s.dma("pool", s.omix[t * 128:(t + 1) * 128, :].k(t), go[:, :])
        s.barrier()
        s.outproj_phase(li, "sgu_w_out", 4096, False)


    def rope_evac(s, p, cw, t, scale, rope, cosT, sinT, xs, tmpr, ob):
        if not rope:
            s.act(ob[:, 0:cw], p[:, 0:cw], AF.Copy, scale=scale)
            return
        s.act(xs[:, 0:cw], p[:, 0:cw], AF.Copy, scale=scale)
        h = cw // 2
        x1 = View(xs, xs.base[:, 0:cw].rearrange("p (n two) -> p n two", two=2)[:, :, 0])
        x2 = View(xs, xs.base[:, 0:cw].rearrange("p (n two) -> p n two", two=2)[:, :, 1])
        y1 = View(ob, ob.base[:, 0:cw].rearrange("p (n two) -> p n two", two=2)[:, :, 0])
        y2 = View(ob, ob.base[:, 0:cw].rearrange("p (n two) -> p n two", two=2)[:, :, 1])
        c = cosT[:, t - NCTX, 0:h]
        sn = sinT[:, t - NCTX, 0:h]
        t1, t2, t3, t4 = [tmpr.next() for _ in range(4)]
        s.tt("dve", t1[:, 0:h], x1, c, ALU.mult)
        s.tt("pool", t2[:, 0:h], x2, sn, ALU.mult)
        s.tt("dve", y1, t1[:, 0:h], t2[:, 0:h], ALU.subtract)
        s.tt("pool", t3[:, 0:h], x1, sn, ALU.mult)
        s.tt("dve", t4[:, 0:h], x2, c, ALU.mult)
        s.tt("pool", y2, t3[:, 0:h], t4[:, 0:h], ALU.add)

    def att_mixer(s, li):
        tiles = list(range(NT))
        with contextlib.ExitStack() as st:
            hT = s.sb([128, KC, NT * 128], BF16, "hT", st)
            s.n1_phase(li, tiles, hT, st)
            with contextlib.ExitStack() as st2:
                wring = Ring([s.sb([128, 16, 512], BF16, "wi", st2) for _ in range(2)])
                cosT = s.sb([128, 16, 256], F32, "cos", st2)
                sinT = s.sb([128, 16, 256], F32, "sin", st2)
                s.dma("sp", cosT[:, :, :], View(s.I["cos_a"], s.I["cos_a"].base.rearrange("t p c -> p t c")))
                s.dma("sp", sinT[:, :, :], View(s.I["sin_a"], s.I["sin_a"].base.rearrange("t p c -> p t c")))
                xsr = Ring([s.sb([128, 512], F32, "xs", st2) for _ in range(2)])
                tmpr = Ring([s.sb([128, 256], F32, "rt", st2) for _ in range(8)])
                obr = Ring([s.sb([128, 512], BF16, "ob", st2) for _ in range(4)])

                def evac(t, c0, cw, p):
                    ob = obr.next()
                    isq = c0 < 2048
                    isv = c0 >= 2560
                    s.rope_evac(p, cw, t, (128 ** -0.5) if isq else 1.0, (not isv) and t >= NCTX,
                                cosT, sinT, xsr.next(), tmpr, ob)
                    s.dma("pool", s.proj[t * 128:(t + 1) * 128, c0:c0 + cw].k((t, c0)), ob[:, 0:cw])
                s.gemm_tm(lambda t, kc: hT[:, kc, t * 128:(t + 1) * 128], tiles, "att_w_in", D, 0, 3072, evac, wring)
            s.barrier()
        pv = s.proj.base.rearrange("(t p) c -> p t c", p=128)
        ov = s.omix.base.rearrange("(t p) c -> p t c", p=128)
        allk = lambda c0: [(t, c0) for t in range(NT)]
        with contextlib.ExitStack() as st:
            mask3 = s.sb([128, 384], F32, "mask3", st)
            s.dma("sp", mask3[:, :], s.I["mask3"][:, :])
            sinkb = s.sb([128, 16], F32, "sinkb", st)
            s.load_bc(sinkb[:, :], s.I["att_sink"], s.I["att_sink"].base[0, :])
            ktm = s.sb([128, NT, 128], BF16, "ktm", st)
            kT = s.sb([128, 1, NT * 128], BF16, "kT", st)
            vtm = Ring([s.sb([128, NT, 128], BF16, "vtm", st) for _ in range(2)])
            qtmr = Ring([s.sb([128, NT, 128], BF16, "qtm", st) for _ in range(2)])
            qTr = Ring([s.sb([128, 1, NT * 128], BF16, "qT", st) for _ in range(2)])
            ohr = Ring([s.sb([128, NT, 128], BF16, "oh", st) for _ in range(2)])
            ssb = Ring([s.sb([128, 640], F32, "ssb", st) for _ in range(3)])
            pbr = Ring([s.sb([128, 640], BF16, "pb", st) for _ in range(3)])
            ptr = Ring([s.sb([128, 5, 128], BF16, "pt", st) for _ in range(3)])
            for g in range(4):
                s.dma("sp", ktm[:, :, :], View(s.proj, pv[:, :, 2048 + g * 128:2048 + (g + 1) * 128], allk(2048)))
                vt = vtm.next()
                s.dma("sp", vt[:, :, :], View(s.proj, pv[:, :, 2560 + g * 128:2560 + (g + 1) * 128], allk(2560)))
                for t in range(NT):
                    s.transpose_tile(_Sub(ktm, t), 128, kT, 0, t * 128)
                for hh in range(4):
                    h = g * 4 + hh
                    qtm = qtmr.next()
                    s.dma("sp", qtm[:, :, :], View(s.proj, pv[:, :, h * 128:(h + 1) * 128], allk((h // 4) * 512)))
                    qT = qTr.next()
                    for t in range(NT):
                        s.transpose_tile(_Sub(qtm, t), 128, qT, 0, t * 128)
                    oh = ohr.next()
                    for t in range(NT):
                        qv = qT[:, 0, t * 128:(t + 1) * 128]
                        sS = ssb.next()
                        if t < NCTX:
                            pa = s.psf.next()
                            s.mm(pa[:, 0:256], qv, kT[:, 0, 0:256], True, True)
                            s.copy("act", sS[:, 0:256], pa[:, 0:256])
                            n = 256
                            ktiles = [0, 1]
                        else:
                            lo, hi = max(NCTX, t - 1), min(NT - 1, t + 1)
                            nl = hi - lo + 1
                            pa = s.psf.next()
                            s.mm(pa[:, 0:256], qv, kT[:, 0, 0:256], True, True)
                            pb_ = s.psf.next()
                            s.mm(pb_[:, 0:nl * 128], qv, kT[:, 0, lo * 128:(hi + 1) * 128], True, True)
                            s.copy("act", sS[:, 0:256], pa[:, 0:256])
                            m0 = (lo - (t - 1)) * 128
                            s.tt("dve", sS[:, 256:256 + nl * 128], pb_[:, 0:nl * 128], mask3[:, m0:m0 + nl * 128], ALU.add)
                            n = 256 + nl * 128
                            ktiles = [0, 1] + list(range(lo, hi + 1))
                        sm = s.small.next()
                        s.rmax(sm[:, 0:1], sS[:, 0:n])
                        s.ts("dve", sm[:, 1:2], sm[:, 0:1], sinkb[:, h:h + 1], -1.0, ALU.max, ALU.mult)
                        pb = pbr.next()
                        s.act(pb[:, 0:n], sS[:, 0:n], AF.Exp, bias=sm[:, 1:2], accum=sm[:, 2:3])
                        s.act(sm[:, 3:4], sinkb[:, h:h + 1], AF.Exp, bias=sm[:, 1:2])
                        s.tt("dve", sm[:, 4:5], sm[:, 2:3], sm[:, 3:4], ALU.add)
                        s.recip(sm[:, 5:6], sm[:, 4:5])
                        nk = n // 128
                        pT = ptr.next()
                        pp = s.psb.next()
                        for j in range(nk):
                            s.tr(pp[:, j * 128:(j + 1) * 128], pb[:, j * 128:(j + 1) * 128], s.identb[:, :])
                        s.copy("act", pT[:, 0:nk, :], View(pp, pp.base[:, 0:nk * 128].rearrange("p (j c) -> p j c", c=128)))
                        po = s.psf.next()
                        for j in range(nk):
                            s.mm(po[:, 0:128], pT[:, j, :], vt[:, ktiles[j], :], j == 0, j == nk - 1)
                        s.ts("dve", oh[:, t, :], po[:, 0:128], sm[:, 5:6], None, ALU.mult)
                    s.dma("pool", View(s.omix, ov[:, :, h * 128:(h + 1) * 128], list(range(NT))), oh[:, :, :])
        s.barrier()
        s.outproj_phase(li, "att_w_out", 2048, True)


    def ret_mixer(s, li):
        tiles = list(range(NT))
        with contextlib.ExitStack() as st:
            hT = s.sb([128, KC, NT * 128], BF16, "hT", st)
            s.n1_phase(li, tiles, hT, st)
            with contextlib.ExitStack() as st2:
                wring = Ring([s.sb([128, 16, 512], BF16, "wi", st2) for _ in range(2)])
                cosT = s.sb([128, 16, 256], F32, "cos", st2)
                sinT = s.sb([128, 16, 256], F32, "sin", st2)
                s.dma("sp", cosT[:, :, :], View(s.I["cos_r"], s.I["cos_r"].base.rearrange("t p c -> p t c")))
                s.dma("sp", sinT[:, :, :], View(s.I["sin_r"], s.I["sin_r"].base.rearrange("t p c -> p t c")))
                xsr = Ring([s.sb([128, 512], F32, "xs", st2) for _ in range(2)])
                tmpr = Ring([s.sb([128, 256], F32, "rt", st2) for _ in range(8)])
                obr = Ring([s.sb([128, 512], BF16, "ob", st2) for _ in range(4)])

                def evac(t, c0, cw, p):
                    ob = obr.next()
                    if c0 >= 8192:
                        s.act(ob[:, 0:cw], p[:, 0:cw], AF.Silu)
                    else:
                        isk = 2048 <= c0 < 4096
                        s.rope_evac(p, cw, t, (256 ** -0.5) if isk else 1.0, c0 < 4096 and t >= NCTX,
                                    cosT, sinT, xsr.next(), tmpr, ob)
                    s.dma("pool", s.proj[t * 128:(t + 1) * 128, c0:c0 + cw].k((t, c0)), ob[:, 0:cw])
                s.gemm_tm(lambda t, kc: hT[:, kc, t * 128:(t + 1) * 128], tiles, "ret_w_in", D, 0, 12288, evac, wring)
            s.barrier()
        pv = s.proj.base.rearrange("(t p) c -> p t c", p=128)
        ov = s.omix.base.rearrange("(t p) c -> p t c", p=128)
        allk = lambda c0: [(t, (c0 // 512) * 512) for t in range(NT)]
        lg = np.log1p(-np.exp2(-5.0 - np.arange(8, dtype=np.float32))).astype(np.float32)
        with contextlib.ExitStack() as st:
            qtm = s.sb([128, NT, 256], BF16, "qtm", st)
            ktm = s.sb([128, NT, 256], BF16, "ktm", st)
            vtm = s.sb([128, NT, 512], BF16, "vtm", st)
            gtm = s.sb([128, NT, 512], BF16, "gtm", st)
            oh = s.sb([128, NT, 512], BF16, "oh", st)
            qT = s.sb([128, 2, NT * 128], BF16, "qT", st)
            kT = s.sb([128, 2, NT * 128], BF16, "kT", st)
            sbst = s.sb([128, NT, 2, 512], BF16, "sbst", st)
            S32 = s.sb([128, 2, 512], F32, "S32", st)
            Sfb = Ring([s.sb([128, 2, 512], BF16, "Sfb", st) for _ in range(2)])
            Dm = s.sb([128, 128], F32, "Dm", st)
            qdec = s.sb([128, 2, 128], F32, "qdec", st)
            kdec = s.sb([128, 16], F32, "kdec", st)
            gng = s.sb([128, 512], F32, "gng", st)
            s.dma("sp", kdec[:, :], s.I["ret_kdec"][:, :])
            ksr = Ring([s.sb([128, 256], BF16, "ks", st) for _ in range(3)])
            ptr = Ring([s.sb([128, 128], BF16, "PT", st) for _ in range(2)])
            qfr = Ring([s.sb([128, 2, 128], BF16, "qf", st) for _ in range(4)])
            o32 = Ring([s.sb([128, 512], F32, "o32", st) for _ in range(2)])
            junk = s.sb([128, 512], BF16, "junk", st)
            for h in range(8):
                cd = float(np.exp(lg[h] * np.float32(128.0)))
                s.dma("sp", qtm[:, :, :], View(s.proj, pv[:, :, h * 256:(h + 1) * 256], allk(h * 256)))
                s.dma("sp", ktm[:, :, :], View(s.proj, pv[:, :, 2048 + h * 256:2048 + (h + 1) * 256], allk(2048 + h * 256)))
                s.dma("sp", vtm[:, :, :], View(s.proj, pv[:, :, 4096 + h * 512:4096 + (h + 1) * 512], allk(4096 + h * 512)))
                s.dma("sp", gtm[:, :, :], View(s.proj, pv[:, :, 8192 + h * 512:8192 + (h + 1) * 512], allk(8192 + h * 512)))
                s.dma("sp", Dm[:, :], s.I["ret_D"][h, :, :])
                s.load_bc(qdec[:, 0, :], s.I["ret_qdec"], s.I["ret_qdec"].base[h, 0, :])
                s.load_bc(qdec[:, 1, :], s.I["ret_qdec"], s.I["ret_qdec"].base[h, 1, :])
                s.load_bc(gng[:, :], s.I["ret_gn_g"], s.I["ret_gn_g"].base[0, h * 512:(h + 1) * 512])
                for t in range(NT):
                    s.transpose_tile(_Sub(qtm, t), 256, qT, 0, t * 128)
                    s.transpose_tile(_Sub(ktm, t), 256, kT, 0, t * 128)

                def upd(t, col):
                    ks = ksr.next()
                    s.ts("pool", ks[:, :], ktm[:, t, :], kdec[:, col:col + 1], None, ALU.mult)
                    for dc in range(2):
                        p = s.psf.next()
                        s.mm(p[:, :], ks[:, dc * 128:(dc + 1) * 128], vtm[:, t, :], True, True)
                        s.stt(S32[:, dc, :], S32[:, dc, :], cd, p[:, :], ALU.mult, ALU.add)
                s.memset("dve", S32[:, :, :], 0.0)
                for t in [1, 0] + list(range(NT - 1, NCTX - 1, -1)):
                    s.copy("act", sbst[:, t, :, :], S32[:, :, :])
                    upd(t, h * 2 + 1)
                s.memset("dve", S32[:, :, :], 0.0)
                for t in range(NT):
                    sf = Sfb.next()
                    s.copy("act", sf[:, :, :], S32[:, :, :])
                    tk = slice(t * 128, (t + 1) * 128)
                    p1 = s.psf.next()
                    for dc in range(2):
                        s.mm(p1[:, 0:128], kT[:, dc, tk], qT[:, dc, tk], dc == 0, dc == 1)
                    PT = ptr.next()
                    s.tt("dve", PT[:, :], p1[:, 0:128], Dm[:, :], ALU.mult)
                    qf = qfr.next()
                    qb = qfr.next()
                    for dc in range(2):
                        s.tt("pool", qf[:, dc, :], qT[:, dc, tk], qdec[:, 0, :], ALU.mult)
                        s.tt("pool", qb[:, dc, :], qT[:, dc, tk], qdec[:, 1, :], ALU.mult)
                    po = s.psf.next()
                    s.mm(po[:, :], PT[:, :], vtm[:, t, :], True, False)
                    for dc in range(2):
                        s.mm(po[:, :], qf[:, dc, :], sf[:, dc, :], False, False)
                    for dc in range(2):
                        s.mm(po[:, :], qb[:, dc, :], sbst[:, t, dc, :], False, dc == 1)
                    sm = s.small.next()
                    s.act(junk[:, :], po[:, :], AF.Square, accum=sm[:, 0:1])
                    s.ts("dve", sm[:, 1:2], sm[:, 0:1], 1.0 / 512, EPS, ALU.mult, ALU.add)
                    s.act(sm[:, 2:3], sm[:, 1:2], AF.Sqrt)
                    s.recip(sm[:, 3:4], sm[:, 2:3])
                    o = o32.next()
                    s.stt(o[:, :], po[:, :], sm[:, 3:4], gng[:, :], ALU.mult, ALU.mult)
                    s.tt("pool", oh[:, t, :], o[:, :], gtm[:, t, :], ALU.mult)
                    upd(t, h * 2 + 0)
                s.dma("pool", View(s.omix, ov[:, :, h * 512:(h + 1) * 512], list(range(NT))), oh[:, :, :])
        s.barrier()
        s.outproj_phase(li, "ret_w_out", 4096, True)


    def gdn_mixer(s, li):
        tiles = list(range(NT))
        XW = 2310
        mixT = s.mixT
        with contextlib.ExitStack() as st:
            hT = s.sb([128, KC, NT * 128], BF16, "hT", st)
            s.n1_phase(li, tiles, hT, st)
            with contextlib.ExitStack() as st2:
                wring = Ring([s.sb([128, 16, 512], BF16, "wi", st2) for _ in range(2)])
                X = s.sb([128, XW], F32, "X", st2)
                Y = s.sb([128, NT * 128], F32, "Y", st2)
                Z = s.sb([128, NT * 128], F32, "Z", st2)
                SQ = s.sb([128, NT * 128], F32, "SQ", st2)
                OB = Ring([s.sb([128, NT * 128], BF16, "OB", st2) for _ in range(2)])
                rn = Ring([s.sb([128, 512], F32, "rn", st2) for _ in range(2)])
                onesf = s.sb([128, 128], F32, "onesf", st2)
                s.memset("dve", onesf[:, :], 1.0)
                s.memset("dve", X[:, :], 0.0)
                cw4 = s.sb([4, 8192], F32, "cw4", st2)
                cwT = s.sb([128, 64, 4], F32, "cwT", st2)
                s.dma("sp", cw4[:, :], s.I["gdn_conv_w"][0, :, :])
                p = s.psf.next()
                for j in range(64):
                    s.tr(p[:, j * 4:(j + 1) * 4], cw4[0:4, j * 128:(j + 1) * 128], s.identf[0:4, 0:4])
                s.copy("dve", cwT[:, :, :], View(p, p.base[:, 0:256].rearrange("p (j k) -> p j k", k=4)))
                wv = s.wview("gdn_w_in")
                blocks = [(0, 256)] + [(256 + i * 512, 512) for i in range(4)]
                xcol = lambda tok: tok + 2 if tok < 256 else tok + 5
                for c0 in range(0, 8192, 512):
                    wt = wring.next()
                    s.dma("sp", wt[:, :, :], View(s.wb["gdn_w_in"], wv[:, :, c0:c0 + 512], s.wkeys(0, D)))
                    for j in range(4):
                        ch = c0 // 128 + j
                        for tb, tw in blocks:
                            p = s.psf.next()
                            for kc in range(KC):
                                s.mm(p[:, 0:tw], wt[:, kc, j * 128:(j + 1) * 128], hT[:, kc, tb:tb + tw], kc == 0, kc == KC - 1)
                            s.copy("act", X[:, xcol(tb):xcol(tb) + tw], p[:, 0:tw])
                        for (t0, n) in [(0, 256), (256, 2048)]:
                            x0 = xcol(t0)
                            s.ts("dve", Y[:, t0:t0 + n], X[:, x0 - 2:x0 - 2 + n], cwT[:, ch, 0:1], None, ALU.mult)
                            for k in range(1, 4):
                                s.stt(Y[:, t0:t0 + n], X[:, x0 - 2 + k:x0 - 2 + k + n], cwT[:, ch, k:k + 1], Y[:, t0:t0 + n],
                                      ALU.mult, ALU.add)
                        ob = OB.next()
                        if ch >= 32:
                            s.act(ob[:, :], Y[:, :], AF.Silu)
                        else:
                            s.act(Z[:, :], Y[:, :], AF.Silu)
                            s.tt("pool", SQ[:, :], Z[:, :], Z[:, :], ALU.mult)
                            for tb, tw in blocks:
                                p = s.psf.next()
                                s.mm(p[:, 0:tw], onesf[:, :], SQ[:, tb:tb + tw], True, True)
                                r = rn.next()
                                s.ts("dve", r[:, 0:tw], p[:, 0:tw], 1e-6, None, ALU.add)
                                s.act(r[:, 0:tw], r[:, 0:tw], AF.Sqrt)
                                s.recip(r[:, 0:tw], r[:, 0:tw])
                                s.stt(ob[:, tb:tb + tw], Z[:, tb:tb + tw], (128 ** -0.5) if ch < 16 else 1.0, r[:, 0:tw],
                                      ALU.mult, ALU.mult)
                        s.dma("pool", mixT[ch, :, :].k(ch), ob[:, :])
                zr = Ring([s.sb([128, 512], BF16, "z", st2) for _ in range(3)])
                fr = Ring([s.sb([128, 128], F32, "f", st2) for _ in range(2)])

                def evac(t, c0, cw, p):
                    if c0 < 12288:
                        z = zr.next()
                        s.act(z[:, 0:cw], p[:, 0:cw], AF.Silu)
                        s.dma("pool", s.proj[t * 128:(t + 1) * 128, c0 - 8192:c0 - 8192 + cw].k((t, c0 - 8192)), z[:, 0:cw])
                    else:
                        f = fr.next()
                        s.copy("dve", f[:, 0:cw], p[:, 0:cw])
                        s.dma("pool", s.projf[t * 128:(t + 1) * 128, 0:cw].k(t), f[:, 0:cw])
                s.gemm_tm(lambda t, kc: hT[:, kc, t * 128:(t + 1) * 128], tiles, "gdn_w_in", D, 8192, 12416, evac, wring)
            s.barrier()
        if os.environ.get("GDNDBG") == "1":
            s.outproj_phase(li, "gdn_w_out", 4096, False)
            return
        pv = s.proj.base.rearrange("(t p) c -> p t c", p=128)
        ov = s.omix.base.rearrange("(t p) c -> p t c", p=128)
        with contextlib.ExitStack() as st:
            f32t = lambda nm, shp: s.sb(shp, F32, nm, st)
            beta = f32t("beta", [128, NT, 64])
            nbeta = f32t("nbeta", [128, NT, 64])
            gc = f32t("gc", [128, NT, 64])
            eg = f32t("eg", [128, NT, 64])
            egl = f32t("egl", [128, NT, 64])
            etot = f32t("etot", [128, NT, 64])
            beg = f32t("beg", [128, NT, 64])
            onesf = f32t("onesf", [128, 128])
            masks = f32t("masks", [128, 14, 128])
            ngb = f32t("ngb", [128, 128])
            st_g = contextlib.ExitStack()
            f32g = lambda nm, shp: s.sb(shp, F32, nm, st_g)
            raw = f32g("raw", [128, NT, 128])
            g = f32g("g", [128, NT, 64])
            t1 = f32g("t1", [128, NT, 64])
            t2 = f32g("t2", [128, NT, 64])
            tot = f32g("tot", [128, NT, 64])
            alb = f32g("alb", [128, 64])
            dtb = f32g("dtb", [128, 64])
            nA = f32g("nA", [128, 64])
            s.dma("sp", raw[:, :, :], View(s.projf, s.projf.base.rearrange("(t p) c -> p t c", p=128), list(range(NT))))
            s.memset("dve", onesf[:, :], 1.0)
            s.dma("sp", masks[:, :, :], View(s.I["gdn_masks"], s.I["gdn_masks"].base.rearrange("k p c -> p k c")))
            s.load_bc(alb[:, :], s.I["gdn_a_log"], s.I["gdn_a_log"].base[0].rearrange("a b -> (a b)"))
            s.load_bc(dtb[:, :], s.I["gdn_dt_bias"], s.I["gdn_dt_bias"].base[0].rearrange("a b -> (a b)"))
            s.load_bc(ngb[:, :], s.I["gdn_norm_g"], s.I["gdn_norm_g"].base[0, :])
            s.act(nA[:, :], alb[:, :], AF.Exp)
            s.ts("dve", nA[:, :], nA[:, :], -1.0, None, ALU.mult)
            s.act(beta[:, :, :], raw[:, :, 0:64], AF.Sigmoid)
            s.ts("dve", nbeta[:, :, :], beta[:, :, :], -1.0, None, ALU.mult)
            for t in range(NT):
                s.tt("dve", t1[:, t, :], raw[:, t, 64:128], dtb[:, :], ALU.add)
            s.act(t2[:, :, :], t1[:, :, :], AF.Abs)
            s.act(t2[:, :, :], t2[:, :, :], AF.Exp, scale=-1.0)
            s.act(t2[:, :, :], t2[:, :, :], AF.Ln, bias=1.0)
            s.ts("dve", t1[:, :, :], t1[:, :, :], 0.0, None, ALU.max)
            s.tt("dve", t1[:, :, :], t1[:, :, :], t2[:, :, :], ALU.add)
            for t in range(NT):
                s.tt("dve", g[:, t, :], t1[:, t, :], nA[:, :], ALU.mult)
            MU_F, MU_B, M_SL, M_SU, M_UI, M_LI = range(6)
            for t in range(NT):
                p = s.psf.next()
                s.mm(p[:, 0:32], masks[:, MU_F, :], g[:, t, 0:32], True, True)
                s.mm(p[:, 32:64], masks[:, MU_B, :], g[:, t, 32:64], True, True)
                s.mm(p[:, 64:128], onesf[:, :], g[:, t, :], True, True)
                s.copy("act", gc[:, t, :], p[:, 0:64])
                s.copy("dve", tot[:, t, :], p[:, 64:128])
            s.act(eg[:, :, :], gc[:, :, :], AF.Exp)
            s.tt("dve", t1[:, :, :], tot[:, :, :], gc[:, :, :], ALU.subtract)
            s.act(egl[:, :, :], t1[:, :, :], AF.Exp)
            s.act(etot[:, :, :], tot[:, :, :], AF.Exp)
            s.tt("dve", beg[:, :, :], beta[:, :, :], eg[:, :, :], ALU.mult)
            s.barrier()
            st_g.close()
            kT = s.sb([128, 1, NT * 128], BF16, "kT", st)
            qT = s.sb([128, 1, NT * 128], BF16, "qT", st)
            Ktm = s.sb([128, NT, 128], BF16, "Ktm", st)
            vT = [s.sb([128, 1, NT * 128], BF16, "vT", st) for _ in range(2)]
            Vtm = [s.sb([128, NT, 128], BF16, "Vtm", st) for _ in range(2)]
            ztm = [s.sb([128, NT, 128], BF16, "ztm", st) for _ in range(2)]
            acc = [f32t("acc", [128, NT, 128]) for _ in range(2)]
            oh = [s.sb([128, NT, 128], BF16, "oh", st) for _ in range(2)]
            S32 = [[f32t("S32", [128, 128]) for _ in range(2)] for _ in range(2)]
            Sb = [[Ring([s.sb([128, 128], BF16, "Sb", st) for _ in range(2)]) for _ in range(2)] for _ in range(2)]
            shr = Ring([f32t("shr", [128, 5, 128]) for _ in range(4)])
            frs = [[Ring([f32t("fr", [128, 128]) for _ in range(16)]) for _ in range(2)] for _ in range(2)]
            brs = [[Ring([s.sb([128, 128], BF16, "br", st) for _ in range(26)]) for _ in range(2)] for _ in range(2)]
            junk = s.sb([128, 128], BF16, "junk", st)

            def unit(hk, e, d, t, sh, sbc, accw):
                fr = frs[e][d]
                br = brs[e][d]
                tk = slice(t * 128, (t + 1) * 128)
                hv = hk * 2 + e
                col = d * 32 + hv
                gcc = gc[:, t, col:col + 1]
                M = fr.next()
                s.act(M[:, :], onesf[:, :], AF.Copy, scale=gcc)
                pR = s.psf.next()
                s.tr(pR[:, 0:128], M[:, :], s.identf[:, :])
                A1 = fr.next()
                s.ts("dve", A1[:, :], pR[:, 0:128], gcc, 0.0, ALU.subtract, ALU.max)
                B1 = fr.next()
                s.ts("dve", B1[:, :], pR[:, 0:128], gcc, 0.0, ALU.subtract, ALU.min)
                yield
                s.act(A1[:, :], A1[:, :], AF.Exp, scale=-1.0)
                s.act(B1[:, :], B1[:, :], AF.Exp)
                yield
                P = fr.next()
                s.stt(P[:, :], A1[:, :], nbeta[:, t, col:col + 1], sh[:, 0, :], ALU.mult, ALU.mult)
                Pp = [P]
                for j in range(1, 4):
                    pj = br.next()
                    s.stt(pj[:, :], A1[:, :], nbeta[:, t, col:col + 1], sh[:, j, :], ALU.mult, ALU.mult)
                    Pp.append(pj)
                AT = br.next()
                s.tt("pool", AT[:, :], B1[:, :], sh[:, 4, :], ALU.mult)
                yield
                pt = s.psf.next()
                s.tr(pt[:, 0:128], P[:, :], s.identf[:, :])
                PT = fr.next()
                s.copy("act", PT[:, :], pt[:, 0:128])
                Xt = fr.next()
                s.tt("dve", Xt[:, :], pt[:, 0:128], s.identf[:, :], ALU.add)
                X = fr.next()
                s.tt("pool", X[:, :], P[:, :], s.identf[:, :], ALU.add)
                pp = s.psb.next()
                for j in range(1, 4):
                    s.tr(pp[:, (j - 1) * 128:j * 128], Pp[j][:, :], s.identb[:, :])
                PTb = br.next()
                PTb2 = br.next()
                PTb3 = br.next()
                PTp = [PT, PTb, PTb2, PTb3]
                for j in range(1, 4):
                    s.copy("act", PTp[j][:, :], pp[:, (j - 1) * 128:j * 128])
                yield
                for lv in range(1, 4):
                    p1 = s.psf.next()
                    s.mm(p1[:, 0:128], PT[:, :], P[:, :], True, True)
                    s.mm(p1[:, 128:256], P[:, :], PT[:, :], True, True)
                    Pn = fr.next()
                    PTn = fr.next()
                    s.copy("act", Pn[:, :], p1[:, 0:128])
                    s.copy("act", PTn[:, :], p1[:, 128:256])
                    yield
                    p3 = s.psf.next()
                    s.mm(p3[:, 0:128], PTn[:, :], X[:, :], True, True)
                    s.mm(p3[:, 128:256], Pn[:, :], Xt[:, :], True, True)
                    last = lv == 3
                    if last:
                        Xb = br.next()
                        Xtb = br.next()
                        s.tt("dve", Xb[:, :], X[:, :], p3[:, 0:128], ALU.add)
                        s.tt("dve", Xtb[:, :], Xt[:, :], p3[:, 128:256], ALU.add)
                    else:
                        s.tt("dve", X[:, :], X[:, :], p3[:, 0:128], ALU.add)
                        s.tt("dve", Xt[:, :], Xt[:, :], p3[:, 128:256], ALU.add)
                    P, PT = Pn, PTn
                    yield
                for j in range(1, 4):
                    pa = s.psf.next()
                    A1s = br.next()
                    B1s = br.next()
                    if j < 3:
                        s.mm(pa[:, 0:128], PTp[j][:, :], Xb[:, :], True, True)
                    s.mm(pa[:, 128:256], Pp[j][:, :], Xtb[:, :], True, True)
                    if j < 3:
                        s.copy("act", A1s[:, :], pa[:, 0:128])
                    s.copy("act", B1s[:, :], pa[:, 128:256])
                    yield
                    pc = s.psf.next()
                    if j < 3:
                        s.mm(pc[:, 0:128], Xtb[:, :], A1s[:, :], True, True)
                    s.mm(pc[:, 128:256], Xb[:, :], B1s[:, :], True, True)
                    Xtn = br.next()
                    if j < 3:
                        Xn = br.next()
                        s.tt("dve", Xn[:, :], Xb[:, :], pc[:, 0:128], ALU.add)
                    s.tt("dve", Xtn[:, :], Xtb[:, :], pc[:, 128:256], ALU.add)
                    if j < 3:
                        Xb = Xn
                    Xtb = Xtn
                    yield
                TTb = Xtb
                vb = br.next()
                s.act(vb[:, :], Vtm[e][:, t, :], AF.Copy, scale=beta[:, t, col:col + 1])
                kbg = br.next()
                s.act(kbg[:, :], Ktm[:, t, :], AF.Copy, scale=beg[:, t, col:col + 1])
                kdl = br.next()
                s.ts("pool", kdl[:, :], Ktm[:, t, :], egl[:, t, col:col + 1], None, ALU.mult)
                yield
                pw = s.psf.next()
                s.mm(pw[:, 0:128], kbg[:, :], TTb[:, :], True, True)
                nw = br.next()
                s.act(nw[:, :], pw[:, 0:128], AF.Copy, scale=-1.0)
                yield
                S_ = S32[e][d]
                sb_old = sbc[(e, d)]
                pvn = s.psf.next()
                s.mm(pvn[:, 0:128], TTb[:, :], vb[:, :], True, False)
                s.mm(pvn[:, 0:128], nw[:, :], sb_old[:, :], False, True)
                if t >= NCTX:
                    s.mm(pvn[:, 128:256], qT[:, 0, tk], sb_old[:, :], True, True)
                vn = br.next()
                s.copy("act", vn[:, :], pvn[:, 0:128])
                yield
                po2 = s.psf.next()
                s.mm(po2[:, 128:256], kdl[:, :], vn[:, :], True, True)
                if t >= NCTX:
                    s.mm(po2[:, 0:128], AT[:, :], vn[:, :], True, True)
                s.stt(S_[:, :], S_[:, :], etot[:, t, col:col + 1], po2[:, 128:256], ALU.mult, ALU.add)
                sbn = Sb[e][d].next()
                s.copy("act", sbn[:, :], S_[:, :])
                sbc[(e, d)] = sbn
                if t >= NCTX:
                    o1 = fr.next()
                    s.ts("dve", o1[:, :], pvn[:, 128:256], eg[:, t, col:col + 1], None, ALU.mult)
                    if (e, t) not in accw:
                        accw.add((e, t))
                        s.tt("dve", acc[e][:, t, :], o1[:, :], po2[:, 0:128], ALU.add)
                    else:
                        s.tt("dve", o1[:, :], o1[:, :], po2[:, 0:128], ALU.add)
                        s.tt("pool", acc[e][:, t, :], acc[e][:, t, :], o1[:, :], ALU.add)

            for hk in range(int(os.environ.get("GDNHK", "16"))):
                s.dma("sp", kT[:, 0, :], mixT[16 + hk, :, :].k(16 + hk))
                s.dma("sp", qT[:, 0, :], mixT[hk, :, :].k(hk))
                for t in range(NT):
                    pp = s.psb.next()
                    s.tr(pp[:, 0:128], kT[:, 0, t * 128:(t + 1) * 128], s.identb[:, :])
                    s.copy("act", Ktm[:, t, :], pp[:, 0:128])
                for e in range(2):
                    hv = hk * 2 + e
                    s.dma("sp", vT[e][:, 0, :], mixT[32 + hv, :, :].k(32 + hv))
                    s.dma("sp", ztm[e][:, :, :], View(s.proj, pv[:, :, hv * 128:(hv + 1) * 128],
                                                      [(t, (hv // 4) * 512) for t in range(NT)]))
                    for t in range(NT):
                        pp = s.psb.next()
                        s.tr(pp[:, 0:128], vT[e][:, 0, t * 128:(t + 1) * 128], s.identb[:, :])
                        s.copy("dve", Vtm[e][:, t, :], pp[:, 0:128])
                orders = {0: list(range(NT)), 1: [1, 0] + list(range(NT - 1, NCTX - 1, -1))}
                sbc = {}
                accw = set()
                for e in range(2):
                    for d in range(2):
                        s.memset("dve", S32[e][d][:, :], 0.0)
                        sbc[(e, d)] = Sb[e][d].next()
                        s.memset("pool", sbc[(e, d)][:, :], 0.0)
                for step in range(NT):
                    gens = []
                    for d in range(2):
                        t = orders[d][step]
                        tk = slice(t * 128, (t + 1) * 128)
                        sh = shr.next()
                        pG = s.psf.next()
                        s.mm(pG[:, 0:128], kT[:, 0, tk], kT[:, 0, tk], True, True)
                        s.mm(pG[:, 128:256], kT[:, 0, tk], qT[:, 0, tk], True, True)
                        for j in range(4):
                            s.tt("dve", sh[:, j, :], pG[:, 0:128], masks[:, (6 if d == 0 else 10) + j, :], ALU.mult)
                        s.tt("dve", sh[:, 4, :], pG[:, 128:256], masks[:, M_UI if d == 0 else M_LI, :], ALU.mult)
                        for e in range(2):
                            gens.append(unit(hk, e, d, t, sh, sbc, accw))
                    while gens:
                        for g_ in list(gens):
                            try:
                                next(g_)
                            except StopIteration:
                                gens.remove(g_)
                for e in range(2):
                    hv = hk * 2 + e
                    for t in range(NCTX, NT):
                        sm = s.small.next()
                        s.act(junk[:, :], acc[e][:, t, :], AF.Square, accum=sm[:, 0:1])
                        s.ts("dve", sm[:, 1:2], sm[:, 0:1], 1.0 / 128, EPS, ALU.mult, ALU.add)
                        s.act(sm[:, 2:3], sm[:, 1:2], AF.Sqrt)
                        s.recip(sm[:, 3:4], sm[:, 2:3])
                        s.stt(acc[e][:, t, :], acc[e][:, t, :], sm[:, 3:4], ngb[:, :], ALU.mult, ALU.mult)
                        s.tt("pool", oh[e][:, t, :], acc[e][:, t, :], ztm[e][:, t, :], ALU.mult)
                    s.dma("pool", View(s.omix, ov[:, NCTX:NT, hv * 128:(hv + 1) * 128], list(range(NCTX, NT))),
                          oh[e][:, NCTX:NT, :])
        s.barrier()
        s.outproj_phase(li, "gdn_w_out", 4096, False)

    def build(s):
        s.cast_weights(s.layers[0])
        s.ada_phase()
        for i, li in enumerate(s.layers):
            kind = li % 4
            want_ctx = li < 2
            if i + 1 < len(s.layers):
                s.cast_weights(s.layers[i + 1])
            if kind == 3:
                s.sgu_mixer(li)
            elif kind == 1:
                s.att_mixer(li)
            elif kind == 0:
                s.ret_mixer(li)
            elif kind == 2:
                s.gdn_mixer(li)
            s.mlp_phase(li, want_ctx)
        s.final_phase()


def make_consts():
    c = {}
    c["ident"] = np.eye(128, dtype=np.float32)
    i = np.arange(128)[:, None]
    j = np.arange(128)[None, :]
    NEG = -30000.0
    mprev = np.where(j >= i, 0.0, NEG)
    mnext = np.where(j <= i, 0.0, NEG)
    c["mask3"] = np.concatenate([mprev, np.zeros((128, 128)), mnext], axis=1).astype(np.float32)

    def rope(hd, rep):
        rows = 2048 // 64
        row = np.repeat(np.arange(rows, dtype=np.float32), 64)
        col = np.tile(np.arange(64, dtype=np.float32), rows)
        ad = hd // 2
        inv = np.exp(np.float32(-math.log(10000.0)) * np.arange(0, ad, 2, dtype=np.float32) / np.float32(ad)).astype(np.float32)
        ang = np.concatenate([row[:, None] * inv, col[:, None] * inv], axis=-1).astype(np.float32)
        cs = np.cos(ang).astype(np.float32)
        sn = np.sin(ang).astype(np.float32)
        cs = np.tile(cs, (1, rep)).reshape(16, 128, -1)
        sn = np.tile(sn, (1, rep)).reshape(16, 128, -1)
        return np.ascontiguousarray(cs), np.ascontiguousarray(sn)
    c["cos_a"], c["sin_a"] = rope(128, 4)
    c["cos_r"], c["sin_r"] = rope(256, 2)
    tt_ = np.arange(128)[:, None]
    cc_ = np.arange(128)[None, :]
    lo = tt_ > cc_
    F = [lo & (tt_ // 16 == cc_ // 16),
         lo & (tt_ // 32 == cc_ // 32) & (tt_ // 16 != cc_ // 16),
         lo & (tt_ // 64 == cc_ // 64) & (tt_ // 32 != cc_ // 32),
         lo & (tt_ // 64 != cc_ // 64)]
    Bm = [f.T for f in F]
    c["gdn_masks"] = np.stack([tt_ <= cc_, tt_ >= cc_, tt_ > cc_, tt_ < cc_, cc_ >= tt_, cc_ <= tt_] + F + Bm).astype(np.float32)
    lg = np.log1p(-np.exp2(-5.0 - np.arange(8, dtype=np.float32))).astype(np.float32)
    pos = np.arange(128, dtype=np.float32)
    diff = np.abs(pos[:, None] - pos[None, :])
    Dm = np.exp(lg[:, None, None] * diff[None]).astype(np.float32)
    Dm[:, np.arange(128), np.arange(128)] = 2.0
    c["ret_D"] = np.ascontiguousarray(Dm)
    qd = np.stack([np.exp(lg[:, None] * (pos + 1.0)[None]), np.exp(lg[:, None] * (128.0 - pos)[None])], axis=1)
    c["ret_qdec"] = np.ascontiguousarray(qd.astype(np.float32))
    kd = np.stack([np.exp(lg[:, None] * (127.0 - pos)[None]), np.exp(lg[:, None] * pos[None])], axis=1)
    c["ret_kdec"] = np.ascontiguousarray(kd.astype(np.float32).transpose(2, 0, 1).reshape(128, 16))
    return c


_CACHE = {}


def kernel(**inputs):
    layers = inputs.pop("_layers", (0, 1, 2, 3))
    ncores = inputs.pop("_ncores", 8)
    consts = make_consts()
    key = (tuple(layers), ncores)
    if key not in _CACHE:
        _CACHE[key] = Model(layers=layers, consts=consts)
    m = _CACHE[key]
    x = np.asarray(inputs["x"], dtype=np.float32)
    ctx = np.asarray(inputs["ctx"], dtype=np.float32)
    c = np.asarray(inputs["c"], dtype=np.float32)
    cc = np.asarray(inputs["c_ctx"], dtype=np.float32)
    shared = {k: np.ascontiguousarray(np.asarray(inputs[k], dtype=np.float32)) for k in IN_SHAPES}
    shared.update(consts)
    in_maps = []
    for b in range(ncores):
        d = dict(shared)
        d["xin"] = np.ascontiguousarray(np.concatenate([ctx[b], x[b]], axis=0))
        d["c2"] = np.ascontiguousarray(np.stack([c[b], cc], axis=0))
        in_maps.append(d)
    if os.environ.get("KTRACE") == "1":
        res = run_bass_kernel_spmd(m.nc, in_maps, core_ids=list(range(ncores)), trace=True)
        print("EXEC_NS", res.exec_time_ns)
    else:
        res = run_bass_kernel_spmd(m.nc, in_maps, core_ids=list(range(ncores)))
    out = np.stack([np.asarray(r["y"], dtype=np.float32) for r in res.results], axis=0)
    return out
```

```python
import contextlib
import math
import os
import numpy as np
import concourse.bass as bass
import concourse.mybir as mybir
from concourse.bass_utils import run_bass_kernel_spmd

F32 = mybir.dt.float32
BF16 = mybir.dt.bfloat16
F32R = mybir.dt.float32r
AF = mybir.ActivationFunctionType
ALU = mybir.AluOpType
AX = mybir.AxisListType

D = 2048
KC = 16
NT = 18
NCTX = 2
EPS = 1e-6
HID = 8192


class View:
    __slots__ = ("tl", "ap", "key")

    def __init__(s, tl, ap, key="*"):
        s.tl = tl
        s.ap = ap
        s.key = key

    def k(s, key):
        return View(s.tl, s.ap, key)


class Tl:
    def __init__(s, base, name):
        s.base = base
        s.name = name
        s.st = {}
        s.psum = False

    def __getitem__(s, idx):
        return View(s, s.base[idx])

    def v(s, ap, key="*"):
        return View(s, ap, key)


class _Sub:
    def __init__(s, tl, t):
        s.tl = tl
        s.t = t

    def __getitem__(s, idx):
        return View(s.tl, s.tl.base[:, s.t, :][idx])


class Ring:
    def __init__(s, tls):
        s.tls = tls
        s.i = 0

    def next(s):
        t = s.tls[s.i % len(s.tls)]
        s.i += 1
        return t


class Bld:
    def __init__(s):
        s.nc = bass.Bass("TRN2", target_bir_lowering=False)
        nc = s.nc
        s.es = contextlib.ExitStack()
        s.E = {"pe": nc.tensor, "act": nc.scalar, "dve": nc.vector, "pool": nc.gpsimd, "sp": nc.sync}
        s.sem = {}
        s.cnt = {}
        for e in ["pe", "act", "dve", "pool"]:
            s.sem[e] = s.es.enter_context(nc.semaphore("s_" + e))
            s.cnt[e] = 0
        s.dq = {}
        for q, n in [("sp", 10), ("pool", 16), ("act", 2)]:
            names = []
            for i in range(n):
                nm = "d_%s%d" % (q, i)
                s.sem[nm] = s.es.enter_context(nc.semaphore(nm))
                s.cnt[nm] = 0
                names.append(nm)
            s.dq[q] = names
        s.dqi = {"sp": 0, "pool": 0, "act": 0}
        s.known = {e: {} for e in s.E}
        s.ninst = 0
        s.uid = 0

    def sb(s, shape, dt, name=None, stack=None):
        s.uid += 1
        nm = "%s_%d" % (name or "t", s.uid)
        t = (stack or s.es).enter_context(s.nc.sbuf_tensor(nm, list(shape), dt))
        return Tl(t, nm)

    def ps(s, shape, dt, name=None):
        s.uid += 1
        nm = "%s_%d" % (name or "p", s.uid)
        t = s.es.enter_context(s.nc.psum_tensor(nm, list(shape), dt))
        tl = Tl(t, nm)
        tl.psum = True
        return tl

    def dram(s, name, shape, dt, kind="Internal"):
        t = s.nc.dram_tensor(name, list(shape), dt, kind=kind)
        return Tl(t.ap(), name)

    def _states(s, v):
        st = v.tl.st
        if "*" not in st:
            st["*"] = [{}, {}]
        if v.key == "*":
            return list(st.values())
        keys = v.key if isinstance(v.key, list) else [v.key]
        out = [st["*"]]
        for k in keys:
            if k not in st:
                st[k] = [{}, {}]
            out.append(st[k])
        return out

    def _need(s, eng, reads, writes, is_dma):
        need = {}

        def mg(d, skip):
            for src, val in d.items():
                if skip and src == eng:
                    continue
                if need.get(src, 0) < val:
                    need[src] = val

        for v in reads:
            for st in s._states(v):
                mg(st[0], False)
                if v.tl.psum:
                    mg(st[1], True)
        for v in writes:
            for st in s._states(v):
                mg(st[0], not is_dma)
                mg(st[1], not is_dma)
        if eng == "pe":
            need.pop("pe", None)
        return need

    def _wait(s, eng, need):
        kn = s.known[eng]
        for src, val in need.items():
            if kn.get(src, 0) < val:
                s.E[eng].wait_ge(s.sem[src], val)
                kn[src] = val
                s.ninst += 1

    def _upd(s, src, val, reads, writes):
        for v in reads:
            st = v.tl.st
            if v.key == "*":
                st["*"][1][src] = val
            else:
                for k in v.key if isinstance(v.key, list) else [v.key]:
                    st[k][1][src] = val
        for v in writes:
            st = v.tl.st
            if v.key == "*":
                st.clear()
                st["*"] = [{src: val}, {}]
            else:
                for k in v.key if isinstance(v.key, list) else [v.key]:
                    st[k] = [{src: val}, {}]

    def op(s, eng, fn, reads, writes):
        need = s._need(eng, reads, writes, False)
        s._wait(eng, need)
        inst = fn()
        s.cnt[eng] += 1
        inst.then_inc(s.sem[eng], 1)
        s.ninst += 1
        s._upd(eng, s.cnt[eng], reads, writes)
        return inst

    def dma(s, q, out, in_, **kw):
        need = s._need(q, [in_], [out], True)
        ring = s.dq[q]
        nm = ring[s.dqi[q] % len(ring)]
        s.dqi[q] += 1
        if s.cnt[nm] > 0:
            need[nm] = max(need.get(nm, 0), s.cnt[nm])
        s._wait(q, need)
        inst = s.E[q].dma_start(out=out.ap, in_=in_.ap, **kw)
        s.cnt[nm] += 16
        inst.then_inc(s.sem[nm], 16)
        s.ninst += 1
        s._upd(nm, s.cnt[nm], [in_], [out])

    def barrier(s, engines=None):
        for e in engines or list(s.E):
            need = {src: val for src, val in s.cnt.items() if val > 0}
            if e == "pe":
                need.pop("pe", None)
            s._wait(e, need)

    def mm(s, out, lhsT, rhs, start, stop):
        return s.op("pe", lambda: s.nc.tensor.matmul(out.ap, lhsT.ap, rhs.ap, start=start, stop=stop),
                    [lhsT, rhs], [out])

    def tr(s, out, in_, ident):
        return s.op("pe", lambda: s.nc.tensor.transpose(out.ap, in_.ap, ident.ap), [in_, ident], [out])

    def act(s, out, in_, func, bias=None, scale=None, accum=None, eng="act"):
        kw = {}
        reads = [in_]
        writes = [out]
        if bias is not None:
            if isinstance(bias, View):
                kw["bias"] = bias.ap
                reads.append(bias)
            else:
                kw["bias"] = bias
        if scale is not None:
            if isinstance(scale, View):
                kw["scale"] = scale.ap
                reads.append(scale)
            else:
                kw["scale"] = scale
        if accum is not None:
            kw["accum_out"] = accum.ap
            writes.append(accum)
        return s.op("act", lambda: s.nc.scalar.activation(out.ap, in_.ap, func, **kw), reads, writes)

    def tt(s, eng, out, in0, in1, op):
        return s.op(eng, lambda: s.E[eng].tensor_tensor(out.ap, in0.ap, in1.ap, op), [in0, in1], [out])

    def ts(s, eng, out, in0, s1, s2, op0, op1=None):
        reads = [in0]
        a1 = s1
        a2 = s2
        if isinstance(s1, View):
            reads.append(s1)
            a1 = s1.ap
        if isinstance(s2, View):
            reads.append(s2)
            a2 = s2.ap
        if op1 is None:
            return s.op(eng, lambda: s.E[eng].tensor_scalar(out.ap, in0.ap, a1, None, op0), reads, [out])
        return s.op(eng, lambda: s.E[eng].tensor_scalar(out.ap, in0.ap, a1, a2, op0, op1), reads, [out])

    def stt(s, out, in0, scalar, in1, op0, op1, eng="dve"):
        reads = [in0, in1]
        a = scalar
        if isinstance(scalar, View):
            reads.append(scalar)
            a = scalar.ap
        return s.op(eng, lambda: s.E[eng].scalar_tensor_tensor(out.ap, in0.ap, a, in1.ap, op0, op1), reads, [out])

    def copy(s, eng, out, in_):
        if eng == "act":
            return s.op("act", lambda: s.nc.scalar.copy(out.ap, in_.ap), [in_], [out])
        return s.op(eng, lambda: s.E[eng].tensor_copy(out.ap, in_.ap), [in_], [out])

    def memset(s, eng, out, val):
        return s.op(eng, lambda: s.E[eng].memset(out.ap, val), [], [out])

    def recip(s, out, in_):
        return s.op("dve", lambda: s.nc.vector.reciprocal(out.ap, in_.ap), [in_], [out])

    def rmax(s, out, in_):
        return s.op("dve", lambda: s.nc.vector.reduce_max(out.ap, in_.ap, axis=AX.X), [in_], [out])


GROUPS = [[0, 1], [2, 3, 4, 5, 6, 7], [8, 9, 10, 11, 12, 13], [14, 15, 16, 17]]
WSPEC = {
    0: [("ret_w_in", 2048, 12288), ("ret_w_out", 4096, 2048)],
    1: [("att_w_in", 2048, 3072), ("att_w_out", 2048, 2048)],
    2: [("gdn_w_in", 2048, 12416), ("gdn_w_out", 4096, 2048)],
    3: [("sgu_w_in", 2048, 8192), ("sgu_w_out", 4096, 2048)],
}
IN_SHAPES = {
    "mod_w": [4, 2048, 12288], "mod_b": [4, 12288], "norm1_g": [4, 2048], "norm2_g": [4, 2048],
    "mlp_up": [4, 2048, 8192], "mlp_down": [4, 8192, 2048], "final_g": [2048],
    "ret_w_in": [1, 2048, 12288], "ret_gn_g": [1, 4096], "ret_w_out": [1, 4096, 2048],
    "att_w_in": [1, 2048, 3072], "att_sink": [1, 16], "att_w_out": [1, 2048, 2048],
    "gdn_w_in": [1, 2048, 12416], "gdn_conv_w": [1, 4, 8192], "gdn_a_log": [1, 2, 32],
    "gdn_dt_bias": [1, 2, 32], "gdn_norm_g": [1, 128], "gdn_w_out": [1, 4096, 2048],
    "sgu_w_in": [1, 2048, 8192], "sgu_ln_g": [1, 4096], "sgu_ln_b": [1, 4096],
    "sgu_w_s": [1, 8, 128, 128], "sgu_b_s": [1, 8, 128], "sgu_w_out": [1, 4096, 2048],
}


class Model(Bld):
    def __init__(s, layers=(0, 1, 2, 3), final=True, consts=None):
        super().__init__()
        s.layers = list(layers)
        s.I = {}
        s.I["xin"] = s.dram("xin", [NT * 128, D], F32, "ExternalInput")
        s.I["c2"] = s.dram("c2", [2, D], F32, "ExternalInput")
        for k, shp in IN_SHAPES.items():
            s.I[k] = s.dram(k, shp, F32, "ExternalInput")
        for k, arr in (consts or {}).items():
            s.I[k] = s.dram(k, list(arr.shape), F32, "ExternalInput")
        s.out = s.dram("y", [16 * 128, D], F32, "ExternalOutput")
        s.xres = s.dram("xres", [NT * 128, D], F32)
        s.adav = {li: s.dram("adav%d" % li, [2, 6 * D], F32) for li in s.layers}
        s.proj = s.dram("proj", [NT * 128, 12416], BF16)
        s.projf = s.dram("projf", [NT * 128, 128], F32)
        s.omix = s.dram("omix", [NT * 128, 4096], BF16)
        s.mixT = s.dram("mixT", [64, 128, NT * 128], BF16)
        s.wb = {}
        for li in s.layers:
            for nm, k, n in WSPEC[li]:
                s.wb[nm] = s.dram(nm + "_bf", [k, n], BF16)
            s.wb["up%d" % li] = s.dram("up%d_bf" % li, [D, HID], BF16)
            s.wb["down%d" % li] = s.dram("down%d_bf" % li, [HID, D], BF16)
        s.psf = Ring([s.ps([128, 512], F32, "psf") for _ in range(6)])
        s.psb = Ring([s.ps([128, 1024], BF16, "psb") for _ in range(2)])
        s.identf = s.sb([128, 128], F32, "identf")
        s.identb = s.sb([128, 128], BF16, "identb")
        s.dma("sp", s.identf[:, :], s.I["ident"][:, :])
        s.copy("dve", s.identb[:, :], s.identf[:, :])
        s.small = Ring([s.sb([128, 8], F32, "sm") for _ in range(12)])
        s.evi = 0
        s.xsrc = s.I["xin"]
        s.build()

    def ev_eng(s):
        s.evi += 1
        return "act" if s.evi % 2 else "dve"

    def load_bc(s, dst, src_tl, ap1d, q="sp", np_=128):
        s.dma(q, dst, View(src_tl, ap1d.partition_broadcast(np_)))

    def cast_weights(s, li):
        if li not in s.layers:
            return
        lst = [(nm, s.I[nm], s.I[nm].base[0], k, n) for nm, k, n in WSPEC[li]]
        lst.insert(1, ("up%d" % li, s.I["mlp_up"], s.I["mlp_up"].base[li], D, HID))
        lst.append(("down%d" % li, s.I["mlp_down"], s.I["mlp_down"].base[li], HID, D))
        for nm, tl, src, k, n in lst:
            for r in range(0, k, 256):
                s.dma("pool", s.wb[nm][r:r + 256, :].k(r), View(tl, src[r:r + 256, :]))

    def ada_phase(s):
        with contextlib.ExitStack() as st:
            c2t = s.sb([2, D], F32, "c2t", st)
            sc = s.sb([2, D], F32, "sc", st)
            scT = s.sb([128, KC, 2], F32, "scT", st)
            wr = Ring([s.sb([128, KC, 512], F32, "adw", st) for _ in range(2)])
            mbr = Ring([s.sb([2, 512], F32, "mb", st) for _ in range(2)])
            orr = Ring([s.sb([2, 512], F32, "ao", st) for _ in range(2)])
            s.dma("sp", c2t[:, :], s.I["c2"][:, :])
            s.act(sc[:, :], c2t[:, :], AF.Silu)
            p = s.psf.next()
            for kc in range(KC):
                s.tr(p[:, kc * 2:(kc + 1) * 2], sc[0:2, kc * 128:(kc + 1) * 128], s.identf[0:2, 0:2])
            s.copy("dve", scT[:, :, :], View(p, p.base[:, 0:32].rearrange("p (k r) -> p k r", r=2)))
            for li in s.layers:
                wv = s.I["mod_w"].base[li].rearrange("(kc p) n -> p kc n", p=128)
                for c0 in range(0, 6 * D, 512):
                    wt = wr.next()
                    s.dma("sp", wt[:, :, :], View(s.I["mod_w"], wv[:, :, c0:c0 + 512]))
                    mb = mbr.next()
                    s.load_bc(mb[:, :], s.I["mod_b"], s.I["mod_b"].base[li, c0:c0 + 512], np_=2)
                    p = s.psf.next()
                    for kc in range(KC):
                        s.mm(p[0:2, :], scT[:, kc, :], wt[:, kc, :], kc == 0, kc == KC - 1)
                    o = orr.next()
                    s.tt("dve", o[:, :], p[0:2, :], mb[:, :], ALU.add)
                    s.dma("pool", s.adav[li][:, c0:c0 + 512], o[:, :])
        s.barrier()

    def mod_vecs(s, li, r, which, gm, sh, gt, tmp):
        a = s.adav[li]
        o = 0 if which == 1 else 3
        ng = s.I["norm1_g" if which == 1 else "norm2_g"]
        s.load_bc(sh[:, :], a, a.base[r, (o + 0) * D:(o + 1) * D])
        s.load_bc(tmp[:, :], a, a.base[r, (o + 1) * D:(o + 2) * D])
        s.load_bc(gm[:, :], ng, ng.base[li, :])
        s.stt(gm[:, :], tmp[:, :], 1.0, gm[:, :], ALU.add, ALU.mult)
        if gt is not None:
            s.load_bc(gt[:, :], a, a.base[r, (o + 2) * D:(o + 3) * D])

    def norm_tile(s, xt, gm, sh, h32, hb, junk):
        ss = s.small.next()
        s.act(junk[:, :], xt, AF.Square, accum=ss[:, 0:1])
        s.ts("dve", ss[:, 1:2], ss[:, 0:1], 1.0 / D, EPS, ALU.mult, ALU.add)
        s.act(ss[:, 2:3], ss[:, 1:2], AF.Sqrt)
        s.recip(ss[:, 3:4], ss[:, 2:3])
        s.stt(h32[:, :], xt, ss[:, 3:4], gm, ALU.mult, ALU.mult)
        s.tt("pool", hb[:, :], h32[:, :], sh, ALU.add)

    def transpose_tile(s, src, ncol, dst, kc0, tok0):
        nb = ncol // 128
        for b0 in range(0, nb, 8):
            n = min(8, nb - b0)
            p = s.psb.next()
            for j in range(n):
                s.tr(p[:, j * 128:(j + 1) * 128], src[:, (b0 + j) * 128:(b0 + j + 1) * 128], s.identb[:, :])
            s.copy(s.ev_eng(), dst[:, kc0 + b0:kc0 + b0 + n, tok0:tok0 + 128],
                   View(p, p.base[:, 0:n * 128].rearrange("p (j c) -> p j c", c=128)))

    def wview(s, nm):
        return s.wb[nm].base.rearrange("(kc p) n -> p kc n", p=128)

    def wkeys(s, k0, k1):
        return [r for r in range(0, 8192, 256) if r < k1 and r + 256 > k0]

    def gemm_tm(s, lhsT_fn, tiles, wname, K, n0, n1, evac, wring):
        kcn = K // 128
        wv = s.wview(wname)
        for c0 in range(n0, n1, 512):
            cw = min(512, n1 - c0)
            wt = wring.next()
            s.dma("sp", wt[:, 0:kcn, 0:cw], View(s.wb[wname], wv[:, :, c0:c0 + cw], s.wkeys(0, K)))
            for t in tiles:
                p = s.psf.next()
                for kc in range(kcn):
                    s.mm(p[:, 0:cw], lhsT_fn(t, kc), wt[:, kc, 0:cw], kc == 0, kc == kcn - 1)
                evac(t, c0, cw, p)

    def gemm_acc(s, lhsT_fn, tiles, wname, K, evac, wring):
        kcn = K // 128
        wv = s.wview(wname)
        assert len(tiles) <= 6
        for c0 in range(0, D, 512):
            banks = [s.psf.next() for _ in tiles]
            for hb in range(0, kcn, 16):
                wt = wring.next()
                s.dma("sp", wt[:, 0:16, :], View(s.wb[wname], wv[:, hb:hb + 16, c0:c0 + 512],
                                                  s.wkeys(hb * 128, (hb + 16) * 128)))
                for i, t in enumerate(tiles):
                    for kc in range(16):
                        s.mm(banks[i][:, :], lhsT_fn(i, hb + kc), wt[:, kc, :], hb + kc == 0, hb + kc == kcn - 1)
            for i, t in enumerate(tiles):
                evac(t, c0, banks[i])

    def resid_evac_fn(s, gt, xr, tr_):
        def evac(t, c0, p):
            xt = xr.next()
            s.dma("sp", xt[:, :], s.xsrc[t * 128:(t + 1) * 128, c0:c0 + 512].k((t, c0)))
            s.tt("dve", p[:, :], p[:, :], gt[:, c0:c0 + 512], ALU.mult)
            s.tt("dve", xt[:, :], xt[:, :], p[:, :], ALU.add)
            s.dma("pool", s.xres[t * 128:(t + 1) * 128, c0:c0 + 512].k((t, c0)), xt[:, :])
        return evac

    def xkeys(s, t):
        return [(t, c) for c in range(0, D, 512)]

    def outproj_phase(s, li, wname, K, do_ctx):
        kcn = K // 128
        with contextlib.ExitStack() as st:
            oT = s.sb([128, kcn, 6 * 128], BF16, "oT", st)
            gt = s.sb([128, D], F32, "g1", st)
            otr = Ring([s.sb([128, K], BF16, "ot", st) for _ in range(2)])
            wring = Ring([s.sb([128, 16, 512], BF16, "wo", st) for _ in range(2)])
            xr = Ring([s.sb([128, 512], F32, "xr", st) for _ in range(3)])
            tr_ = None
            a = s.adav[li]
            cur_r = None
            for g in GROUPS:
                r = 1 if g[0] < NCTX else 0
                if r == 1 and not do_ctx:
                    continue
                if r != cur_r:
                    s.load_bc(gt[:, :], a, a.base[r, 2 * D:3 * D])
                    cur_r = r
                for i, t in enumerate(g):
                    ot = otr.next()
                    s.dma("sp", ot[:, :], s.omix[t * 128:(t + 1) * 128, 0:K].k(t))
                    s.transpose_tile(ot, K, oT, 0, i * 128)
                s.gemm_acc(lambda i, kc: oT[:, kc, i * 128:(i + 1) * 128], g, wname, K,
                           s.resid_evac_fn(gt, xr, tr_), wring)
        s.xsrc = s.xres if do_ctx or True else s.xsrc
        s.barrier()

    def mlp_phase(s, li, do_ctx):
        with contextlib.ExitStack() as st:
            hT = s.sb([128, KC, 6 * 128], BF16, "hT2", st)
            upT = s.sb([128, 64, 6 * 128], BF16, "upT", st)
            gm = s.sb([128, D], F32, "gm2", st)
            sh = s.sb([128, D], F32, "sh2", st)
            gt = s.sb([128, D], F32, "g2", st)
            xtr = Ring([s.sb([128, D], F32, "xt", st) for _ in range(1)])
            h32 = s.sb([128, D], F32, "h32", st)
            hb = s.sb([128, D], BF16, "hb", st)
            junk = hb
            wring = Ring([s.sb([128, 16, 512], BF16, "wm", st) for _ in range(2)])
            xr = Ring([s.sb([128, 512], F32, "xr", st) for _ in range(3)])
            tr_ = None
            r32 = Ring([s.sb([128, 512], F32, "r32", st) for _ in range(2)])
            cur_r = None
            wv = s.wview("up%d" % li)
            for g in GROUPS:
                r = 1 if g[0] < NCTX else 0
                if r == 1 and not do_ctx:
                    continue
                if r != cur_r:
                    s.mod_vecs(li, r, 2, gm, sh, gt, h32)
                    cur_r = r
                ntok = len(g) * 128
                for i, t in enumerate(g):
                    xt = xtr.next()
                    s.dma("sp", xt[:, :], s.xsrc[t * 128:(t + 1) * 128, :].k(s.xkeys(t)))
                    s.norm_tile(xt[:, :], gm[:, :], sh[:, :], h32, hb, junk)
                    s.transpose_tile(hb, D, hT, 0, i * 128)
                for c0 in range(0, HID, 512):
                    wt = wring.next()
                    s.dma("sp", wt[:, :, :], View(s.wb["up%d" % li], wv[:, :, c0:c0 + 512], s.wkeys(0, D)))
                    for j in range(4):
                        hc = c0 // 128 + j
                        for tb in range(0, ntok, 512):
                            tw = min(512, ntok - tb)
                            p = s.psf.next()
                            for kc in range(KC):
                                s.mm(p[:, 0:tw], wt[:, kc, j * 128:(j + 1) * 128], hT[:, kc, tb:tb + tw],
                                     kc == 0, kc == KC - 1)
                            rr = r32.next()
                            s.act(rr[:, 0:tw], p[:, 0:tw], AF.Relu)
                            s.tt("pool" if (hc + tb // 512) % 2 else "dve", upT[:, hc, tb:tb + tw],
                                 rr[:, 0:tw], rr[:, 0:tw], ALU.mult)
                s.gemm_acc(lambda i, kc: upT[:, kc, i * 128:(i + 1) * 128], g, "down%d" % li, HID,
                           s.resid_evac_fn(gt, xr, tr_), wring)
        s.barrier()

    def n1_phase(s, li, tiles, hT, st):
        with contextlib.ExitStack() as st2:
            gm = s.sb([128, D], F32, "gm1", st2)
            sh = s.sb([128, D], F32, "sh1", st2)
            xtr = Ring([s.sb([128, D], F32, "xt", st2) for _ in range(2)])
            h32 = s.sb([128, D], F32, "h32", st2)
            hb = s.sb([128, D], BF16, "hb", st2)
            junk = hb
            cur_r = None
            for t in tiles:
                r = 1 if t < NCTX else 0
                if r != cur_r:
                    s.mod_vecs(li, r, 1, gm, sh, None, h32)
                    cur_r = r
                xt = xtr.next()
                s.dma("sp", xt[:, :], s.xsrc[t * 128:(t + 1) * 128, :].k(s.xkeys(t)))
                s.norm_tile(xt[:, :], gm[:, :], sh[:, :], h32, hb, junk)
                s.transpose_tile(hb, D, hT, 0, t * 128)
            s.barrier()

    def final_phase(s):
        with contextlib.ExitStack() as st:
            gm = s.sb([128, D], F32, "fg", st)
            xtr = Ring([s.sb([128, D], F32, "xt", st) for _ in range(2)])
            otr = Ring([s.sb([128, D], F32, "ot", st) for _ in range(2)])
            junk = s.sb([128, D], BF16, "junk", st)
            s.load_bc(gm[:, :], s.I["final_g"], s.I["final_g"].base[:])
            off = NCTX if os.environ.get("DBGCTX") != "1" else 0
            for t in range(off, off + 16):
                xt = xtr.next()
                s.dma("sp", xt[:, :], s.xsrc[t * 128:(t + 1) * 128, :].k(s.xkeys(t)))
                ss = s.small.next()
                s.act(junk[:, :], xt[:, :], AF.Square, accum=ss[:, 0:1])
                s.ts("dve", ss[:, 1:2], ss[:, 0:1], 1.0 / D, EPS, ALU.mult, ALU.add)
                s.act(ss[:, 2:3], ss[:, 1:2], AF.Sqrt)
                s.recip(ss[:, 3:4], ss[:, 2:3])
                ot = otr.next()
                s.stt(ot[:, :], xt[:, :], ss[:, 3:4], gm[:, :], ALU.mult, ALU.mult)
                s.dma("pool", s.out[(t - off) * 128:(t - off + 1) * 128, :], ot[:, :])
        s.barrier()

    def sgu_mixer(s, li):
        tiles = list(range(NCTX, NT))
        with contextlib.ExitStack() as st:
            hT = s.sb([128, KC, NT * 128], BF16, "hT", st)
            s.n1_phase(li, tiles, hT, st)
            with contextlib.ExitStack() as st2:
                wring = Ring([s.sb([128, 16, 512], BF16, "wi", st2) for _ in range(2)])
                zr = Ring([s.sb([128, 512], BF16, "z", st2) for _ in range(4)])

                def evac(t, c0, cw, p):
                    z = zr.next()
                    s.act(z[:, 0:cw], p[:, 0:cw], AF.Gelu)
                    s.dma("pool", s.proj[t * 128:(t + 1) * 128, c0:c0 + cw].k((t, c0)), z[:, 0:cw])
                s.gemm_tm(lambda t, kc: hT[:, kc, t * 128:(t + 1) * 128], tiles, "sgu_w_in", D, 0, 8192, evac, wring)
            s.barrier()
        with contextlib.ExitStack() as st:
            lg = s.sb([128, 4096], F32, "lng", st)
            lb = s.sb([128, 4096], F32, "lnb", st)
            s.load_bc(lg[:, :], s.I["sgu_ln_g"], s.I["sgu_ln_g"].base[0, :])
            s.load_bc(lb[:, :], s.I["sgu_ln_b"], s.I["sgu_ln_b"].base[0, :])
            wsT = s.sb([128, 8, 128], BF16, "wsT", st)
            bs = s.sb([128, 8], F32, "bs", st)
            wsf = s.sb([128, 8, 128], F32, "wsf", st)
            wsb = s.sb([128, 8, 128], BF16, "wsb", st)
            bsr = s.sb([8, 128], F32, "bsr", st)
            s.dma("sp", wsf[:, :, :], View(s.I["sgu_w_s"], s.I["sgu_w_s"].base[0].rearrange("g p q -> p g q")))
            s.copy("dve", wsb[:, :, :], wsf[:, :, :])
            for g0 in range(0, 8, 4):
                p = s.psb.next()
                for j in range(4):
                    s.tr(p[:, j * 128:(j + 1) * 128], wsb[:, g0 + j, :], s.identb[:, :])
                s.copy("dve", wsT[:, g0:g0 + 4, :], View(p, p.base[:, 0:512].rearrange("p (j c) -> p j c", c=128)))
            s.dma("sp", bsr[:, :], s.I["sgu_b_s"][0, :, :])
            p = s.psf.next()
            s.tr(p[:, 0:8], bsr[0:8, :], s.identf[0:8, 0:8])
            s.copy("dve", bs[:, :], p[:, 0:8])
            zt = Ring([s.sb([128, 8192], BF16, "zt", st) for _ in range(2)])
            vn32 = s.sb([128, 4096], F32, "vn32", st)
            vnb = s.sb([128, 4096], BF16, "vnb", st)
            junk = s.sb([128, 4096], BF16, "junk", st)
            gor = Ring([s.sb([128, 4096], BF16, "go", st) for _ in range(2)])
            for t in tiles:
                z = zt.next()
                s.dma("sp", z[:, :], s.proj[t * 128:(t + 1) * 128, 0:8192].k([(t, c) for c in range(0, 8192, 512)]))
                ss = s.small.next()
                v = z[:, 4096:8192]
                s.act(junk[:, :], v, AF.Copy, accum=ss[:, 0:1])
                s.act(junk[:, :], v, AF.Square, accum=ss[:, 1:2])
                s.ts("dve", ss[:, 2:3], ss[:, 0:1], 1.0 / 4096, None, ALU.mult)
                s.tt("dve", ss[:, 3:4], ss[:, 2:3], ss[:, 2:3], ALU.mult)
                s.stt(ss[:, 4:5], ss[:, 1:2], 1.0 / 4096, ss[:, 3:4], ALU.mult, ALU.subtract)
                s.ts("dve", ss[:, 5:6], ss[:, 4:5], EPS, None, ALU.add)
                s.act(ss[:, 6:7], ss[:, 5:6], AF.Sqrt)
                s.recip(ss[:, 7:8], ss[:, 6:7])
                s.ts("dve", vn32[:, :], v, ss[:, 2:3], ss[:, 7:8], ALU.subtract, ALU.mult)
                s.tt("pool", vn32[:, :], vn32[:, :], lg[:, :], ALU.mult)
                s.tt("dve", vnb[:, :], vn32[:, :], lb[:, :], ALU.add)
                go = gor.next()
                for g in range(8):
                    p = s.psf.next()
                    s.mm(p[:, :], wsT[:, g, :], vnb[:, g * 512:(g + 1) * 512], True, True)
                    s.stt(go[:, g * 512:(g + 1) * 512], p[:, :], bs[:, g:g + 1], z[:, g * 512:(g + 1) * 512],
                          ALU.add, ALU.mult)
                s.dma("pool", s.omix[t * 128:(t + 1) * 128, :].k(t), go[:, :])
        s.barrier()
        s.outproj_phase(li, "sgu_w_out", 4096, False)


    def rope_evac(s, p, cw, t, scale, rope, cosT, sinT, xs, tmpr, ob):
        if not rope:
            s.act(ob[:, 0:cw], p[:, 0:cw], AF.Copy, scale=scale)
            return
        s.act(xs[:, 0:cw], p[:, 0:cw], AF.Copy, scale=scale)
        h = cw // 2
        x1 = View(xs, xs.base[:, 0:cw].rearrange("p (n two) -> p n two", two=2)[:, :, 0])
        x2 = View(xs, xs.base[:, 0:cw].rearrange("p (n two) -> p n two", two=2)[:, :, 1])
        y1 = View(ob, ob.base[:, 0:cw].rearrange("p (n two) -> p n two", two=2)[:, :, 0])
        y2 = View(ob, ob.base[:, 0:cw].rearrange("p (n two) -> p n two", two=2)[:, :, 1])
        c = cosT[:, t - NCTX, 0:h]
        sn = sinT[:, t - NCTX, 0:h]
        t1, t2, t3, t4 = [tmpr.next() for _ in range(4)]
        s.tt("dve", t1[:, 0:h], x1, c, ALU.mult)
        s.tt("pool", t2[:, 0:h], x2, sn, ALU.mult)
        s.tt("dve", y1, t1[:, 0:h], t2[:, 0:h], ALU.subtract)
        s.tt("pool", t3[:, 0:h], x1, sn, ALU.mult)
        s.tt("dve", t4[:, 0:h], x2, c, ALU.mult)
        s.tt("pool", y2, t3[:, 0:h], t4[:, 0:h], ALU.add)

    def att_mixer(s, li):
        tiles = list(range(NT))
        with contextlib.ExitStack() as st:
            hT = s.sb([128, KC, NT * 128], BF16, "hT", st)
            s.n1_phase(li, tiles, hT, st)
            with contextlib.ExitStack() as st2:
                wring = Ring([s.sb([128, 16, 512], BF16, "wi", st2) for _ in range(2)])
                cosT = s.sb([128, 16, 256], F32, "cos", st2)
                sinT = s.sb([128, 16, 256], F32, "sin", st2)
                s.dma("sp", cosT[:, :, :], View(s.I["cos_a"], s.I["cos_a"].base.rearrange("t p c -> p t c")))
                s.dma("sp", sinT[:, :, :], View(s.I["sin_a"], s.I["sin_a"].base.rearrange("t p c -> p t c")))
                xsr = Ring([s.sb([128, 512], F32, "xs", st2) for _ in range(2)])
                tmpr = Ring([s.sb([128, 256], F32, "rt", st2) for _ in range(8)])
                obr = Ring([s.sb([128, 512], BF16, "ob", st2) for _ in range(4)])

                def evac(t, c0, cw, p):
                    ob = obr.next()
                    isq = c0 < 2048
                    isv = c0 >= 2560
                    s.rope_evac(p, cw, t, (128 ** -0.5) if isq else 1.0, (not isv) and t >= NCTX,
                                cosT, sinT, xsr.next(), tmpr, ob)
                    s.dma("pool", s.proj[t * 128:(t + 1) * 128, c0:c0 + cw].k((t, c0)), ob[:, 0:cw])
                s.gemm_tm(lambda t, kc: hT[:, kc, t * 128:(t + 1) * 128], tiles, "att_w_in", D, 0, 3072, evac, wring)
            s.barrier()
        pv = s.proj.base.rearrange("(t p) c -> p t c", p=128)
        ov = s.omix.base.rearrange("(t p) c -> p t c", p=128)
        allk = lambda c0: [(t, c0) for t in range(NT)]
        with contextlib.ExitStack() as st:
            mask3 = s.sb([128, 384], F32, "mask3", st)
            s.dma("sp", mask3[:, :], s.I["mask3"][:, :])
            sinkb = s.sb([128, 16], F32, "sinkb", st)
            s.load_bc(sinkb[:, :], s.I["att_sink"], s.I["att_sink"].base[0, :])
            ktm = s.sb([128, NT, 128], BF16, "ktm", st)
            kT = s.sb([128, 1, NT * 128], BF16, "kT", st)
            vtm = Ring([s.sb([128, NT, 128], BF16, "vtm", st) for _ in range(2)])
            qtmr = Ring([s.sb([128, NT, 128], BF16, "qtm", st) for _ in range(2)])
            qTr = Ring([s.sb([128, 1, NT * 128], BF16, "qT", st) for _ in range(2)])
            ohr = Ring([s.sb([128, NT, 128], BF16, "oh", st) for _ in range(2)])
            ssb = Ring([s.sb([128, 640], F32, "ssb", st) for _ in range(3)])
            pbr = Ring([s.sb([128, 640], BF16, "pb", st) for _ in range(3)])
            ptr = Ring([s.sb([128, 5, 128], BF16, "pt", st) for _ in range(3)])
            for g in range(4):
                s.dma("sp", ktm[:, :, :], View(s.proj, pv[:, :, 2048 + g * 128:2048 + (g + 1) * 128], allk(2048)))
                vt = vtm.next()
                s.dma("sp", vt[:, :, :], View(s.proj, pv[:, :, 2560 + g * 128:2560 + (g + 1) * 128], allk(2560)))
                for t in range(NT):
                    s.transpose_tile(_Sub(ktm, t), 128, kT, 0, t * 128)
                for hh in range(4):
                    h = g * 4 + hh
                    qtm = qtmr.next()
                    s.dma("sp", qtm[:, :, :], View(s.proj, pv[:, :, h * 128:(h + 1) * 128], allk((h // 4) * 512)))
                    qT = qTr.next()
                    for t in range(NT):
                        s.transpose_tile(_Sub(qtm, t), 128, qT, 0, t * 128)
                    oh = ohr.next()
                    for t in range(NT):
                        qv = qT[:, 0, t * 128:(t + 1) * 128]
                        sS = ssb.next()
                        if t < NCTX:
                            pa = s.psf.next()
                            s.mm(pa[:, 0:256], qv, kT[:, 0, 0:256], True, True)
                            s.copy("act", sS[:, 0:256], pa[:, 0:256])
                            n = 256
                            ktiles = [0, 1]
                        else:
                            lo, hi = max(NCTX, t - 1), min(NT - 1, t + 1)
                            nl = hi - lo + 1
                            pa = s.psf.next()
                            s.mm(pa[:, 0:256], qv, kT[:, 0, 0:256], True, True)
                            pb_ = s.psf.next()
                            s.mm(pb_[:, 0:nl * 128], qv, kT[:, 0, lo * 128:(hi + 1) * 128], True, True)
                            s.copy("act", sS[:, 0:256], pa[:, 0:256])
                            m0 = (lo - (t - 1)) * 128
                            s.tt("dve", sS[:, 256:256 + nl * 128], pb_[:, 0:nl * 128], mask3[:, m0:m0 + nl * 128], ALU.add)
                            n = 256 + nl * 128
                            ktiles = [0, 1] + list(range(lo, hi + 1))
                        sm = s.small.next()
                        s.rmax(sm[:, 0:1], sS[:, 0:n])
                        s.ts("dve", sm[:, 1:2], sm[:, 0:1], sinkb[:, h:h + 1], -1.0, ALU.max, ALU.mult)
                        pb = pbr.next()
                        s.act(pb[:, 0:n], sS[:, 0:n], AF.Exp, bias=sm[:, 1:2], accum=sm[:, 2:3])
                        s.act(sm[:, 3:4], sinkb[:, h:h + 1], AF.Exp, bias=sm[:, 1:2])
                        s.tt("dve", sm[:, 4:5], sm[:, 2:3], sm[:, 3:4], ALU.add)
                        s.recip(sm[:, 5:6], sm[:, 4:5])
                        nk = n // 128
                        pT = ptr.next()
                        pp = s.psb.next()
                        for j in range(nk):
                            s.tr(pp[:, j * 128:(j + 1) * 128], pb[:, j * 128:(j + 1) * 128], s.identb[:, :])
                        s.copy("act", pT[:, 0:nk, :], View(pp, pp.base[:, 0:nk * 128].rearrange("p (j c) -> p j c", c=128)))
                        po = s.psf.next()
                        for j in range(nk):
                            s.mm(po[:, 0:128], pT[:, j, :], vt[:, ktiles[j], :], j == 0, j == nk - 1)
                        s.ts("dve", oh[:, t, :], po[:, 0:128], sm[:, 5:6], None, ALU.mult)
                    s.dma("pool", View(s.omix, ov[:, :, h * 128:(h + 1) * 128], list(range(NT))), oh[:, :, :])
        s.barrier()
        s.outproj_phase(li, "att_w_out", 2048, True)


    def ret_mixer(s, li):
        tiles = list(range(NT))
        with contextlib.ExitStack() as st:
            hT = s.sb([128, KC, NT * 128], BF16, "hT", st)
            s.n1_phase(li, tiles, hT, st)
            with contextlib.ExitStack() as st2:
                wring = Ring([s.sb([128, 16, 512], BF16, "wi", st2) for _ in range(2)])
                cosT = s.sb([128, 16, 256], F32, "cos", st2)
                sinT = s.sb([128, 16, 256], F32, "sin", st2)
                s.dma("sp", cosT[:, :, :], View(s.I["cos_r"], s.I["cos_r"].base.rearrange("t p c -> p t c")))
                s.dma("sp", sinT[:, :, :], View(s.I["sin_r"], s.I["sin_r"].base.rearrange("t p c -> p t c")))
                xsr = Ring([s.sb([128, 512], F32, "xs", st2) for _ in range(2)])
                tmpr = Ring([s.sb([128, 256], F32, "rt", st2) for _ in range(8)])
                obr = Ring([s.sb([128, 512], BF16, "ob", st2) for _ in range(4)])

                def evac(t, c0, cw, p):
                    ob = obr.next()
                    if c0 >= 8192:
                        s.act(ob[:, 0:cw], p[:, 0:cw], AF.Silu)
                    else:
                        isk = 2048 <= c0 < 4096
                        s.rope_evac(p, cw, t, (256 ** -0.5) if isk else 1.0, c0 < 4096 and t >= NCTX,
                                    cosT, sinT, xsr.next(), tmpr, ob)
                    s.dma("pool", s.proj[t * 128:(t + 1) * 128, c0:c0 + cw].k((t, c0)), ob[:, 0:cw])
                s.gemm_tm(lambda t, kc: hT[:, kc, t * 128:(t + 1) * 128], tiles, "ret_w_in", D, 0, 12288, evac, wring)
            s.barrier()
        pv = s.proj.base.rearrange("(t p) c -> p t c", p=128)
        ov = s.omix.base.rearrange("(t p) c -> p t c", p=128)
        allk = lambda c0: [(t, (c0 // 512) * 512) for t in range(NT)]
        lg = np.log1p(-np.exp2(-5.0 - np.arange(8, dtype=np.float32))).astype(np.float32)
        with contextlib.ExitStack() as st:
            qtm = s.sb([128, NT, 256], BF16, "qtm", st)
            ktm = s.sb([128, NT, 256], BF16, "ktm", st)
            vtm = s.sb([128, NT, 512], BF16, "vtm", st)
            gtm = s.sb([128, NT, 512], BF16, "gtm", st)
            oh = s.sb([128, NT, 512], BF16, "oh", st)
            qT = s.sb([128, 2, NT * 128], BF16, "qT", st)
            kT = s.sb([128, 2, NT * 128], BF16, "kT", st)
            sbst = s.sb([128, NT, 2, 512], BF16, "sbst", st)
            S32 = s.sb([128, 2, 512], F32, "S32", st)
            Sfb = Ring([s.sb([128, 2, 512], BF16, "Sfb", st) for _ in range(2)])
            Dm = s.sb([128, 128], F32, "Dm", st)
            qdec = s.sb([128, 2, 128], F32, "qdec", st)
            kdec = s.sb([128, 16], F32, "kdec", st)
            gng = s.sb([128, 512], F32, "gng", st)
            s.dma("sp", kdec[:, :], s.I["ret_kdec"][:, :])
            ksr = Ring([s.sb([128, 256], BF16, "ks", st) for _ in range(3)])
            ptr = Ring([s.sb([128, 128], BF16, "PT", st) for _ in range(2)])
            qfr = Ring([s.sb([128, 2, 128], BF16, "qf", st) for _ in range(4)])
            o32 = Ring([s.sb([128, 512], F32, "o32", st) for _ in range(2)])
            junk = s.sb([128, 512], BF16, "junk", st)
            for h in range(8):
                cd = float(np.exp(lg[h] * np.float32(128.0)))
                s.dma("sp", qtm[:, :, :], View(s.proj, pv[:, :, h * 256:(h + 1) * 256], allk(h * 256)))
                s.dma("sp", ktm[:, :, :], View(s.proj, pv[:, :, 2048 + h * 256:2048 + (h + 1) * 256], allk(2048 + h * 256)))
                s.dma("sp", vtm[:, :, :], View(s.proj, pv[:, :, 4096 + h * 512:4096 + (h + 1) * 512], allk(4096 + h * 512)))
                s.dma("sp", gtm[:, :, :], View(s.proj, pv[:, :, 8192 + h * 512:8192 + (h + 1) * 512], allk(8192 + h * 512)))
                s.dma("sp", Dm[:, :], s.I["ret_D"][h, :, :])
                s.load_bc(qdec[:, 0, :], s.I["ret_qdec"], s.I["ret_qdec"].base[h, 0, :])
                s.load_bc(qdec[:, 1, :], s.I["ret_qdec"], s.I["ret_qdec"].base[h, 1, :])
                s.load_bc(gng[:, :], s.I["ret_gn_g"], s.I["ret_gn_g"].base[0, h * 512:(h + 1) * 512])
                for t in range(NT):
                    s.transpose_tile(_Sub(qtm, t), 256, qT, 0, t * 128)
                    s.transpose_tile(_Sub(ktm, t), 256, kT, 0, t * 128)

                def upd(t, col):
                    ks = ksr.next()
                    s.ts("pool", ks[:, :], ktm[:, t, :], kdec[:, col:col + 1], None, ALU.mult)
                    for dc in range(2):
                        p = s.psf.next()
                        s.mm(p[:, :], ks[:, dc * 128:(dc + 1) * 128], vtm[:, t, :], True, True)
                        s.stt(S32[:, dc, :], S32[:, dc, :], cd, p[:, :], ALU.mult, ALU.add)
                s.memset("dve", S32[:, :, :], 0.0)
                for t in [1, 0] + list(range(NT - 1, NCTX - 1, -1)):
                    s.copy("act", sbst[:, t, :, :], S32[:, :, :])
                    upd(t, h * 2 + 1)
                s.memset("dve", S32[:, :, :], 0.0)
                for t in range(NT):
                    sf = Sfb.next()
                    s.copy("act", sf[:, :, :], S32[:, :, :])
                    tk = slice(t * 128, (t + 1) * 128)
                    p1 = s.psf.next()
                    for dc in range(2):
                        s.mm(p1[:, 0:128], kT[:, dc, tk], qT[:, dc, tk], dc == 0, dc == 1)
                    PT = ptr.next()
                    s.tt("dve", PT[:, :], p1[:, 0:128], Dm[:, :], ALU.mult)
                    qf = qfr.next()
                    qb = qfr.next()
                    for dc in range(2):
                        s.tt("pool", qf[:, dc, :], qT[:, dc, tk], qdec[:, 0, :], ALU.mult)
                        s.tt("pool", qb[:, dc, :], qT[:, dc, tk], qdec[:, 1, :], ALU.mult)
                    po = s.psf.next()
                    s.mm(po[:, :], PT[:, :], vtm[:, t, :], True, False)
                    for dc in range(2):
                        s.mm(po[:, :], qf[:, dc, :], sf[:, dc, :], False, False)
                    for dc in range(2):
                        s.mm(po[:, :], qb[:, dc, :], sbst[:, t, dc, :], False, dc == 1)
                    sm = s.small.next()
                    s.act(junk[:, :], po[:, :], AF.Square, accum=sm[:, 0:1])
                    s.ts("dve", sm[:, 1:2], sm[:, 0:1], 1.0 / 512, EPS, ALU.mult, ALU.add)
                    s.act(sm[:, 2:3], sm[:, 1:2], AF.Sqrt)
                    s.recip(sm[:, 3:4], sm[:, 2:3])
                    o = o32.next()
                    s.stt(o[:, :], po[:, :], sm[:, 3:4], gng[:, :], ALU.mult, ALU.mult)
                    s.tt("pool", oh[:, t, :], o[:, :], gtm[:, t, :], ALU.mult)
                    upd(t, h * 2 + 0)
                s.dma("pool", View(s.omix, ov[:, :, h * 512:(h + 1) * 512], list(range(NT))), oh[:, :, :])
        s.barrier()
        s.outproj_phase(li, "ret_w_out", 4096, True)


    def gdn_mixer(s, li):
        tiles = list(range(NT))
        XW = 2310
        mixT = s.mixT
        with contextlib.ExitStack() as st:
            hT = s.sb([128, KC, NT * 128], BF16, "hT", st)
            s.n1_phase(li, tiles, hT, st)
            with contextlib.ExitStack() as st2:
                wring = Ring([s.sb([128, 16, 512], BF16, "wi", st2) for _ in range(2)])
                X = s.sb([128, XW], F32, "X", st2)
                Y = s.sb([128, NT * 128], F32, "Y", st2)
                Z = s.sb([128, NT * 128], F32, "Z", st2)
                SQ = s.sb([128, NT * 128], F32, "SQ", st2)
                OB = Ring([s.sb([128, NT * 128], BF16, "OB", st2) for _ in range(2)])
                rn = Ring([s.sb([128, 512], F32, "rn", st2) for _ in range(2)])
                onesf = s.sb([128, 128], F32, "onesf", st2)
                s.memset("dve", onesf[:, :], 1.0)
                s.memset("dve", X[:, :], 0.0)
                cw4 = s.sb([4, 8192], F32, "cw4", st2)
                cwT = s.sb([128, 64, 4], F32, "cwT", st2)
                s.dma("sp", cw4[:, :], s.I["gdn_conv_w"][0, :, :])
                p = s.psf.next()
                for j in range(64):
                    s.tr(p[:, j * 4:(j + 1) * 4], cw4[0:4, j * 128:(j + 1) * 128], s.identf[0:4, 0:4])
                s.copy("dve", cwT[:, :, :], View(p, p.base[:, 0:256].rearrange("p (j k) -> p j k", k=4)))
                wv = s.wview("gdn_w_in")
                blocks = [(0, 256)] + [(256 + i * 512, 512) for i in range(4)]
                xcol = lambda tok: tok + 2 if tok < 256 else tok + 5
                for c0 in range(0, 8192, 512):
                    wt = wring.next()
                    s.dma("sp", wt[:, :, :], View(s.wb["gdn_w_in"], wv[:, :, c0:c0 + 512], s.wkeys(0, D)))
                    for j in range(4):
                        ch = c0 // 128 + j
                        for tb, tw in blocks:
                            p = s.psf.next()
                            for kc in range(KC):
                                s.mm(p[:, 0:tw], wt[:, kc, j * 128:(j + 1) * 128], hT[:, kc, tb:tb + tw], kc == 0, kc == KC - 1)
                            s.copy("act", X[:, xcol(tb):xcol(tb) + tw], p[:, 0:tw])
                        for (t0, n) in [(0, 256), (256, 2048)]:
                            x0 = xcol(t0)
                            s.ts("dve", Y[:, t0:t0 + n], X[:, x0 - 2:x0 - 2 + n], cwT[:, ch, 0:1], None, ALU.mult)
                            for k in range(1, 4):
                                s.stt(Y[:, t0:t0 + n], X[:, x0 - 2 + k:x0 - 2 + k + n], cwT[:, ch, k:k + 1], Y[:, t0:t0 + n],
                                      ALU.mult, ALU.add)
                        ob = OB.next()
                        if ch >= 32:
                            s.act(ob[:, :], Y[:, :], AF.Silu)
                        else:
                            s.act(Z[:, :], Y[:, :], AF.Silu)
                            s.tt("pool", SQ[:, :], Z[:, :], Z[:, :], ALU.mult)
                            for tb, tw in blocks:
                                p = s.psf.next()
                                s.mm(p[:, 0:tw], onesf[:, :], SQ[:, tb:tb + tw], True, True)
                                r = rn.next()
                                s.ts("dve", r[:, 0:tw], p[:, 0:tw], 1e-6, None, ALU.add)
                                s.act(r[:, 0:tw], r[:, 0:tw], AF.Sqrt)
                                s.recip(r[:, 0:tw], r[:, 0:tw])
                                s.stt(ob[:, tb:tb + tw], Z[:, tb:tb + tw], (128 ** -0.5) if ch < 16 else 1.0, r[:, 0:tw],
                                      ALU.mult, ALU.mult)
                        s.dma("pool", mixT[ch, :, :].k(ch), ob[:, :])
                zr = Ring([s.sb([128, 512], BF16, "z", st2) for _ in range(3)])
                fr = Ring([s.sb([128, 128], F32, "f", st2) for _ in range(2)])

                def evac(t, c0, cw, p):
                    if c0 < 12288:
                        z = zr.next()
                        s.act(z[:, 0:cw], p[:, 0:cw], AF.Silu)
                        s.dma("pool", s.proj[t * 128:(t + 1) * 128, c0 - 8192:c0 - 8192 + cw].k((t, c0 - 8192)), z[:, 0:cw])
                    else:
                        f = fr.next()
                        s.copy("dve", f[:, 0:cw], p[:, 0:cw])
                        s.dma("pool", s.projf[t * 128:(t + 1) * 128, 0:cw].k(t), f[:, 0:cw])
                s.gemm_tm(lambda t, kc: hT[:, kc, t * 128:(t + 1) * 128], tiles, "gdn_w_in", D, 8192, 12416, evac, wring)
            s.barrier()
        if os.environ.get("GDNDBG") == "1":
            s.outproj_phase(li, "gdn_w_out", 4096, False)
            return
        pv = s.proj.base.rearrange("(t p) c -> p t c", p=128)
        ov = s.omix.base.rearrange("(t p) c -> p t c", p=128)
        with contextlib.ExitStack() as st:
            f32t = lambda nm, shp: s.sb(shp, F32, nm, st)
            beta = f32t("beta", [128, NT, 64])
            nbeta = f32t("nbeta", [128, NT, 64])
            gc = f32t("gc", [128, NT, 64])
            eg = f32t("eg", [128, NT, 64])
            egl = f32t("egl", [128, NT, 64])
            etot = f32t("etot", [128, NT, 64])
            beg = f32t("beg", [128, NT, 64])
            onesf = f32t("onesf", [128, 128])
            masks = f32t("masks", [128, 14, 128])
            ngb = f32t("ngb", [128, 128])
            st_g = contextlib.ExitStack()
            f32g = lambda nm, shp: s.sb(shp, F32, nm, st_g)
            raw = f32g("raw", [128, NT, 128])
            g = f32g("g", [128, NT, 64])
            t1 = f32g("t1", [128, NT, 64])
            t2 = f32g("t2", [128, NT, 64])
            tot = f32g("tot", [128, NT, 64])
            alb = f32g("alb", [128, 64])
            dtb = f32g("dtb", [128, 64])
            nA = f32g("nA", [128, 64])
            s.dma("sp", raw[:, :, :], View(s.projf, s.projf.base.rearrange("(t p) c -> p t c", p=128), list(range(NT))))
            s.memset("dve", onesf[:, :], 1.0)
            s.dma("sp", masks[:, :, :], View(s.I["gdn_masks"], s.I["gdn_masks"].base.rearrange("k p c -> p k c")))
            s.load_bc(alb[:, :], s.I["gdn_a_log"], s.I["gdn_a_log"].base[0].rearrange("a b -> (a b)"))
            s.load_bc(dtb[:, :], s.I["gdn_dt_bias"], s.I["gdn_dt_bias"].base[0].rearrange("a b -> (a b)"))
            s.load_bc(ngb[:, :], s.I["gdn_norm_g"], s.I["gdn_norm_g"].base[0, :])
            s.act(nA[:, :], alb[:, :], AF.Exp)
            s.ts("dve", nA[:, :], nA[:, :], -1.0, None, ALU.mult)
            s.act(beta[:, :, :], raw[:, :, 0:64], AF.Sigmoid)
            s.ts("dve", nbeta[:, :, :], beta[:, :, :], -1.0, None, ALU.mult)
            for t in range(NT):
                s.tt("dve", t1[:, t, :], raw[:, t, 64:128], dtb[:, :], ALU.add)
            s.act(t2[:, :, :], t1[:, :, :], AF.Abs)
            s.act(t2[:, :, :], t2[:, :, :], AF.Exp, scale=-1.0)
            s.act(t2[:, :, :], t2[:, :, :], AF.Ln, bias=1.0)
            s.ts("dve", t1[:, :, :], t1[:, :, :], 0.0, None, ALU.max)
            s.tt("dve", t1[:, :, :], t1[:, :, :], t2[:, :, :], ALU.add)
            for t in range(NT):
                s.tt("dve", g[:, t, :], t1[:, t, :], nA[:, :], ALU.mult)
            MU_F, MU_B, M_SL, M_SU, M_UI, M_LI = range(6)
            for t in range(NT):
                p = s.psf.next()
                s.mm(p[:, 0:32], masks[:, MU_F, :], g[:, t, 0:32], True, True)
                s.mm(p[:, 32:64], masks[:, MU_B, :], g[:, t, 32:64], True, True)
                s.mm(p[:, 64:128], onesf[:, :], g[:, t, :], True, True)
                s.copy("act", gc[:, t, :], p[:, 0:64])
                s.copy("dve", tot[:, t, :], p[:, 64:128])
            s.act(eg[:, :, :], gc[:, :, :], AF.Exp)
            s.tt("dve", t1[:, :, :], tot[:, :, :], gc[:, :, :], ALU.subtract)
            s.act(egl[:, :, :], t1[:, :, :], AF.Exp)
            s.act(etot[:, :, :], tot[:, :, :], AF.Exp)
            s.tt("dve", beg[:, :, :], beta[:, :, :], eg[:, :, :], ALU.mult)
            s.barrier()
            st_g.close()
            kT = s.sb([128, 1, NT * 128], BF16, "kT", st)
            qT = s.sb([128, 1, NT * 128], BF16, "qT", st)
            Ktm = s.sb([128, NT, 128], BF16, "Ktm", st)
            vT = [s.sb([128, 1, NT * 128], BF16, "vT", st) for _ in range(2)]
            Vtm = [s.sb([128, NT, 128], BF16, "Vtm", st) for _ in range(2)]
            ztm = [s.sb([128, NT, 128], BF16, "ztm", st) for _ in range(2)]
            acc = [f32t("acc", [128, NT, 128]) for _ in range(2)]
            oh = [s.sb([128, NT, 128], BF16, "oh", st) for _ in range(2)]
            S32 = [[f32t("S32", [128, 128]) for _ in range(2)] for _ in range(2)]
            Sb = [[Ring([s.sb([128, 128], BF16, "Sb", st) for _ in range(2)]) for _ in range(2)] for _ in range(2)]
            shr = Ring([f32t("shr", [128, 5, 128]) for _ in range(4)])
            frs = [[Ring([f32t("fr", [128, 128]) for _ in range(6)]) for _ in range(2)] for _ in range(2)]
            rrs = [[Ring([f32t("rr", [128, 128]) for _ in range(11)]) for _ in range(2)] for _ in range(2)]
            brs = [[Ring([s.sb([128, 128], BF16, "br", st) for _ in range(26)]) for _ in range(2)] for _ in range(2)]
            junk = s.sb([128, 128], BF16, "junk", st)

            def unit(hk, e, d, t, sh, sbc, accw):
                fr = frs[e][d]
                rr = rrs[e][d]
                br = brs[e][d]
                tk = slice(t * 128, (t + 1) * 128)
                hv = hk * 2 + e
                col = d * 32 + hv
                gcc = gc[:, t, col:col + 1]
                M = fr.next()
                s.act(M[:, :], onesf[:, :], AF.Copy, scale=gcc)
                pR = s.psf.next()
                s.tr(pR[:, 0:128], M[:, :], s.identf[:, :])
                A1 = fr.next()
                s.ts("dve", A1[:, :], pR[:, 0:128], gcc, 0.0, ALU.subtract, ALU.max)
                B1 = fr.next()
                s.ts("dve", B1[:, :], pR[:, 0:128], gcc, 0.0, ALU.subtract, ALU.min)
                yield
                s.act(A1[:, :], A1[:, :], AF.Exp, scale=-1.0)
                s.act(B1[:, :], B1[:, :], AF.Exp)
                yield
                R_ = lambda v: View(v.tl, v.ap.bitcast(F32R), v.key)
                P = rr.next()
                s.stt(R_(P[:, :]), A1[:, :], nbeta[:, t, col:col + 1], sh[:, 0, :], ALU.mult, ALU.mult)
                Pp = [P]
                for j in range(1, 4):
                    pj = br.next()
                    s.stt(pj[:, :], A1[:, :], nbeta[:, t, col:col + 1], sh[:, j, :], ALU.mult, ALU.mult)
                    Pp.append(pj)
                AT = br.next()
                s.tt("pool", AT[:, :], B1[:, :], sh[:, 4, :], ALU.mult)
                yield
                pt = s.psf.next()
                s.tr(pt[:, 0:128], P[:, :], s.identf[:, :])
                PT = rr.next()
                s.copy("act", R_(PT[:, :]), pt[:, 0:128])
                Xt = rr.next()
                s.tt("dve", R_(Xt[:, :]), pt[:, 0:128], s.identf[:, :], ALU.add)
                X = rr.next()
                s.tt("dve", R_(X[:, :]), P[:, :], s.identf[:, :], ALU.add)
                pp = s.psb.next()
                for j in range(1, 4):
                    s.tr(pp[:, (j - 1) * 128:j * 128], Pp[j][:, :], s.identb[:, :])
                PTb = br.next()
                PTb2 = br.next()
                PTb3 = br.next()
                PTp = [PT, PTb, PTb2, PTb3]
                for j in range(1, 4):
                    s.copy("act", PTp[j][:, :], pp[:, (j - 1) * 128:j * 128])
                yield
                for lv in range(1, 4):
                    p1 = s.psf.next()
                    s.mm(p1[:, 0:128], R_(PT[:, :]), R_(P[:, :]), True, True)
                    s.mm(p1[:, 128:256], R_(P[:, :]), R_(PT[:, :]), True, True)
                    Pn = rr.next()
                    PTn = rr.next()
                    s.copy("act", R_(Pn[:, :]), p1[:, 0:128])
                    s.copy("act", R_(PTn[:, :]), p1[:, 128:256])
                    yield
                    p3 = s.psf.next()
                    s.mm(p3[:, 0:128], R_(PTn[:, :]), R_(X[:, :]), True, True)
                    s.mm(p3[:, 128:256], R_(Pn[:, :]), R_(Xt[:, :]), True, True)
                    last = lv == 3
                    if last:
                        Xb = br.next()
                        Xtb = br.next()
                        s.tt("dve", Xb[:, :], X[:, :], p3[:, 0:128], ALU.add)
                        s.tt("dve", Xtb[:, :], Xt[:, :], p3[:, 128:256], ALU.add)
                    else:
                        s.tt("dve", R_(X[:, :]), X[:, :], p3[:, 0:128], ALU.add)
                        s.tt("dve", R_(Xt[:, :]), Xt[:, :], p3[:, 128:256], ALU.add)
                    P, PT = Pn, PTn
                    yield
                for j in range(1, 4):
                    pa = s.psf.next()
                    A1s = br.next()
                    B1s = br.next()
                    if j < 3:
                        s.mm(pa[:, 0:128], PTp[j][:, :], Xb[:, :], True, True)
                    s.mm(pa[:, 128:256], Pp[j][:, :], Xtb[:, :], True, True)
                    if j < 3:
                        s.copy("act", A1s[:, :], pa[:, 0:128])
                    s.copy("act", B1s[:, :], pa[:, 128:256])
                    yield
                    pc = s.psf.next()
                    if j < 3:
                        s.mm(pc[:, 0:128], Xtb[:, :], A1s[:, :], True, True)
                    s.mm(pc[:, 128:256], Xb[:, :], B1s[:, :], True, True)
                    Xtn = br.next()
                    if j < 3:
                        Xn = br.next()
                        s.tt("dve", Xn[:, :], Xb[:, :], pc[:, 0:128], ALU.add)
                    s.tt("dve", Xtn[:, :], Xtb[:, :], pc[:, 128:256], ALU.add)
                    if j < 3:
                        Xb = Xn
                    Xtb = Xtn
                    yield
                TTb = Xtb
                vb = br.next()
                s.act(vb[:, :], Vtm[e][:, t, :], AF.Copy, scale=beta[:, t, col:col + 1])
                kbg = br.next()
                s.act(kbg[:, :], Ktm[:, t, :], AF.Copy, scale=beg[:, t, col:col + 1])
                kdl = br.next()
                s.ts("pool", kdl[:, :], Ktm[:, t, :], egl[:, t, col:col + 1], None, ALU.mult)
                yield
                pw = s.psf.next()
                s.mm(pw[:, 0:128], kbg[:, :], TTb[:, :], True, True)
                nw = br.next()
                s.act(nw[:, :], pw[:, 0:128], AF.Copy, scale=-1.0)
                yield
                S_ = S32[e][d]
                sb_old = sbc[(e, d)]
                pvn = s.psf.next()
                s.mm(pvn[:, 0:128], TTb[:, :], vb[:, :], True, False)
                s.mm(pvn[:, 0:128], nw[:, :], sb_old[:, :], False, True)
                if t >= NCTX:
                    s.mm(pvn[:, 128:256], qT[:, 0, tk], sb_old[:, :], True, True)
                vn = br.next()
                s.copy("act", vn[:, :], pvn[:, 0:128])
                yield
                po2 = s.psf.next()
                s.mm(po2[:, 128:256], kdl[:, :], vn[:, :], True, True)
                if t >= NCTX:
                    s.mm(po2[:, 0:128], AT[:, :], vn[:, :], True, True)
                s.stt(S_[:, :], S_[:, :], etot[:, t, col:col + 1], po2[:, 128:256], ALU.mult, ALU.add)
                sbn = Sb[e][d].next()
                s.copy("act", sbn[:, :], S_[:, :])
                sbc[(e, d)] = sbn
                if t >= NCTX:
                    o1 = fr.next()
                    s.ts("dve", o1[:, :], pvn[:, 128:256], eg[:, t, col:col + 1], None, ALU.mult)
                    if (e, t) not in accw:
                        accw.add((e, t))
                        s.tt("dve", acc[e][:, t, :], o1[:, :], po2[:, 0:128], ALU.add)
                    else:
                        s.tt("dve", o1[:, :], o1[:, :], po2[:, 0:128], ALU.add)
                        s.tt("pool", acc[e][:, t, :], acc[e][:, t, :], o1[:, :], ALU.add)

            for hk in range(int(os.environ.get("GDNHK", "16"))):
                s.dma("sp", kT[:, 0, :], mixT[16 + hk, :, :].k(16 + hk))
                s.dma("sp", qT[:, 0, :], mixT[hk, :, :].k(hk))
                for t in range(NT):
                    pp = s.psb.next()
                    s.tr(pp[:, 0:128], kT[:, 0, t * 128:(t + 1) * 128], s.identb[:, :])
                    s.copy("act", Ktm[:, t, :], pp[:, 0:128])
                for e in range(2):
                    hv = hk * 2 + e
                    s.dma("sp", vT[e][:, 0, :], mixT[32 + hv, :, :].k(32 + hv))
                    s.dma("sp", ztm[e][:, :, :], View(s.proj, pv[:, :, hv * 128:(hv + 1) * 128],
                                                      [(t, (hv // 4) * 512) for t in range(NT)]))
                    for t in range(NT):
                        pp = s.psb.next()
                        s.tr(pp[:, 0:128], vT[e][:, 0, t * 128:(t + 1) * 128], s.identb[:, :])
                        s.copy("dve", Vtm[e][:, t, :], pp[:, 0:128])
                orders = {0: list(range(NT)), 1: [1, 0] + list(range(NT - 1, NCTX - 1, -1))}
                sbc = {}
                accw = set()
                for e in range(2):
                    for d in range(2):
                        s.memset("dve", S32[e][d][:, :], 0.0)
                        sbc[(e, d)] = Sb[e][d].next()
                        s.memset("pool", sbc[(e, d)][:, :], 0.0)
                for step in range(NT):
                    gens = []
                    for d in range(2):
                        t = orders[d][step]
                        tk = slice(t * 128, (t + 1) * 128)
                        sh = shr.next()
                        pG = s.psf.next()
                        s.mm(pG[:, 0:128], kT[:, 0, tk], kT[:, 0, tk], True, True)
                        s.mm(pG[:, 128:256], kT[:, 0, tk], qT[:, 0, tk], True, True)
                        for j in range(4):
                            s.tt("dve", sh[:, j, :], pG[:, 0:128], masks[:, (6 if d == 0 else 10) + j, :], ALU.mult)
                        s.tt("dve", sh[:, 4, :], pG[:, 128:256], masks[:, M_UI if d == 0 else M_LI, :], ALU.mult)
                        for e in range(2):
                            gens.append(unit(hk, e, d, t, sh, sbc, accw))
                    while gens:
                        for g_ in list(gens):
                            try:
                                next(g_)
                            except StopIteration:
                                gens.remove(g_)
                for e in range(2):
                    hv = hk * 2 + e
                    for t in range(NCTX, NT):
                        sm = s.small.next()
                        s.act(junk[:, :], acc[e][:, t, :], AF.Square, accum=sm[:, 0:1])
                        s.ts("dve", sm[:, 1:2], sm[:, 0:1], 1.0 / 128, EPS, ALU.mult, ALU.add)
                        s.act(sm[:, 2:3], sm[:, 1:2], AF.Sqrt)
                        s.recip(sm[:, 3:4], sm[:, 2:3])
                        s.stt(acc[e][:, t, :], acc[e][:, t, :], sm[:, 3:4], ngb[:, :], ALU.mult, ALU.mult)
                        s.tt("pool", oh[e][:, t, :], acc[e][:, t, :], ztm[e][:, t, :], ALU.mult)
                    s.dma("pool", View(s.omix, ov[:, NCTX:NT, hv * 128:(hv + 1) * 128], list(range(NCTX, NT))),
                          oh[e][:, NCTX:NT, :])
        s.barrier()
        s.outproj_phase(li, "gdn_w_out", 4096, False)

    def build(s):
        s.cast_weights(s.layers[0])
        s.ada_phase()
        for i, li in enumerate(s.layers):
            kind = li % 4
            want_ctx = li < 2
            if i + 1 < len(s.layers):
                s.cast_weights(s.layers[i + 1])
            if kind == 3:
                s.sgu_mixer(li)
            elif kind == 1:
                s.att_mixer(li)
            elif kind == 0:
                s.ret_mixer(li)
            elif kind == 2:
                s.gdn_mixer(li)
            s.mlp_phase(li, want_ctx)
        s.final_phase()


def make_consts():
    c = {}
    c["ident"] = np.eye(128, dtype=np.float32)
    i = np.arange(128)[:, None]
    j = np.arange(128)[None, :]
    NEG = -30000.0
    mprev = np.where(j >= i, 0.0, NEG)
    mnext = np.where(j <= i, 0.0, NEG)
    c["mask3"] = np.concatenate([mprev, np.zeros((128, 128)), mnext], axis=1).astype(np.float32)

    def rope(hd, rep):
        rows = 2048 // 64
        row = np.repeat(np.arange(rows, dtype=np.float32), 64)
        col = np.tile(np.arange(64, dtype=np.float32), rows)
        ad = hd // 2
        inv = np.exp(np.float32(-math.log(10000.0)) * np.arange(0, ad, 2, dtype=np.float32) / np.float32(ad)).astype(np.float32)
        ang = np.concatenate([row[:, None] * inv, col[:, None] * inv], axis=-1).astype(np.float32)
        cs = np.cos(ang).astype(np.float32)
        sn = np.sin(ang).astype(np.float32)
        cs = np.tile(cs, (1, rep)).reshape(16, 128, -1)
        sn = np.tile(sn, (1, rep)).reshape(16, 128, -1)
        return np.ascontiguousarray(cs), np.ascontiguousarray(sn)
    c["cos_a"], c["sin_a"] = rope(128, 4)
    c["cos_r"], c["sin_r"] = rope(256, 2)
    tt_ = np.arange(128)[:, None]
    cc_ = np.arange(128)[None, :]
    lo = tt_ > cc_
    F = [lo & (tt_ // 16 == cc_ // 16),
         lo & (tt_ // 32 == cc_ // 32) & (tt_ // 16 != cc_ // 16),
         lo & (tt_ // 64 == cc_ // 64) & (tt_ // 32 != cc_ // 32),
         lo & (tt_ // 64 != cc_ // 64)]
    Bm = [f.T for f in F]
    c["gdn_masks"] = np.stack([tt_ <= cc_, tt_ >= cc_, tt_ > cc_, tt_ < cc_, cc_ >= tt_, cc_ <= tt_] + F + Bm).astype(np.float32)
    lg = np.log1p(-np.exp2(-5.0 - np.arange(8, dtype=np.float32))).astype(np.float32)
    pos = np.arange(128, dtype=np.float32)
    diff = np.abs(pos[:, None] - pos[None, :])
    Dm = np.exp(lg[:, None, None] * diff[None]).astype(np.float32)
    Dm[:, np.arange(128), np.arange(128)] = 2.0
    c["ret_D"] = np.ascontiguousarray(Dm)
    qd = np.stack([np.exp(lg[:, None] * (pos + 1.0)[None]), np.exp(lg[:, None] * (128.0 - pos)[None])], axis=1)
    c["ret_qdec"] = np.ascontiguousarray(qd.astype(np.float32))
    kd = np.stack([np.exp(lg[:, None] * (127.0 - pos)[None]), np.exp(lg[:, None] * pos[None])], axis=1)
    c["ret_kdec"] = np.ascontiguousarray(kd.astype(np.float32).transpose(2, 0, 1).reshape(128, 16))
    return c


_CACHE = {}


def kernel(**inputs):
    layers = inputs.pop("_layers", (0, 1, 2, 3))
    ncores = inputs.pop("_ncores", 8)
    consts = make_consts()
    key = (tuple(layers), ncores)
    if key not in _CACHE:
        _CACHE[key] = Model(layers=layers, consts=consts)
    m = _CACHE[key]
    x = np.asarray(inputs["x"], dtype=np.float32)
    ctx = np.asarray(inputs["ctx"], dtype=np.float32)
    c = np.asarray(inputs["c"], dtype=np.float32)
    cc = np.asarray(inputs["c_ctx"], dtype=np.float32)
    shared = {k: np.ascontiguousarray(np.asarray(inputs[k], dtype=np.float32)) for k in IN_SHAPES}
    shared.update(consts)
    in_maps = []
    for b in range(ncores):
        d = dict(shared)
        d["xin"] = np.ascontiguousarray(np.concatenate([ctx[b], x[b]], axis=0))
        d["c2"] = np.ascontiguousarray(np.stack([c[b], cc], axis=0))
        in_maps.append(d)
    if os.environ.get("KTRACE") == "1":
        res = run_bass_kernel_spmd(m.nc, in_maps, core_ids=list(range(ncores)), trace=True)
        print("EXEC_NS", res.exec_time_ns)
    else:
        res = run_bass_kernel_spmd(m.nc, in_maps, core_ids=list(range(ncores)))
    out = np.stack([np.asarray(r["y"], dtype=np.float32) for r in res.results], axis=0)
    return out
```

```python
import contextlib
import math
import os
import numpy as np
import concourse.bass as bass
import concourse.mybir as mybir
from concourse.bass_utils import run_bass_kernel_spmd

F32 = mybir.dt.float32
BF16 = mybir.dt.bfloat16
F32R = mybir.dt.float32r
AF = mybir.ActivationFunctionType
ALU = mybir.AluOpType
AX = mybir.AxisListType

D = 2048
KC = 16
NT = 18
NCTX = 2
EPS = 1e-6
HID = 8192


class View:
    __slots__ = ("tl", "ap", "key")

    def __init__(s, tl, ap, key="*"):
        s.tl = tl
        s.ap = ap
        s.key = key

    def k(s, key):
        return View(s.tl, s.ap, key)


class Tl:
    def __init__(s, base, name):
        s.base = base
        s.name = name
        s.st = {}
        s.psum = False

    def __getitem__(s, idx):
        return View(s, s.base[idx])

    def v(s, ap, key="*"):
        return View(s, ap, key)


class _Sub:
    def __init__(s, tl, t):
        s.tl = tl
        s.t = t

    def __getitem__(s, idx):
        return View(s.tl, s.tl.base[:, s.t, :][idx])


class Ring:
    def __init__(s, tls):
        s.tls = tls
        s.i = 0

    def next(s):
        t = s.tls[s.i % len(s.tls)]
        s.i += 1
        return t


class Bld:
    def __init__(s):
        s.nc = bass.Bass("TRN2", target_bir_lowering=False)
        nc = s.nc
        s.es = contextlib.ExitStack()
        s.E = {"pe": nc.tensor, "act": nc.scalar, "dve": nc.vector, "pool": nc.gpsimd, "sp": nc.sync}
        s.sem = {}
        s.cnt = {}
        for e in ["pe", "act", "dve", "pool"]:
            s.sem[e] = s.es.enter_context(nc.semaphore("s_" + e))
            s.cnt[e] = 0
        s.dq = {}
        for q, n in [("sp", 10), ("pool", 16), ("act", 2)]:
            names = []
            for i in range(n):
                nm = "d_%s%d" % (q, i)
                s.sem[nm] = s.es.enter_context(nc.semaphore(nm))
                s.cnt[nm] = 0
                names.append(nm)
            s.dq[q] = names
        s.dqi = {"sp": 0, "pool": 0, "act": 0}
        s.known = {e: {} for e in s.E}
        s.ninst = 0
        s.uid = 0

    def sb(s, shape, dt, name=None, stack=None):
        s.uid += 1
        nm = "%s_%d" % (name or "t", s.uid)
        t = (stack or s.es).enter_context(s.nc.sbuf_tensor(nm, list(shape), dt))
        return Tl(t, nm)

    def ps(s, shape, dt, name=None):
        s.uid += 1
        nm = "%s_%d" % (name or "p", s.uid)
        t = s.es.enter_context(s.nc.psum_tensor(nm, list(shape), dt))
        tl = Tl(t, nm)
        tl.psum = True
        return tl

    def dram(s, name, shape, dt, kind="Internal"):
        t = s.nc.dram_tensor(name, list(shape), dt, kind=kind)
        return Tl(t.ap(), name)

    def _states(s, v):
        st = v.tl.st
        if "*" not in st:
            st["*"] = [{}, {}]
        if v.key == "*":
            return list(st.values())
        keys = v.key if isinstance(v.key, list) else [v.key]
        out = [st["*"]]
        for k in keys:
            if k not in st:
                st[k] = [{}, {}]
            out.append(st[k])
        return out

    def _need(s, eng, reads, writes, is_dma):
        need = {}

        def mg(d, skip):
            for src, val in d.items():
                if skip and src == eng:
                    continue
                if need.get(src, 0) < val:
                    need[src] = val

        for v in reads:
            for st in s._states(v):
                mg(st[0], False)
                if v.tl.psum:
                    mg(st[1], True)
        for v in writes:
            for st in s._states(v):
                mg(st[0], not is_dma)
                mg(st[1], not is_dma)
        if eng == "pe":
            need.pop("pe", None)
        return need

    def _wait(s, eng, need):
        kn = s.known[eng]
        for src, val in need.items():
            if kn.get(src, 0) < val:
                s.E[eng].wait_ge(s.sem[src], val)
                kn[src] = val
                s.ninst += 1

    def _upd(s, src, val, reads, writes):
        for v in reads:
            st = v.tl.st
            if v.key == "*":
                st["*"][1][src] = val
            else:
                for k in v.key if isinstance(v.key, list) else [v.key]:
                    st[k][1][src] = val
        for v in writes:
            st = v.tl.st
            if v.key == "*":
                st.clear()
                st["*"] = [{src: val}, {}]
            else:
                for k in v.key if isinstance(v.key, list) else [v.key]:
                    st[k] = [{src: val}, {}]

    def op(s, eng, fn, reads, writes):
        need = s._need(eng, reads, writes, False)
        s._wait(eng, need)
        inst = fn()
        s.cnt[eng] += 1
        inst.then_inc(s.sem[eng], 1)
        s.ninst += 1
        s._upd(eng, s.cnt[eng], reads, writes)
        return inst

    def dma(s, q, out, in_, **kw):
        need = s._need(q, [in_], [out], True)
        ring = s.dq[q]
        nm = ring[s.dqi[q] % len(ring)]
        s.dqi[q] += 1
        if s.cnt[nm] > 0:
            need[nm] = max(need.get(nm, 0), s.cnt[nm])
        s._wait(q, need)
        inst = s.E[q].dma_start(out=out.ap, in_=in_.ap, **kw)
        s.cnt[nm] += 16
        inst.then_inc(s.sem[nm], 16)
        s.ninst += 1
        s._upd(nm, s.cnt[nm], [in_], [out])

    def barrier(s, engines=None):
        for e in engines or list(s.E):
            need = {src: val for src, val in s.cnt.items() if val > 0}
            if e == "pe":
                need.pop("pe", None)
            s._wait(e, need)

    def mm(s, out, lhsT, rhs, start, stop):
        return s.op("pe", lambda: s.nc.tensor.matmul(out.ap, lhsT.ap, rhs.ap, start=start, stop=stop),
                    [lhsT, rhs], [out])

    def tr(s, out, in_, ident):
        return s.op("pe", lambda: s.nc.tensor.transpose(out.ap, in_.ap, ident.ap), [in_, ident], [out])

    def act(s, out, in_, func, bias=None, scale=None, accum=None, eng="act"):
        kw = {}
        reads = [in_]
        writes = [out]
        if bias is not None:
            if isinstance(bias, View):
                kw["bias"] = bias.ap
                reads.append(bias)
            else:
                kw["bias"] = bias
        if scale is not None:
            if isinstance(scale, View):
                kw["scale"] = scale.ap
                reads.append(scale)
            else:
                kw["scale"] = scale
        if accum is not None:
            kw["accum_out"] = accum.ap
            writes.append(accum)
        return s.op("act", lambda: s.nc.scalar.activation(out.ap, in_.ap, func, **kw), reads, writes)

    def tt(s, eng, out, in0, in1, op):
        return s.op(eng, lambda: s.E[eng].tensor_tensor(out.ap, in0.ap, in1.ap, op), [in0, in1], [out])

    def ts(s, eng, out, in0, s1, s2, op0, op1=None):
        reads = [in0]
        a1 = s1
        a2 = s2
        if isinstance(s1, View):
            reads.append(s1)
            a1 = s1.ap
        if isinstance(s2, View):
            reads.append(s2)
            a2 = s2.ap
        if op1 is None:
            return s.op(eng, lambda: s.E[eng].tensor_scalar(out.ap, in0.ap, a1, None, op0), reads, [out])
        return s.op(eng, lambda: s.E[eng].tensor_scalar(out.ap, in0.ap, a1, a2, op0, op1), reads, [out])

    def stt(s, out, in0, scalar, in1, op0, op1, eng="dve"):
        reads = [in0, in1]
        a = scalar
        if isinstance(scalar, View):
            reads.append(scalar)
            a = scalar.ap
        return s.op(eng, lambda: s.E[eng].scalar_tensor_tensor(out.ap, in0.ap, a, in1.ap, op0, op1), reads, [out])

    def copy(s, eng, out, in_):
        if eng == "act":
            return s.op("act", lambda: s.nc.scalar.copy(out.ap, in_.ap), [in_], [out])
        return s.op(eng, lambda: s.E[eng].tensor_copy(out.ap, in_.ap), [in_], [out])

    def memset(s, eng, out, val):
        return s.op(eng, lambda: s.E[eng].memset(out.ap, val), [], [out])

    def recip(s, out, in_):
        return s.op("dve", lambda: s.nc.vector.reciprocal(out.ap, in_.ap), [in_], [out])

    def rmax(s, out, in_):
        return s.op("dve", lambda: s.nc.vector.reduce_max(out.ap, in_.ap, axis=AX.X), [in_], [out])


GROUPS = [[0, 1], [2, 3, 4, 5, 6, 7], [8, 9, 10, 11, 12, 13], [14, 15, 16, 17]]
WSPEC = {
    0: [("ret_w_in", 2048, 12288), ("ret_w_out", 4096, 2048)],
    1: [("att_w_in", 2048, 3072), ("att_w_out", 2048, 2048)],
    2: [("gdn_w_in", 2048, 12416), ("gdn_w_out", 4096, 2048)],
    3: [("sgu_w_in", 2048, 8192), ("sgu_w_out", 4096, 2048)],
}
IN_SHAPES = {
    "mod_w": [4, 2048, 12288], "mod_b": [4, 12288], "norm1_g": [4, 2048], "norm2_g": [4, 2048],
    "mlp_up": [4, 2048, 8192], "mlp_down": [4, 8192, 2048], "final_g": [2048],
    "ret_w_in": [1, 2048, 12288], "ret_gn_g": [1, 4096], "ret_w_out": [1, 4096, 2048],
    "att_w_in": [1, 2048, 3072], "att_sink": [1, 16], "att_w_out": [1, 2048, 2048],
    "gdn_w_in": [1, 2048, 12416], "gdn_conv_w": [1, 4, 8192], "gdn_a_log": [1, 2, 32],
    "gdn_dt_bias": [1, 2, 32], "gdn_norm_g": [1, 128], "gdn_w_out": [1, 4096, 2048],
    "sgu_w_in": [1, 2048, 8192], "sgu_ln_g": [1, 4096], "sgu_ln_b": [1, 4096],
    "sgu_w_s": [1, 8, 128, 128], "sgu_b_s": [1, 8, 128], "sgu_w_out": [1, 4096, 2048],
}


class Model(Bld):
    def __init__(s, layers=(0, 1, 2, 3), final=True, consts=None):
        super().__init__()
        s.layers = list(layers)
        s.I = {}
        s.I["xin"] = s.dram("xin", [NT * 128, D], F32, "ExternalInput")
        s.I["c2"] = s.dram("c2", [2, D], F32, "ExternalInput")
        for k, shp in IN_SHAPES.items():
            s.I[k] = s.dram(k, shp, F32, "ExternalInput")
        for k, arr in (consts or {}).items():
            s.I[k] = s.dram(k, list(arr.shape), F32, "ExternalInput")
        s.out = s.dram("y", [16 * 128, D], F32, "ExternalOutput")
        s.xres = s.dram("xres", [NT * 128, D], F32)
        s.adav = {li: s.dram("adav%d" % li, [2, 6 * D], F32) for li in s.layers}
        s.proj = s.dram("proj", [NT * 128, 12416], BF16)
        s.projf = s.dram("projf", [NT * 128, 128], F32)
        s.omix = s.dram("omix", [NT * 128, 4096], BF16)
        s.mixT = s.dram("mixT", [64, 128, NT * 128], BF16)
        s.wb = {}
        for li in s.layers:
            for nm, k, n in WSPEC[li]:
                s.wb[nm] = s.dram(nm + "_bf", [k, n], BF16)
            s.wb["up%d" % li] = s.dram("up%d_bf" % li, [D, HID], BF16)
            s.wb["down%d" % li] = s.dram("down%d_bf" % li, [HID, D], BF16)
        s.psf = Ring([s.ps([128, 512], F32, "psf") for _ in range(6)])
        s.psb = Ring([s.ps([128, 1024], BF16, "psb") for _ in range(2)])
        s.identf = s.sb([128, 128], F32, "identf")
        s.identb = s.sb([128, 128], BF16, "identb")
        s.dma("sp", s.identf[:, :], s.I["ident"][:, :])
        s.copy("dve", s.identb[:, :], s.identf[:, :])
        s.small = Ring([s.sb([128, 8], F32, "sm") for _ in range(12)])
        s.evi = 0
        s.xsrc = s.I["xin"]
        s.build()

    def ev_eng(s):
        s.evi += 1
        return "act" if s.evi % 2 else "dve"

    def load_bc(s, dst, src_tl, ap1d, q="sp", np_=128):
        s.dma(q, dst, View(src_tl, ap1d.partition_broadcast(np_)))

    def cast_weights(s, li):
        if li not in s.layers:
            return
        lst = [(nm, s.I[nm], s.I[nm].base[0], k, n) for nm, k, n in WSPEC[li]]
        lst.insert(1, ("up%d" % li, s.I["mlp_up"], s.I["mlp_up"].base[li], D, HID))
        lst.append(("down%d" % li, s.I["mlp_down"], s.I["mlp_down"].base[li], HID, D))
        for nm, tl, src, k, n in lst:
            for r in range(0, k, 256):
                s.dma("pool", s.wb[nm][r:r + 256, :].k(r), View(tl, src[r:r + 256, :]))

    def ada_phase(s):
        with contextlib.ExitStack() as st:
            c2t = s.sb([2, D], F32, "c2t", st)
            sc = s.sb([2, D], F32, "sc", st)
            scT = s.sb([128, KC, 2], F32, "scT", st)
            wr = Ring([s.sb([128, KC, 512], F32, "adw", st) for _ in range(2)])
            mbr = Ring([s.sb([2, 512], F32, "mb", st) for _ in range(2)])
            orr = Ring([s.sb([2, 512], F32, "ao", st) for _ in range(2)])
            s.dma("sp", c2t[:, :], s.I["c2"][:, :])
            s.act(sc[:, :], c2t[:, :], AF.Silu)
            p = s.psf.next()
            for kc in range(KC):
                s.tr(p[:, kc * 2:(kc + 1) * 2], sc[0:2, kc * 128:(kc + 1) * 128], s.identf[0:2, 0:2])
            s.copy("dve", scT[:, :, :], View(p, p.base[:, 0:32].rearrange("p (k r) -> p k r", r=2)))
            for li in s.layers:
                wv = s.I["mod_w"].base[li].rearrange("(kc p) n -> p kc n", p=128)
                for c0 in range(0, 6 * D, 512):
                    wt = wr.next()
                    s.dma("sp", wt[:, :, :], View(s.I["mod_w"], wv[:, :, c0:c0 + 512]))
                    mb = mbr.next()
                    s.load_bc(mb[:, :], s.I["mod_b"], s.I["mod_b"].base[li, c0:c0 + 512], np_=2)
                    p = s.psf.next()
                    for kc in range(KC):
                        s.mm(p[0:2, :], scT[:, kc, :], wt[:, kc, :], kc == 0, kc == KC - 1)
                    o = orr.next()
                    s.tt("dve", o[:, :], p[0:2, :], mb[:, :], ALU.add)
                    s.dma("pool", s.adav[li][:, c0:c0 + 512], o[:, :])
        s.barrier()

    def mod_vecs(s, li, r, which, gm, sh, gt, tmp):
        a = s.adav[li]
        o = 0 if which == 1 else 3
        ng = s.I["norm1_g" if which == 1 else "norm2_g"]
        s.load_bc(sh[:, :], a, a.base[r, (o + 0) * D:(o + 1) * D])
        s.load_bc(tmp[:, :], a, a.base[r, (o + 1) * D:(o + 2) * D])
        s.load_bc(gm[:, :], ng, ng.base[li, :])
        s.stt(gm[:, :], tmp[:, :], 1.0, gm[:, :], ALU.add, ALU.mult)
        if gt is not None:
            s.load_bc(gt[:, :], a, a.base[r, (o + 2) * D:(o + 3) * D])

    def norm_tile(s, xt, gm, sh, h32, hb, junk):
        ss = s.small.next()
        s.act(junk[:, :], xt, AF.Square, accum=ss[:, 0:1])
        s.ts("dve", ss[:, 1:2], ss[:, 0:1], 1.0 / D, EPS, ALU.mult, ALU.add)
        s.act(ss[:, 2:3], ss[:, 1:2], AF.Sqrt)
        s.recip(ss[:, 3:4], ss[:, 2:3])
        s.stt(h32[:, :], xt, ss[:, 3:4], gm, ALU.mult, ALU.mult)
        s.tt("pool", hb[:, :], h32[:, :], sh, ALU.add)

    def transpose_tile(s, src, ncol, dst, kc0, tok0):
        nb = ncol // 128
        for b0 in range(0, nb, 8):
            n = min(8, nb - b0)
            p = s.psb.next()
            for j in range(n):
                s.tr(p[:, j * 128:(j + 1) * 128], src[:, (b0 + j) * 128:(b0 + j + 1) * 128], s.identb[:, :])
            s.copy(s.ev_eng(), dst[:, kc0 + b0:kc0 + b0 + n, tok0:tok0 + 128],
                   View(p, p.base[:, 0:n * 128].rearrange("p (j c) -> p j c", c=128)))

    def wview(s, nm):
        return s.wb[nm].base.rearrange("(kc p) n -> p kc n", p=128)

    def wkeys(s, k0, k1):
        return [r for r in range(0, 8192, 256) if r < k1 and r + 256 > k0]

    def gemm_tm(s, lhsT_fn, tiles, wname, K, n0, n1, evac, wring):
        kcn = K // 128
        wv = s.wview(wname)
        for c0 in range(n0, n1, 512):
            cw = min(512, n1 - c0)
            wt = wring.next()
            s.dma("sp", wt[:, 0:kcn, 0:cw], View(s.wb[wname], wv[:, :, c0:c0 + cw], s.wkeys(0, K)))
            for t in tiles:
                p = s.psf.next()
                for kc in range(kcn):
                    s.mm(p[:, 0:cw], lhsT_fn(t, kc), wt[:, kc, 0:cw], kc == 0, kc == kcn - 1)
                evac(t, c0, cw, p)

    def gemm_acc(s, lhsT_fn, tiles, wname, K, evac, wring):
        kcn = K // 128
        wv = s.wview(wname)
        assert len(tiles) <= 6
        for c0 in range(0, D, 512):
            banks = [s.psf.next() for _ in tiles]
            for hb in range(0, kcn, 16):
                wt = wring.next()
                s.dma("sp", wt[:, 0:16, :], View(s.wb[wname], wv[:, hb:hb + 16, c0:c0 + 512],
                                                  s.wkeys(hb * 128, (hb + 16) * 128)))
                for i, t in enumerate(tiles):
                    for kc in range(16):
                        s.mm(banks[i][:, :], lhsT_fn(i, hb + kc), wt[:, kc, :], hb + kc == 0, hb + kc == kcn - 1)
            for i, t in enumerate(tiles):
                evac(t, c0, banks[i])

    def resid_evac_fn(s, gt, xr, tr_):
        def evac(t, c0, p):
            xt = xr.next()
            s.dma("sp", xt[:, :], s.xsrc[t * 128:(t + 1) * 128, c0:c0 + 512].k((t, c0)))
            s.tt("dve", p[:, :], p[:, :], gt[:, c0:c0 + 512], ALU.mult)
            s.tt("dve", xt[:, :], xt[:, :], p[:, :], ALU.add)
            s.dma("pool", s.xres[t * 128:(t + 1) * 128, c0:c0 + 512].k((t, c0)), xt[:, :])
        return evac

    def xkeys(s, t):
        return [(t, c) for c in range(0, D, 512)]

    def outproj_phase(s, li, wname, K, do_ctx):
        kcn = K // 128
        with contextlib.ExitStack() as st:
            oT = s.sb([128, kcn, 6 * 128], BF16, "oT", st)
            gt = s.sb([128, D], F32, "g1", st)
            otr = Ring([s.sb([128, K], BF16, "ot", st) for _ in range(2)])
            wring = Ring([s.sb([128, 16, 512], BF16, "wo", st) for _ in range(2)])
            xr = Ring([s.sb([128, 512], F32, "xr", st) for _ in range(3)])
            tr_ = None
            a = s.adav[li]
            cur_r = None
            for g in GROUPS:
                r = 1 if g[0] < NCTX else 0
                if r == 1 and not do_ctx:
                    continue
                if r != cur_r:
                    s.load_bc(gt[:, :], a, a.base[r, 2 * D:3 * D])
                    cur_r = r
                for i, t in enumerate(g):
                    ot = otr.next()
                    s.dma("sp", ot[:, :], s.omix[t * 128:(t + 1) * 128, 0:K].k(t))
                    s.transpose_tile(ot, K, oT, 0, i * 128)
                s.gemm_acc(lambda i, kc: oT[:, kc, i * 128:(i + 1) * 128], g, wname, K,
                           s.resid_evac_fn(gt, xr, tr_), wring)
        s.xsrc = s.xres if do_ctx or True else s.xsrc
        s.barrier()

    def mlp_phase(s, li, do_ctx):
        with contextlib.ExitStack() as st:
            hT = s.sb([128, KC, 6 * 128], BF16, "hT2", st)
            upT = s.sb([128, 64, 6 * 128], BF16, "upT", st)
            gm = s.sb([128, D], F32, "gm2", st)
            sh = s.sb([128, D], F32, "sh2", st)
            gt = s.sb([128, D], F32, "g2", st)
            xtr = Ring([s.sb([128, D], F32, "xt", st) for _ in range(2)])
            hbr = Ring([s.sb([128, D], BF16, "hb", st) for _ in range(1)])
            wring = Ring([s.sb([128, 16, 512], BF16, "wm", st) for _ in range(2)])
            xr = Ring([s.sb([128, 512], F32, "xr", st) for _ in range(3)])
            tr_ = None
            r32 = Ring([s.sb([128, 512], F32, "r32", st) for _ in range(2)])
            cur_r = None
            wv = s.wview("up%d" % li)
            for g in GROUPS:
                r = 1 if g[0] < NCTX else 0
                if r == 1 and not do_ctx:
                    continue
                if r != cur_r:
                    s.mod_vecs(li, r, 2, gm, sh, gt, xtr.next())
                    cur_r = r
                ntok = len(g) * 128
                for i, t in enumerate(g):
                    xt = xtr.next()
                    s.dma("sp", xt[:, :], s.xsrc[t * 128:(t + 1) * 128, :].k(s.xkeys(t)))
                    hb = hbr.next()
                    s.norm_tile(xt[:, :], gm[:, :], sh[:, :], xt, hb, hb)
                    s.transpose_tile(hb, D, hT, 0, i * 128)
                for c0 in range(0, HID, 512):
                    wt = wring.next()
                    s.dma("sp", wt[:, :, :], View(s.wb["up%d" % li], wv[:, :, c0:c0 + 512], s.wkeys(0, D)))
                    for j in range(4):
                        hc = c0 // 128 + j
                        for tb in range(0, ntok, 512):
                            tw = min(512, ntok - tb)
                            p = s.psf.next()
                            for kc in range(KC):
                                s.mm(p[:, 0:tw], wt[:, kc, j * 128:(j + 1) * 128], hT[:, kc, tb:tb + tw],
                                     kc == 0, kc == KC - 1)
                            rr = r32.next()
                            s.act(rr[:, 0:tw], p[:, 0:tw], AF.Relu)
                            s.tt("pool" if (hc + tb // 512) % 2 else "dve", upT[:, hc, tb:tb + tw],
                                 rr[:, 0:tw], rr[:, 0:tw], ALU.mult)
                s.gemm_acc(lambda i, kc: upT[:, kc, i * 128:(i + 1) * 128], g, "down%d" % li, HID,
                           s.resid_evac_fn(gt, xr, tr_), wring)
        s.barrier()

    def n1_phase(s, li, tiles, hT, st):
        with contextlib.ExitStack() as st2:
            gm = s.sb([128, D], F32, "gm1", st2)
            sh = s.sb([128, D], F32, "sh1", st2)
            xtr = Ring([s.sb([128, D], F32, "xt", st2) for _ in range(3)])
            h32r = Ring([s.sb([128, D], F32, "h32", st2) for _ in range(2)])
            hbr = Ring([s.sb([128, D], BF16, "hb", st2) for _ in range(3)])
            cur_r = None
            for t in tiles:
                r = 1 if t < NCTX else 0
                if r != cur_r:
                    s.mod_vecs(li, r, 1, gm, sh, None, h32r.next())
                    cur_r = r
                xt = xtr.next()
                s.dma("sp", xt[:, :], s.xsrc[t * 128:(t + 1) * 128, :].k(s.xkeys(t)))
                hb = hbr.next()
                s.norm_tile(xt[:, :], gm[:, :], sh[:, :], h32r.next(), hb, hb)
                s.transpose_tile(hb, D, hT, 0, t * 128)
            s.barrier()

    def final_phase(s):
        with contextlib.ExitStack() as st:
            gm = s.sb([128, D], F32, "fg", st)
            xtr = Ring([s.sb([128, D], F32, "xt", st) for _ in range(2)])
            otr = Ring([s.sb([128, D], F32, "ot", st) for _ in range(2)])
            junk = s.sb([128, D], BF16, "junk", st)
            s.load_bc(gm[:, :], s.I["final_g"], s.I["final_g"].base[:])
            off = NCTX if os.environ.get("DBGCTX") != "1" else 0
            for t in range(off, off + 16):
                xt = xtr.next()
                s.dma("sp", xt[:, :], s.xsrc[t * 128:(t + 1) * 128, :].k(s.xkeys(t)))
                ss = s.small.next()
                s.act(junk[:, :], xt[:, :], AF.Square, accum=ss[:, 0:1])
                s.ts("dve", ss[:, 1:2], ss[:, 0:1], 1.0 / D, EPS, ALU.mult, ALU.add)
                s.act(ss[:, 2:3], ss[:, 1:2], AF.Sqrt)
                s.recip(ss[:, 3:4], ss[:, 2:3])
                ot = otr.next()
                s.stt(ot[:, :], xt[:, :], ss[:, 3:4], gm[:, :], ALU.mult, ALU.mult)
                s.dma("pool", s.out[(t - off) * 128:(t - off + 1) * 128, :], ot[:, :])
        s.barrier()

    def sgu_mixer(s, li):
        tiles = list(range(NCTX, NT))
        with contextlib.ExitStack() as st:
            hT = s.sb([128, KC, NT * 128], BF16, "hT", st)
            s.n1_phase(li, tiles, hT, st)
            with contextlib.ExitStack() as st2:
                wring = Ring([s.sb([128, 16, 512], BF16, "wi", st2) for _ in range(2)])
                zr = Ring([s.sb([128, 512], BF16, "z", st2) for _ in range(4)])

                def evac(t, c0, cw, p):
                    z = zr.next()
                    s.act(z[:, 0:cw], p[:, 0:cw], AF.Gelu)
                    s.dma("pool", s.proj[t * 128:(t + 1) * 128, c0:c0 + cw].k((t, c0)), z[:, 0:cw])
                s.gemm_tm(lambda t, kc: hT[:, kc, t * 128:(t + 1) * 128], tiles, "sgu_w_in", D, 0, 8192, evac, wring)
            s.barrier()
        with contextlib.ExitStack() as st:
            lg = s.sb([128, 4096], F32, "lng", st)
            lb = s.sb([128, 4096], F32, "lnb", st)
            s.load_bc(lg[:, :], s.I["sgu_ln_g"], s.I["sgu_ln_g"].base[0, :])
            s.load_bc(lb[:, :], s.I["sgu_ln_b"], s.I["sgu_ln_b"].base[0, :])
            wsT = s.sb([128, 8, 128], BF16, "wsT", st)
            bs = s.sb([128, 8], F32, "bs", st)
            wsf = s.sb([128, 8, 128], F32, "wsf", st)
            wsb = s.sb([128, 8, 128], BF16, "wsb", st)
            bsr = s.sb([8, 128], F32, "bsr", st)
            s.dma("sp", wsf[:, :, :], View(s.I["sgu_w_s"], s.I["sgu_w_s"].base[0].rearrange("g p q -> p g q")))
            s.copy("dve", wsb[:, :, :], wsf[:, :, :])
            for g0 in range(0, 8, 4):
                p = s.psb.next()
                for j in range(4):
                    s.tr(p[:, j * 128:(j + 1) * 128], wsb[:, g0 + j, :], s.identb[:, :])
                s.copy("dve", wsT[:, g0:g0 + 4, :], View(p, p.base[:, 0:512].rearrange("p (j c) -> p j c", c=128)))
            s.dma("sp", bsr[:, :], s.I["sgu_b_s"][0, :, :])
            p = s.psf.next()
            s.tr(p[:, 0:8], bsr[0:8, :], s.identf[0:8, 0:8])
            s.copy("dve", bs[:, :], p[:, 0:8])
            zt = Ring([s.sb([128, 8192], BF16, "zt", st) for _ in range(2)])
            vn32 = s.sb([128, 4096], F32, "vn32", st)
            vnb = s.sb([128, 4096], BF16, "vnb", st)
            junk = s.sb([128, 4096], BF16, "junk", st)
            gor = Ring([s.sb([128, 4096], BF16, "go", st) for _ in range(2)])
            for t in tiles:
                z = zt.next()
                s.dma("sp", z[:, :], s.proj[t * 128:(t + 1) * 128, 0:8192].k([(t, c) for c in range(0, 8192, 512)]))
                ss = s.small.next()
                v = z[:, 4096:8192]
                s.act(junk[:, :], v, AF.Copy, accum=ss[:, 0:1])
                s.act(junk[:, :], v, AF.Square, accum=ss[:, 1:2])
                s.ts("dve", ss[:, 2:3], ss[:, 0:1], 1.0 / 4096, None, ALU.mult)
                s.tt("dve", ss[:, 3:4], ss[:, 2:3], ss[:, 2:3], ALU.mult)
                s.stt(ss[:, 4:5], ss[:, 1:2], 1.0 / 4096, ss[:, 3:4], ALU.mult, ALU.subtract)
                s.ts("dve", ss[:, 5:6], ss[:, 4:5], EPS, None, ALU.add)
                s.act(ss[:, 6:7], ss[:, 5:6], AF.Sqrt)
                s.recip(ss[:, 7:8], ss[:, 6:7])
                s.ts("dve", vn32[:, :], v, ss[:, 2:3], ss[:, 7:8], ALU.subtract, ALU.mult)
                s.tt("pool", vn32[:, :], vn32[:, :], lg[:, :], ALU.mult)
                s.tt("dve", vnb[:, :], vn32[:, :], lb[:, :], ALU.add)
                go = gor.next()
                for g in range(8):
                    p = s.psf.next()
                    s.mm(p[:, :], wsT[:, g, :], vnb[:, g * 512:(g + 1) * 512], True, True)
                    s.stt(go[:, g * 512:(g + 1) * 512], p[:, :], bs[:, g:g + 1], z[:, g * 512:(g + 1) * 512],
                          ALU.add, ALU.mult)
                s.dma("pool", s.omix[t * 128:(t + 1) * 128, :].k(t), go[:, :])
        s.barrier()
        s.outproj_phase(li, "sgu_w_out", 4096, False)


    def rope_evac(s, p, cw, t, scale, rope, cosT, sinT, xs, tmpr, ob):
        if not rope:
            s.act(ob[:, 0:cw], p[:, 0:cw], AF.Copy, scale=scale)
            return
        s.act(xs[:, 0:cw], p[:, 0:cw], AF.Copy, scale=scale)
        h = cw // 2
        x1 = View(xs, xs.base[:, 0:cw].rearrange("p (n two) -> p n two", two=2)[:, :, 0])
        x2 = View(xs, xs.base[:, 0:cw].rearrange("p (n two) -> p n two", two=2)[:, :, 1])
        y1 = View(ob, ob.base[:, 0:cw].rearrange("p (n two) -> p n two", two=2)[:, :, 0])
        y2 = View(ob, ob.base[:, 0:cw].rearrange("p (n two) -> p n two", two=2)[:, :, 1])
        c = cosT[:, t - NCTX, 0:h]
        sn = sinT[:, t - NCTX, 0:h]
        t1, t2, t3, t4 = [tmpr.next() for _ in range(4)]
        s.tt("dve", t1[:, 0:h], x1, c, ALU.mult)
        s.tt("pool", t2[:, 0:h], x2, sn, ALU.mult)
        s.tt("dve", y1, t1[:, 0:h], t2[:, 0:h], ALU.subtract)
        s.tt("pool", t3[:, 0:h], x1, sn, ALU.mult)
        s.tt("dve", t4[:, 0:h], x2, c, ALU.mult)
        s.tt("pool", y2, t3[:, 0:h], t4[:, 0:h], ALU.add)

    def att_mixer(s, li):
        tiles = list(range(NT))
        with contextlib.ExitStack() as st:
            hT = s.sb([128, KC, NT * 128], BF16, "hT", st)
            s.n1_phase(li, tiles, hT, st)
            with contextlib.ExitStack() as st2:
                wring = Ring([s.sb([128, 16, 512], BF16, "wi", st2) for _ in range(2)])
                cosT = s.sb([128, 16, 256], F32, "cos", st2)
                sinT = s.sb([128, 16, 256], F32, "sin", st2)
                s.dma("sp", cosT[:, :, :], View(s.I["cos_a"], s.I["cos_a"].base.rearrange("t p c -> p t c")))
                s.dma("sp", sinT[:, :, :], View(s.I["sin_a"], s.I["sin_a"].base.rearrange("t p c -> p t c")))
                xsr = Ring([s.sb([128, 512], F32, "xs", st2) for _ in range(2)])
                tmpr = Ring([s.sb([128, 256], F32, "rt", st2) for _ in range(8)])
                obr = Ring([s.sb([128, 512], BF16, "ob", st2) for _ in range(4)])

                def evac(t, c0, cw, p):
                    ob = obr.next()
                    isq = c0 < 2048
                    isv = c0 >= 2560
                    s.rope_evac(p, cw, t, (128 ** -0.5) if isq else 1.0, (not isv) and t >= NCTX,
                                cosT, sinT, xsr.next(), tmpr, ob)
                    s.dma("pool", s.proj[t * 128:(t + 1) * 128, c0:c0 + cw].k((t, c0)), ob[:, 0:cw])
                s.gemm_tm(lambda t, kc: hT[:, kc, t * 128:(t + 1) * 128], tiles, "att_w_in", D, 0, 3072, evac, wring)
            s.barrier()
        pv = s.proj.base.rearrange("(t p) c -> p t c", p=128)
        ov = s.omix.base.rearrange("(t p) c -> p t c", p=128)
        allk = lambda c0: [(t, c0) for t in range(NT)]
        with contextlib.ExitStack() as st:
            mask3 = s.sb([128, 384], F32, "mask3", st)
            s.dma("sp", mask3[:, :], s.I["mask3"][:, :])
            sinkb = s.sb([128, 16], F32, "sinkb", st)
            s.load_bc(sinkb[:, :], s.I["att_sink"], s.I["att_sink"].base[0, :])
            ktm = s.sb([128, NT, 128], BF16, "ktm", st)
            kT = s.sb([128, 1, NT * 128], BF16, "kT", st)
            vtm = Ring([s.sb([128, NT, 128], BF16, "vtm", st) for _ in range(2)])
            qtmr = Ring([s.sb([128, NT, 128], BF16, "qtm", st) for _ in range(2)])
            qTr = Ring([s.sb([128, 1, NT * 128], BF16, "qT", st) for _ in range(2)])
            ohr = Ring([s.sb([128, NT, 128], BF16, "oh", st) for _ in range(2)])
            ssb = Ring([s.sb([128, 640], F32, "ssb", st) for _ in range(3)])
            pbr = Ring([s.sb([128, 640], BF16, "pb", st) for _ in range(3)])
            ptr = Ring([s.sb([128, 5, 128], BF16, "pt", st) for _ in range(3)])
            for g in range(4):
                s.dma("sp", ktm[:, :, :], View(s.proj, pv[:, :, 2048 + g * 128:2048 + (g + 1) * 128], allk(2048)))
                vt = vtm.next()
                s.dma("sp", vt[:, :, :], View(s.proj, pv[:, :, 2560 + g * 128:2560 + (g + 1) * 128], allk(2560)))
                for t in range(NT):
                    s.transpose_tile(_Sub(ktm, t), 128, kT, 0, t * 128)
                for hh in range(4):
                    h = g * 4 + hh
                    qtm = qtmr.next()
                    s.dma("sp", qtm[:, :, :], View(s.proj, pv[:, :, h * 128:(h + 1) * 128], allk((h // 4) * 512)))
                    qT = qTr.next()
                    for t in range(NT):
                        s.transpose_tile(_Sub(qtm, t), 128, qT, 0, t * 128)
                    oh = ohr.next()
                    for t in range(NT):
                        qv = qT[:, 0, t * 128:(t + 1) * 128]
                        sS = ssb.next()
                        if t < NCTX:
                            pa = s.psf.next()
                            s.mm(pa[:, 0:256], qv, kT[:, 0, 0:256], True, True)
                            s.copy("act", sS[:, 0:256], pa[:, 0:256])
                            n = 256
                            ktiles = [0, 1]
                        else:
                            lo, hi = max(NCTX, t - 1), min(NT - 1, t + 1)
                            nl = hi - lo + 1
                            pa = s.psf.next()
                            s.mm(pa[:, 0:256], qv, kT[:, 0, 0:256], True, True)
                            pb_ = s.psf.next()
                            s.mm(pb_[:, 0:nl * 128], qv, kT[:, 0, lo * 128:(hi + 1) * 128], True, True)
                            s.copy("act", sS[:, 0:256], pa[:, 0:256])
                            m0 = (lo - (t - 1)) * 128
                            s.tt("dve", sS[:, 256:256 + nl * 128], pb_[:, 0:nl * 128], mask3[:, m0:m0 + nl * 128], ALU.add)
                            n = 256 + nl * 128
                            ktiles = [0, 1] + list(range(lo, hi + 1))
                        sm = s.small.next()
                        s.rmax(sm[:, 0:1], sS[:, 0:n])
                        s.ts("dve", sm[:, 1:2], sm[:, 0:1], sinkb[:, h:h + 1], -1.0, ALU.max, ALU.mult)
                        pb = pbr.next()
                        s.act(pb[:, 0:n], sS[:, 0:n], AF.Exp, bias=sm[:, 1:2], accum=sm[:, 2:3])
                        s.act(sm[:, 3:4], sinkb[:, h:h + 1], AF.Exp, bias=sm[:, 1:2])
                        s.tt("dve", sm[:, 4:5], sm[:, 2:3], sm[:, 3:4], ALU.add)
                        s.recip(sm[:, 5:6], sm[:, 4:5])
                        nk = n // 128
                        pT = ptr.next()
                        pp = s.psb.next()
                        for j in range(nk):
                            s.tr(pp[:, j * 128:(j + 1) * 128], pb[:, j * 128:(j + 1) * 128], s.identb[:, :])
                        s.copy("act", pT[:, 0:nk, :], View(pp, pp.base[:, 0:nk * 128].rearrange("p (j c) -> p j c", c=128)))
                        po = s.psf.next()
                        for j in range(nk):
                            s.mm(po[:, 0:128], pT[:, j, :], vt[:, ktiles[j], :], j == 0, j == nk - 1)
                        s.ts("dve", oh[:, t, :], po[:, 0:128], sm[:, 5:6], None, ALU.mult)
                    s.dma("pool", View(s.omix, ov[:, :, h * 128:(h + 1) * 128], list(range(NT))), oh[:, :, :])
        s.barrier()
        s.outproj_phase(li, "att_w_out", 2048, True)


    def ret_mixer(s, li):
        tiles = list(range(NT))
        with contextlib.ExitStack() as st:
            hT = s.sb([128, KC, NT * 128], BF16, "hT", st)
            s.n1_phase(li, tiles, hT, st)
            with contextlib.ExitStack() as st2:
                wring = Ring([s.sb([128, 16, 512], BF16, "wi", st2) for _ in range(2)])
                cosT = s.sb([128, 16, 256], F32, "cos", st2)
                sinT = s.sb([128, 16, 256], F32, "sin", st2)
                s.dma("sp", cosT[:, :, :], View(s.I["cos_r"], s.I["cos_r"].base.rearrange("t p c -> p t c")))
                s.dma("sp", sinT[:, :, :], View(s.I["sin_r"], s.I["sin_r"].base.rearrange("t p c -> p t c")))
                xsr = Ring([s.sb([128, 512], F32, "xs", st2) for _ in range(2)])
                tmpr = Ring([s.sb([128, 256], F32, "rt", st2) for _ in range(8)])
                obr = Ring([s.sb([128, 512], BF16, "ob", st2) for _ in range(4)])

                def evac(t, c0, cw, p):
                    ob = obr.next()
                    if c0 >= 8192:
                        s.act(ob[:, 0:cw], p[:, 0:cw], AF.Silu)
                    else:
                        isk = 2048 <= c0 < 4096
                        s.rope_evac(p, cw, t, (256 ** -0.5) if isk else 1.0, c0 < 4096 and t >= NCTX,
                                    cosT, sinT, xsr.next(), tmpr, ob)
                    s.dma("pool", s.proj[t * 128:(t + 1) * 128, c0:c0 + cw].k((t, c0)), ob[:, 0:cw])
                s.gemm_tm(lambda t, kc: hT[:, kc, t * 128:(t + 1) * 128], tiles, "ret_w_in", D, 0, 12288, evac, wring)
            s.barrier()
        pv = s.proj.base.rearrange("(t p) c -> p t c", p=128)
        ov = s.omix.base.rearrange("(t p) c -> p t c", p=128)
        allk = lambda c0: [(t, (c0 // 512) * 512) for t in range(NT)]
        lg = np.log1p(-np.exp2(-5.0 - np.arange(8, dtype=np.float32))).astype(np.float32)
        with contextlib.ExitStack() as st:
            qtm = s.sb([128, NT, 256], BF16, "qtm", st)
            ktm = s.sb([128, NT, 256], BF16, "ktm", st)
            vtm = s.sb([128, NT, 512], BF16, "vtm", st)
            gtm = s.sb([128, NT, 512], BF16, "gtm", st)
            oh = s.sb([128, NT, 512], BF16, "oh", st)
            qT = s.sb([128, 2, NT * 128], BF16, "qT", st)
            kT = s.sb([128, 2, NT * 128], BF16, "kT", st)
            sbst = s.sb([128, NT, 2, 512], BF16, "sbst", st)
            S32 = s.sb([128, 2, 512], F32, "S32", st)
            Sfb = Ring([s.sb([128, 2, 512], BF16, "Sfb", st) for _ in range(2)])
            Dm = s.sb([128, 128], F32, "Dm", st)
            qdec = s.sb([128, 2, 128], F32, "qdec", st)
            kdec = s.sb([128, 16], F32, "kdec", st)
            gng = s.sb([128, 512], F32, "gng", st)
            s.dma("sp", kdec[:, :], s.I["ret_kdec"][:, :])
            ksr = Ring([s.sb([128, 256], BF16, "ks", st) for _ in range(3)])
            ptr = Ring([s.sb([128, 128], BF16, "PT", st) for _ in range(2)])
            qfr = Ring([s.sb([128, 2, 128], BF16, "qf", st) for _ in range(4)])
            o32 = Ring([s.sb([128, 512], F32, "o32", st) for _ in range(2)])
            junk = s.sb([128, 512], BF16, "junk", st)
            for h in range(8):
                cd = float(np.exp(lg[h] * np.float32(128.0)))
                s.dma("sp", qtm[:, :, :], View(s.proj, pv[:, :, h * 256:(h + 1) * 256], allk(h * 256)))
                s.dma("sp", ktm[:, :, :], View(s.proj, pv[:, :, 2048 + h * 256:2048 + (h + 1) * 256], allk(2048 + h * 256)))
                s.dma("sp", vtm[:, :, :], View(s.proj, pv[:, :, 4096 + h * 512:4096 + (h + 1) * 512], allk(4096 + h * 512)))
                s.dma("sp", gtm[:, :, :], View(s.proj, pv[:, :, 8192 + h * 512:8192 + (h + 1) * 512], allk(8192 + h * 512)))
                s.dma("sp", Dm[:, :], s.I["ret_D"][h, :, :])
                s.load_bc(qdec[:, 0, :], s.I["ret_qdec"], s.I["ret_qdec"].base[h, 0, :])
                s.load_bc(qdec[:, 1, :], s.I["ret_qdec"], s.I["ret_qdec"].base[h, 1, :])
                s.load_bc(gng[:, :], s.I["ret_gn_g"], s.I["ret_gn_g"].base[0, h * 512:(h + 1) * 512])
                for t in range(NT):
                    s.transpose_tile(_Sub(qtm, t), 256, qT, 0, t * 128)
                    s.transpose_tile(_Sub(ktm, t), 256, kT, 0, t * 128)

                def upd(t, col):
                    ks = ksr.next()
                    s.act(ks[:, :], ktm[:, t, :], AF.Copy, scale=kdec[:, col:col + 1])
                    for dc in range(2):
                        p = s.psf.next()
                        s.mm(p[:, :], ks[:, dc * 128:(dc + 1) * 128], vtm[:, t, :], True, True)
                        s.stt(S32[:, dc, :], S32[:, dc, :], cd, p[:, :], ALU.mult, ALU.add)
                s.memset("dve", S32[:, :, :], 0.0)
                for t in [1, 0] + list(range(NT - 1, NCTX - 1, -1)):
                    s.copy("act", sbst[:, t, :, :], S32[:, :, :])
                    upd(t, h * 2 + 1)
                s.memset("dve", S32[:, :, :], 0.0)
                for t in range(NT):
                    sf = Sfb.next()
                    s.copy("act", sf[:, :, :], S32[:, :, :])
                    tk = slice(t * 128, (t + 1) * 128)
                    p1 = s.psf.next()
                    for dc in range(2):
                        s.mm(p1[:, 0:128], kT[:, dc, tk], qT[:, dc, tk], dc == 0, dc == 1)
                    PT = ptr.next()
                    s.tt("dve", PT[:, :], p1[:, 0:128], Dm[:, :], ALU.mult)
                    qf = qfr.next()
                    qb = qfr.next()
                    for dc in range(2):
                        s.tt("pool", qf[:, dc, :], qT[:, dc, tk], qdec[:, 0, :], ALU.mult)
                        s.tt("dve", qb[:, dc, :], qT[:, dc, tk], qdec[:, 1, :], ALU.mult)
                    po = s.psf.next()
                    s.mm(po[:, :], PT[:, :], vtm[:, t, :], True, False)
                    for dc in range(2):
                        s.mm(po[:, :], qf[:, dc, :], sf[:, dc, :], False, False)
                    for dc in range(2):
                        s.mm(po[:, :], qb[:, dc, :], sbst[:, t, dc, :], False, dc == 1)
                    sm = s.small.next()
                    s.act(junk[:, :], po[:, :], AF.Square, accum=sm[:, 0:1])
                    s.ts("dve", sm[:, 1:2], sm[:, 0:1], 1.0 / 512, EPS, ALU.mult, ALU.add)
                    s.act(sm[:, 2:3], sm[:, 1:2], AF.Sqrt)
                    s.recip(sm[:, 3:4], sm[:, 2:3])
                    o = o32.next()
                    s.stt(o[:, :], po[:, :], sm[:, 3:4], gng[:, :], ALU.mult, ALU.mult)
                    s.tt("pool", oh[:, t, :], o[:, :], gtm[:, t, :], ALU.mult)
                    upd(t, h * 2 + 0)
                s.dma("pool", View(s.omix, ov[:, :, h * 512:(h + 1) * 512], list(range(NT))), oh[:, :, :])
        s.barrier()
        s.outproj_phase(li, "ret_w_out", 4096, True)


    def gdn_mixer(s, li):
        tiles = list(range(NT))
        XW = 2310
        mixT = s.mixT
        with contextlib.ExitStack() as st:
            hT = s.sb([128, KC, NT * 128], BF16, "hT", st)
            s.n1_phase(li, tiles, hT, st)
            with contextlib.ExitStack() as st2:
                wring = Ring([s.sb([128, 16, 512], BF16, "wi", st2) for _ in range(2)])
                X = s.sb([128, XW], F32, "X", st2)
                Y = s.sb([128, NT * 128], F32, "Y", st2)
                Z = s.sb([128, NT * 128], F32, "Z", st2)
                SQ = s.sb([128, NT * 128], F32, "SQ", st2)
                OB = Ring([s.sb([128, NT * 128], BF16, "OB", st2) for _ in range(2)])
                rn = Ring([s.sb([128, 512], F32, "rn", st2) for _ in range(2)])
                onesf = s.sb([128, 128], F32, "onesf", st2)
                s.memset("dve", onesf[:, :], 1.0)
                s.memset("dve", X[:, :], 0.0)
                cw4 = s.sb([4, 8192], F32, "cw4", st2)
                cwT = s.sb([128, 64, 4], F32, "cwT", st2)
                s.dma("sp", cw4[:, :], s.I["gdn_conv_w"][0, :, :])
                p = s.psf.next()
                for j in range(64):
                    s.tr(p[:, j * 4:(j + 1) * 4], cw4[0:4, j * 128:(j + 1) * 128], s.identf[0:4, 0:4])
                s.copy("dve", cwT[:, :, :], View(p, p.base[:, 0:256].rearrange("p (j k) -> p j k", k=4)))
                wv = s.wview("gdn_w_in")
                blocks = [(0, 256)] + [(256 + i * 512, 512) for i in range(4)]
                xcol = lambda tok: tok + 2 if tok < 256 else tok + 5
                for c0 in range(0, 8192, 512):
                    wt = wring.next()
                    s.dma("sp", wt[:, :, :], View(s.wb["gdn_w_in"], wv[:, :, c0:c0 + 512], s.wkeys(0, D)))
                    for j in range(4):
                        ch = c0 // 128 + j
                        for tb, tw in blocks:
                            p = s.psf.next()
                            for kc in range(KC):
                                s.mm(p[:, 0:tw], wt[:, kc, j * 128:(j + 1) * 128], hT[:, kc, tb:tb + tw], kc == 0, kc == KC - 1)
                            s.copy("act", X[:, xcol(tb):xcol(tb) + tw], p[:, 0:tw])
                        for (t0, n) in [(0, 256), (256, 2048)]:
                            x0 = xcol(t0)
                            s.ts("dve", Y[:, t0:t0 + n], X[:, x0 - 2:x0 - 2 + n], cwT[:, ch, 0:1], None, ALU.mult)
                            for k in range(1, 4):
                                s.stt(Y[:, t0:t0 + n], X[:, x0 - 2 + k:x0 - 2 + k + n], cwT[:, ch, k:k + 1], Y[:, t0:t0 + n],
                                      ALU.mult, ALU.add)
                        ob = OB.next()
                        if ch >= 32:
                            s.act(ob[:, :], Y[:, :], AF.Silu)
                        else:
                            s.act(Z[:, :], Y[:, :], AF.Silu)
                            s.tt("pool", SQ[:, :], Z[:, :], Z[:, :], ALU.mult)
                            for tb, tw in blocks:
                                p = s.psf.next()
                                s.mm(p[:, 0:tw], onesf[:, :], SQ[:, tb:tb + tw], True, True)
                                r = rn.next()
                                s.ts("dve", r[:, 0:tw], p[:, 0:tw], 1e-6, None, ALU.add)
                                s.act(r[:, 0:tw], r[:, 0:tw], AF.Sqrt)
                                s.recip(r[:, 0:tw], r[:, 0:tw])
                                s.stt(ob[:, tb:tb + tw], Z[:, tb:tb + tw], (128 ** -0.5) if ch < 16 else 1.0, r[:, 0:tw],
                                      ALU.mult, ALU.mult)
                        s.dma("pool", mixT[ch, :, :].k(ch), ob[:, :])
                zr = Ring([s.sb([128, 512], BF16, "z", st2) for _ in range(3)])
                fr = Ring([s.sb([128, 128], F32, "f", st2) for _ in range(2)])

                def evac(t, c0, cw, p):
                    if c0 < 12288:
                        z = zr.next()
                        s.act(z[:, 0:cw], p[:, 0:cw], AF.Silu)
                        s.dma("pool", s.proj[t * 128:(t + 1) * 128, c0 - 8192:c0 - 8192 + cw].k((t, c0 - 8192)), z[:, 0:cw])
                    else:
                        f = fr.next()
                        s.copy("dve", f[:, 0:cw], p[:, 0:cw])
                        s.dma("pool", s.projf[t * 128:(t + 1) * 128, 0:cw].k(t), f[:, 0:cw])
                s.gemm_tm(lambda t, kc: hT[:, kc, t * 128:(t + 1) * 128], tiles, "gdn_w_in", D, 8192, 12416, evac, wring)
            s.barrier()
        if os.environ.get("GDNDBG") == "1":
            s.outproj_phase(li, "gdn_w_out", 4096, False)
            return
        pv = s.proj.base.rearrange("(t p) c -> p t c", p=128)
        ov = s.omix.base.rearrange("(t p) c -> p t c", p=128)
        with contextlib.ExitStack() as st:
            f32t = lambda nm, shp: s.sb(shp, F32, nm, st)
            beta = f32t("beta", [128, NT, 64])
            nbeta = f32t("nbeta", [128, NT, 64])
            gc = f32t("gc", [128, NT, 64])
            eg = f32t("eg", [128, NT, 64])
            egl = f32t("egl", [128, NT, 64])
            etot = f32t("etot", [128, NT, 64])
            beg = f32t("beg", [128, NT, 64])
            onesf = f32t("onesf", [128, 128])
            masks = f32t("masks", [128, 14, 128])
            ngb = f32t("ngb", [128, 128])
            st_g = contextlib.ExitStack()
            f32g = lambda nm, shp: s.sb(shp, F32, nm, st_g)
            raw = f32g("raw", [128, NT, 128])
            g = f32g("g", [128, NT, 64])
            t1 = f32g("t1", [128, NT, 64])
            t2 = f32g("t2", [128, NT, 64])
            tot = f32g("tot", [128, NT, 64])
            alb = f32g("alb", [128, 64])
            dtb = f32g("dtb", [128, 64])
            nA = f32g("nA", [128, 64])
            s.dma("sp", raw[:, :, :], View(s.projf, s.projf.base.rearrange("(t p) c -> p t c", p=128), list(range(NT))))
            s.memset("dve", onesf[:, :], 1.0)
            s.dma("sp", masks[:, :, :], View(s.I["gdn_masks"], s.I["gdn_masks"].base.rearrange("k p c -> p k c")))
            s.load_bc(alb[:, :], s.I["gdn_a_log"], s.I["gdn_a_log"].base[0].rearrange("a b -> (a b)"))
            s.load_bc(dtb[:, :], s.I["gdn_dt_bias"], s.I["gdn_dt_bias"].base[0].rearrange("a b -> (a b)"))
            s.load_bc(ngb[:, :], s.I["gdn_norm_g"], s.I["gdn_norm_g"].base[0, :])
            s.act(nA[:, :], alb[:, :], AF.Exp)
            s.ts("dve", nA[:, :], nA[:, :], -1.0, None, ALU.mult)
            s.act(beta[:, :, :], raw[:, :, 0:64], AF.Sigmoid)
            s.ts("dve", nbeta[:, :, :], beta[:, :, :], -1.0, None, ALU.mult)
            for t in range(NT):
                s.tt("dve", t1[:, t, :], raw[:, t, 64:128], dtb[:, :], ALU.add)
            s.act(t2[:, :, :], t1[:, :, :], AF.Abs)
            s.act(t2[:, :, :], t2[:, :, :], AF.Exp, scale=-1.0)
            s.act(t2[:, :, :], t2[:, :, :], AF.Ln, bias=1.0)
            s.ts("dve", t1[:, :, :], t1[:, :, :], 0.0, None, ALU.max)
            s.tt("dve", t1[:, :, :], t1[:, :, :], t2[:, :, :], ALU.add)
            for t in range(NT):
                s.tt("dve", g[:, t, :], t1[:, t, :], nA[:, :], ALU.mult)
            MU_F, MU_B, M_SL, M_SU, M_UI, M_LI = range(6)
            for t in range(NT):
                p = s.psf.next()
                s.mm(p[:, 0:32], masks[:, MU_F, :], g[:, t, 0:32], True, True)
                s.mm(p[:, 32:64], masks[:, MU_B, :], g[:, t, 32:64], True, True)
                s.mm(p[:, 64:128], onesf[:, :], g[:, t, :], True, True)
                s.copy("act", gc[:, t, :], p[:, 0:64])
                s.copy("dve", tot[:, t, :], p[:, 64:128])
            s.act(eg[:, :, :], gc[:, :, :], AF.Exp)
            s.tt("dve", t1[:, :, :], tot[:, :, :], gc[:, :, :], ALU.subtract)
            s.act(egl[:, :, :], t1[:, :, :], AF.Exp)
            s.act(etot[:, :, :], tot[:, :, :], AF.Exp)
            s.tt("dve", beg[:, :, :], beta[:, :, :], eg[:, :, :], ALU.mult)
            s.barrier()
            st_g.close()
            kT = s.sb([128, 1, NT * 128], BF16, "kT", st)
            qT = s.sb([128, 1, NT * 128], BF16, "qT", st)
            Ktm = s.sb([128, NT, 128], BF16, "Ktm", st)
            vT = [s.sb([128, 1, NT * 128], BF16, "vT", st) for _ in range(2)]
            Vtm = [s.sb([128, NT, 128], BF16, "Vtm", st) for _ in range(2)]
            ztm = [s.sb([128, NT, 128], BF16, "ztm", st) for _ in range(2)]
            acc = [f32t("acc", [128, NT, 128]) for _ in range(2)]
            oh = [s.sb([128, NT, 128], BF16, "oh", st) for _ in range(2)]
            S32 = [[f32t("S32", [128, 128]) for _ in range(2)] for _ in range(2)]
            Sb = [[Ring([s.sb([128, 128], BF16, "Sb", st) for _ in range(2)]) for _ in range(2)] for _ in range(2)]
            shr = Ring([f32t("shr", [128, 5, 128]) for _ in range(4)])
            frs = [[Ring([f32t("fr", [128, 128]) for _ in range(6)]) for _ in range(2)] for _ in range(2)]
            rrs = [[Ring([f32t("rr", [128, 128]) for _ in range(11)]) for _ in range(2)] for _ in range(2)]
            brs = [[Ring([s.sb([128, 128], BF16, "br", st) for _ in range(26)]) for _ in range(2)] for _ in range(2)]
            junk = s.sb([128, 128], BF16, "junk", st)

            def unit(hk, e, d, t, sh, sbc, accw):
                fr = frs[e][d]
                rr = rrs[e][d]
                br = brs[e][d]
                tk = slice(t * 128, (t + 1) * 128)
                hv = hk * 2 + e
                col = d * 32 + hv
                gcc = gc[:, t, col:col + 1]
                M = fr.next()
                s.act(M[:, :], onesf[:, :], AF.Copy, scale=gcc)
                pR = s.psf.next()
                s.tr(pR[:, 0:128], M[:, :], s.identf[:, :])
                A1 = fr.next()
                s.ts("dve", A1[:, :], pR[:, 0:128], gcc, 0.0, ALU.subtract, ALU.max)
                B1 = fr.next()
                s.ts("dve", B1[:, :], pR[:, 0:128], gcc, 0.0, ALU.subtract, ALU.min)
                yield
                s.act(A1[:, :], A1[:, :], AF.Exp, scale=-1.0)
                s.act(B1[:, :], B1[:, :], AF.Exp)
                yield
                R_ = lambda v: View(v.tl, v.ap.bitcast(F32R), v.key)
                P = rr.next()
                s.stt(R_(P[:, :]), A1[:, :], nbeta[:, t, col:col + 1], sh[:, 0, :], ALU.mult, ALU.mult)
                Pp = [P]
                for j in range(1, 4):
                    pj = br.next()
                    s.stt(pj[:, :], A1[:, :], nbeta[:, t, col:col + 1], sh[:, j, :], ALU.mult, ALU.mult)
                    Pp.append(pj)
                AT = br.next()
                s.tt("pool", AT[:, :], B1[:, :], sh[:, 4, :], ALU.mult)
                yield
                pt = s.psf.next()
                s.tr(pt[:, 0:128], P[:, :], s.identf[:, :])
                PT = rr.next()
                s.copy("act", R_(PT[:, :]), pt[:, 0:128])
                Xt = rr.next()
                s.tt("dve", R_(Xt[:, :]), pt[:, 0:128], s.identf[:, :], ALU.add)
                X = rr.next()
                s.tt("dve", R_(X[:, :]), P[:, :], s.identf[:, :], ALU.add)
                pp = s.psb.next()
                for j in range(1, 4):
                    s.tr(pp[:, (j - 1) * 128:j * 128], Pp[j][:, :], s.identb[:, :])
                PTb = br.next()
                PTb2 = br.next()
                PTb3 = br.next()
                PTp = [PT, PTb, PTb2, PTb3]
                for j in range(1, 4):
                    s.copy("act", PTp[j][:, :], pp[:, (j - 1) * 128:j * 128])
                yield
                for lv in range(1, 4):
                    p1 = s.psf.next()
                    s.mm(p1[:, 0:128], R_(PT[:, :]), R_(P[:, :]), True, True)
                    s.mm(p1[:, 128:256], R_(P[:, :]), R_(PT[:, :]), True, True)
                    Pn = rr.next()
                    PTn = rr.next()
                    s.copy("act", R_(Pn[:, :]), p1[:, 0:128])
                    s.copy("act", R_(PTn[:, :]), p1[:, 128:256])
                    yield
                    p3 = s.psf.next()
                    s.mm(p3[:, 0:128], R_(PTn[:, :]), R_(X[:, :]), True, True)
                    s.mm(p3[:, 128:256], R_(Pn[:, :]), R_(Xt[:, :]), True, True)
                    last = lv == 3
                    if last:
                        Xb = br.next()
                        Xtb = br.next()
                        s.tt("dve", Xb[:, :], X[:, :], p3[:, 0:128], ALU.add)
                        s.tt("dve", Xtb[:, :], Xt[:, :], p3[:, 128:256], ALU.add)
                    else:
                        s.tt("dve", R_(X[:, :]), X[:, :], p3[:, 0:128], ALU.add)
                        s.tt("dve", R_(Xt[:, :]), Xt[:, :], p3[:, 128:256], ALU.add)
                    P, PT = Pn, PTn
                    yield
                for j in range(1, 4):
                    pa = s.psf.next()
                    A1s = br.next()
                    B1s = br.next()
                    if j < 3:
                        s.mm(pa[:, 0:128], PTp[j][:, :], Xb[:, :], True, True)
                    s.mm(pa[:, 128:256], Pp[j][:, :], Xtb[:, :], True, True)
                    if j < 3:
                        s.copy("act", A1s[:, :], pa[:, 0:128])
                    s.copy("act", B1s[:, :], pa[:, 128:256])
                    yield
                    pc = s.psf.next()
                    if j < 3:
                        s.mm(pc[:, 0:128], Xtb[:, :], A1s[:, :], True, True)
                    s.mm(pc[:, 128:256], Xb[:, :], B1s[:, :], True, True)
                    Xtn = br.next()
                    if j < 3:
                        Xn = br.next()
                        s.tt("dve", Xn[:, :], Xb[:, :], pc[:, 0:128], ALU.add)
                    s.tt("dve", Xtn[:, :], Xtb[:, :], pc[:, 128:256], ALU.add)
                    if j < 3:
                        Xb = Xn
                    Xtb = Xtn
                    yield
                TTb = Xtb
                vb = br.next()
                s.act(vb[:, :], Vtm[e][:, t, :], AF.Copy, scale=beta[:, t, col:col + 1])
                kbg = br.next()
                s.act(kbg[:, :], Ktm[:, t, :], AF.Copy, scale=beg[:, t, col:col + 1])
                kdl = br.next()
                s.ts("pool", kdl[:, :], Ktm[:, t, :], egl[:, t, col:col + 1], None, ALU.mult)
                yield
                pw = s.psf.next()
                s.mm(pw[:, 0:128], kbg[:, :], TTb[:, :], True, True)
                nw = br.next()
                s.act(nw[:, :], pw[:, 0:128], AF.Copy, scale=-1.0)
                yield
                S_ = S32[e][d]
                sb_old = sbc[(e, d)]
                pvn = s.psf.next()
                s.mm(pvn[:, 0:128], TTb[:, :], vb[:, :], True, False)
                s.mm(pvn[:, 0:128], nw[:, :], sb_old[:, :], False, True)
                if t >= NCTX:
                    s.mm(pvn[:, 128:256], qT[:, 0, tk], sb_old[:, :], True, True)
                vn = br.next()
                s.copy("act", vn[:, :], pvn[:, 0:128])
                yield
                po2 = s.psf.next()
                s.mm(po2[:, 128:256], kdl[:, :], vn[:, :], True, True)
                if t >= NCTX:
                    s.mm(po2[:, 0:128], AT[:, :], vn[:, :], True, True)
                s.stt(S_[:, :], S_[:, :], etot[:, t, col:col + 1], po2[:, 128:256], ALU.mult, ALU.add)
                sbn = Sb[e][d].next()
                s.copy("act", sbn[:, :], S_[:, :])
                sbc[(e, d)] = sbn
                if t >= NCTX:
                    o1 = fr.next()
                    s.ts("dve", o1[:, :], pvn[:, 128:256], eg[:, t, col:col + 1], None, ALU.mult)
                    if (e, t) not in accw:
                        accw.add((e, t))
                        s.tt("dve", acc[e][:, t, :], o1[:, :], po2[:, 0:128], ALU.add)
                    else:
                        s.tt("dve", o1[:, :], o1[:, :], po2[:, 0:128], ALU.add)
                        s.tt("pool", acc[e][:, t, :], acc[e][:, t, :], o1[:, :], ALU.add)

            for hk in range(int(os.environ.get("GDNHK", "16"))):
                s.dma("sp", kT[:, 0, :], mixT[16 + hk, :, :].k(16 + hk))
                s.dma("sp", qT[:, 0, :], mixT[hk, :, :].k(hk))
                for t in range(NT):
                    pp = s.psb.next()
                    s.tr(pp[:, 0:128], kT[:, 0, t * 128:(t + 1) * 128], s.identb[:, :])
                    s.copy("act", Ktm[:, t, :], pp[:, 0:128])
                for e in range(2):
                    hv = hk * 2 + e
                    s.dma("sp", vT[e][:, 0, :], mixT[32 + hv, :, :].k(32 + hv))
                    s.dma("sp", ztm[e][:, :, :], View(s.proj, pv[:, :, hv * 128:(hv + 1) * 128],
                                                      [(t, (hv // 4) * 512) for t in range(NT)]))
                    for t in range(NT):
                        pp = s.psb.next()
                        s.tr(pp[:, 0:128], vT[e][:, 0, t * 128:(t + 1) * 128], s.identb[:, :])
                        s.copy("dve", Vtm[e][:, t, :], pp[:, 0:128])
                orders = {0: list(range(NT)), 1: [1, 0] + list(range(NT - 1, NCTX - 1, -1))}
                sbc = {}
                accw = set()
                for e in range(2):
                    for d in range(2):
                        s.memset("dve", S32[e][d][:, :], 0.0)
                        sbc[(e, d)] = Sb[e][d].next()
                        s.memset("pool", sbc[(e, d)][:, :], 0.0)
                for step in range(NT):
                    gens = []
                    for d in range(2):
                        t = orders[d][step]
                        tk = slice(t * 128, (t + 1) * 128)
                        sh = shr.next()
                        pG = s.psf.next()
                        s.mm(pG[:, 0:128], kT[:, 0, tk], kT[:, 0, tk], True, True)
                        s.mm(pG[:, 128:256], kT[:, 0, tk], qT[:, 0, tk], True, True)
                        for j in range(4):
                            s.tt("dve", sh[:, j, :], pG[:, 0:128], masks[:, (6 if d == 0 else 10) + j, :], ALU.mult)
                        s.tt("dve", sh[:, 4, :], pG[:, 128:256], masks[:, M_UI if d == 0 else M_LI, :], ALU.mult)
                        for e in range(2):
                            gens.append(unit(hk, e, d, t, sh, sbc, accw))
                    while gens:
                        for g_ in list(gens):
                            try:
                                next(g_)
                            except StopIteration:
                                gens.remove(g_)
                for e in range(2):
                    hv = hk * 2 + e
                    for t in range(NCTX, NT):
                        sm = s.small.next()
                        s.act(junk[:, :], acc[e][:, t, :], AF.Square, accum=sm[:, 0:1])
                        s.ts("dve", sm[:, 1:2], sm[:, 0:1], 1.0 / 128, EPS, ALU.mult, ALU.add)
                        s.act(sm[:, 2:3], sm[:, 1:2], AF.Sqrt)
                        s.recip(sm[:, 3:4], sm[:, 2:3])
                        s.stt(acc[e][:, t, :], acc[e][:, t, :], sm[:, 3:4], ngb[:, :], ALU.mult, ALU.mult)
                        s.tt("pool", oh[e][:, t, :], acc[e][:, t, :], ztm[e][:, t, :], ALU.mult)
                    s.dma("pool", View(s.omix, ov[:, NCTX:NT, hv * 128:(hv + 1) * 128], list(range(NCTX, NT))),
                          oh[e][:, NCTX:NT, :])
        s.barrier()
        s.outproj_phase(li, "gdn_w_out", 4096, False)

    def build(s):
        s.cast_weights(s.layers[0])
        s.ada_phase()
        for i, li in enumerate(s.layers):
            kind = li % 4
            want_ctx = li < 2
            if i + 1 < len(s.layers):
                s.cast_weights(s.layers[i + 1])
            if kind == 3:
                s.sgu_mixer(li)
            elif kind == 1:
                s.att_mixer(li)
            elif kind == 0:
                s.ret_mixer(li)
            elif kind == 2:
                s.gdn_mixer(li)
            s.mlp_phase(li, want_ctx)
        s.final_phase()


def make_consts():
    c = {}
    c["ident"] = np.eye(128, dtype=np.float32)
    i = np.arange(128)[:, None]
    j = np.arange(128)[None, :]
    NEG = -30000.0
    mprev = np.where(j >= i, 0.0, NEG)
    mnext = np.where(j <= i, 0.0, NEG)
    c["mask3"] = np.concatenate([mprev, np.zeros((128, 128)), mnext], axis=1).astype(np.float32)

    def rope(hd, rep):
        rows = 2048 // 64
        row = np.repeat(np.arange(rows, dtype=np.float32), 64)
        col = np.tile(np.arange(64, dtype=np.float32), rows)
        ad = hd // 2
        inv = np.exp(np.float32(-math.log(10000.0)) * np.arange(0, ad, 2, dtype=np.float32) / np.float32(ad)).astype(np.float32)
        ang = np.concatenate([row[:, None] * inv, col[:, None] * inv], axis=-1).astype(np.float32)
        cs = np.cos(ang).astype(np.float32)
        sn = np.sin(ang).astype(np.float32)
        cs = np.tile(cs, (1, rep)).reshape(16, 128, -1)
        sn = np.tile(sn, (1, rep)).reshape(16, 128, -1)
        return np.ascontiguousarray(cs), np.ascontiguousarray(sn)
    c["cos_a"], c["sin_a"] = rope(128, 4)
    c["cos_r"], c["sin_r"] = rope(256, 2)
    tt_ = np.arange(128)[:, None]
    cc_ = np.arange(128)[None, :]
    lo = tt_ > cc_
    F = [lo & (tt_ // 16 == cc_ // 16),
         lo & (tt_ // 32 == cc_ // 32) & (tt_ // 16 != cc_ // 16),
         lo & (tt_ // 64 == cc_ // 64) & (tt_ // 32 != cc_ // 32),
         lo & (tt_ // 64 != cc_ // 64)]
    Bm = [f.T for f in F]
    c["gdn_masks"] = np.stack([tt_ <= cc_, tt_ >= cc_, tt_ > cc_, tt_ < cc_, cc_ >= tt_, cc_ <= tt_] + F + Bm).astype(np.float32)
    lg = np.log1p(-np.exp2(-5.0 - np.arange(8, dtype=np.float32))).astype(np.float32)
    pos = np.arange(128, dtype=np.float32)
    diff = np.abs(pos[:, None] - pos[None, :])
    Dm = np.exp(lg[:, None, None] * diff[None]).astype(np.float32)
    Dm[:, np.arange(128), np.arange(128)] = 2.0
    c["ret_D"] = np.ascontiguousarray(Dm)
    qd = np.stack([np.exp(lg[:, None] * (pos + 1.0)[None]), np.exp(lg[:, None] * (128.0 - pos)[None])], axis=1)
    c["ret_qdec"] = np.ascontiguousarray(qd.astype(np.float32))
    kd = np.stack([np.exp(lg[:, None] * (127.0 - pos)[None]), np.exp(lg[:, None] * pos[None])], axis=1)
    c["ret_kdec"] = np.ascontiguousarray(kd.astype(np.float32).transpose(2, 0, 1).reshape(128, 16))
    return c


_CACHE = {}


def kernel(**inputs):
    layers = inputs.pop("_layers", (0, 1, 2, 3))
    ncores = inputs.pop("_ncores", 8)
    consts = make_consts()
    key = (tuple(layers), ncores)
    if key not in _CACHE:
        _CACHE[key] = Model(layers=layers, consts=consts)
    m = _CACHE[key]
    x = np.asarray(inputs["x"], dtype=np.float32)
    ctx = np.asarray(inputs["ctx"], dtype=np.float32)
    c = np.asarray(inputs["c"], dtype=np.float32)
    cc = np.asarray(inputs["c_ctx"], dtype=np.float32)
    shared = {k: np.ascontiguousarray(np.asarray(inputs[k], dtype=np.float32)) for k in IN_SHAPES}
    shared.update(consts)
    in_maps = []
    for b in range(ncores):
        d = dict(shared)
        d["xin"] = np.ascontiguousarray(np.concatenate([ctx[b], x[b]], axis=0))
        d["c2"] = np.ascontiguousarray(np.stack([c[b], cc], axis=0))
        in_maps.append(d)
    if os.environ.get("KTRACE") == "1":
        res = run_bass_kernel_spmd(m.nc, in_maps, core_ids=list(range(ncores)), trace=True)
        print("EXEC_NS", res.exec_time_ns)
    else:
        res = run_bass_kernel_spmd(m.nc, in_maps, core_ids=list(range(ncores)))
    out = np.stack([np.asarray(r["y"], dtype=np.float32) for r in res.results], axis=0)
    return out
```

```python
import contextlib
import math
import os
import numpy as np
import concourse.bass as bass
import concourse.mybir as mybir
from concourse.bass_utils import run_bass_kernel_spmd

F32 = mybir.dt.float32
BF16 = mybir.dt.bfloat16
F32R = mybir.dt.float32r
AF = mybir.ActivationFunctionType
ALU = mybir.AluOpType
AX = mybir.AxisListType

D = 2048
KC = 16
NT = 18
NCTX = 2
EPS = 1e-6
HID = 8192


class View:
    __slots__ = ("tl", "ap", "key")

    def __init__(s, tl, ap, key="*"):
        s.tl = tl
        s.ap = ap
        s.key = key

    def k(s, key):
        return View(s.tl, s.ap, key)


class Tl:
    def __init__(s, base, name):
        s.base = base
        s.name = name
        s.st = {}
        s.psum = False

    def __getitem__(s, idx):
        return View(s, s.base[idx])

    def v(s, ap, key="*"):
        return View(s, ap, key)


class _Sub:
    def __init__(s, tl, t):
        s.tl = tl
        s.t = t

    def __getitem__(s, idx):
        return View(s.tl, s.tl.base[:, s.t, :][idx])


class Ring:
    def __init__(s, tls):
        s.tls = tls
        s.i = 0

    def next(s):
        t = s.tls[s.i % len(s.tls)]
        s.i += 1
        return t


class Bld:
    def __init__(s):
        s.nc = bass.Bass("TRN2", target_bir_lowering=False)
        nc = s.nc
        s.es = contextlib.ExitStack()
        s.E = {"pe": nc.tensor, "act": nc.scalar, "dve": nc.vector, "pool": nc.gpsimd, "sp": nc.sync}
        s.sem = {}
        s.cnt = {}
        for e in ["pe", "act", "dve", "pool"]:
            s.sem[e] = s.es.enter_context(nc.semaphore("s_" + e))
            s.cnt[e] = 0
        s.dq = {}
        for q, n in [("sp", 10), ("pool", 16), ("act", 2)]:
            names = []
            for i in range(n):
                nm = "d_%s%d" % (q, i)
                s.sem[nm] = s.es.enter_context(nc.semaphore(nm))
                s.cnt[nm] = 0
                names.append(nm)
            s.dq[q] = names
        s.dqi = {"sp": 0, "pool": 0, "act": 0}
        s.known = {e: {} for e in s.E}
        s.ninst = 0
        s.uid = 0

    def sb(s, shape, dt, name=None, stack=None):
        s.uid += 1
        nm = "%s_%d" % (name or "t", s.uid)
        t = (stack or s.es).enter_context(s.nc.sbuf_tensor(nm, list(shape), dt))
        return Tl(t, nm)

    def ps(s, shape, dt, name=None):
        s.uid += 1
        nm = "%s_%d" % (name or "p", s.uid)
        t = s.es.enter_context(s.nc.psum_tensor(nm, list(shape), dt))
        tl = Tl(t, nm)
        tl.psum = True
        return tl

    def dram(s, name, shape, dt, kind="Internal"):
        t = s.nc.dram_tensor(name, list(shape), dt, kind=kind)
        return Tl(t.ap(), name)

    def _states(s, v):
        st = v.tl.st
        if "*" not in st:
            st["*"] = [{}, {}]
        if v.key == "*":
            return list(st.values())
        keys = v.key if isinstance(v.key, list) else [v.key]
        out = [st["*"]]
        for k in keys:
            if k not in st:
                st[k] = [{}, {}]
            out.append(st[k])
        return out

    def _need(s, eng, reads, writes, is_dma):
        need = {}

        def mg(d, skip):
            for src, val in d.items():
                if skip and src == eng:
                    continue
                if need.get(src, 0) < val:
                    need[src] = val

        for v in reads:
            for st in s._states(v):
                mg(st[0], False)
                if v.tl.psum:
                    mg(st[1], True)
        for v in writes:
            for st in s._states(v):
                mg(st[0], not is_dma)
                mg(st[1], not is_dma)
        if eng == "pe":
            need.pop("pe", None)
        return need

    def _wait(s, eng, need):
        kn = s.known[eng]
        for src, val in need.items():
            if kn.get(src, 0) < val:
                s.E[eng].wait_ge(s.sem[src], val)
                kn[src] = val
                s.ninst += 1

    def _upd(s, src, val, reads, writes):
        for v in reads:
            st = v.tl.st
            if v.key == "*":
                st["*"][1][src] = val
            else:
                for k in v.key if isinstance(v.key, list) else [v.key]:
                    st[k][1][src] = val
        for v in writes:
            st = v.tl.st
            if v.key == "*":
                st.clear()
                st["*"] = [{src: val}, {}]
            else:
                for k in v.key if isinstance(v.key, list) else [v.key]:
                    st[k] = [{src: val}, {}]

    def op(s, eng, fn, reads, writes):
        need = s._need(eng, reads, writes, False)
        s._wait(eng, need)
        inst = fn()
        s.cnt[eng] += 1
        inst.then_inc(s.sem[eng], 1)
        s.ninst += 1
        s._upd(eng, s.cnt[eng], reads, writes)
        return inst

    def dma(s, q, out, in_, **kw):
        need = s._need(q, [in_], [out], True)
        ring = s.dq[q]
        nm = ring[s.dqi[q] % len(ring)]
        s.dqi[q] += 1
        if s.cnt[nm] > 0:
            need[nm] = max(need.get(nm, 0), s.cnt[nm])
        s._wait(q, need)
        inst = s.E[q].dma_start(out=out.ap, in_=in_.ap, **kw)
        s.cnt[nm] += 16
        inst.then_inc(s.sem[nm], 16)
        s.ninst += 1
        s._upd(nm, s.cnt[nm], [in_], [out])

    def barrier(s, engines=None):
        for e in engines or list(s.E):
            need = {src: val for src, val in s.cnt.items() if val > 0}
            if e == "pe":
                need.pop("pe", None)
            s._wait(e, need)

    def mm(s, out, lhsT, rhs, start, stop):
        return s.op("pe", lambda: s.nc.tensor.matmul(out.ap, lhsT.ap, rhs.ap, start=start, stop=stop),
                    [lhsT, rhs], [out])

    def tr(s, out, in_, ident):
        return s.op("pe", lambda: s.nc.tensor.transpose(out.ap, in_.ap, ident.ap), [in_, ident], [out])

    def act(s, out, in_, func, bias=None, scale=None, accum=None, eng="act"):
        kw = {}
        reads = [in_]
        writes = [out]
        if bias is not None:
            if isinstance(bias, View):
                kw["bias"] = bias.ap
                reads.append(bias)
            else:
                kw["bias"] = bias
        if scale is not None:
            if isinstance(scale, View):
                kw["scale"] = scale.ap
                reads.append(scale)
            else:
                kw["scale"] = scale
        if accum is not None:
            kw["accum_out"] = accum.ap
            writes.append(accum)
        return s.op("act", lambda: s.nc.scalar.activation(out.ap, in_.ap, func, **kw), reads, writes)

    def tt(s, eng, out, in0, in1, op):
        return s.op(eng, lambda: s.E[eng].tensor_tensor(out.ap, in0.ap, in1.ap, op), [in0, in1], [out])

    def ts(s, eng, out, in0, s1, s2, op0, op1=None):
        reads = [in0]
        a1 = s1
        a2 = s2
        if isinstance(s1, View):
            reads.append(s1)
            a1 = s1.ap
        if isinstance(s2, View):
            reads.append(s2)
            a2 = s2.ap
        if op1 is None:
            return s.op(eng, lambda: s.E[eng].tensor_scalar(out.ap, in0.ap, a1, None, op0), reads, [out])
        return s.op(eng, lambda: s.E[eng].tensor_scalar(out.ap, in0.ap, a1, a2, op0, op1), reads, [out])

    def stt(s, out, in0, scalar, in1, op0, op1, eng="dve"):
        reads = [in0, in1]
        a = scalar
        if isinstance(scalar, View):
            reads.append(scalar)
            a = scalar.ap
        return s.op(eng, lambda: s.E[eng].scalar_tensor_tensor(out.ap, in0.ap, a, in1.ap, op0, op1), reads, [out])

    def copy(s, eng, out, in_):
        if eng == "act":
            return s.op("act", lambda: s.nc.scalar.copy(out.ap, in_.ap), [in_], [out])
        return s.op(eng, lambda: s.E[eng].tensor_copy(out.ap, in_.ap), [in_], [out])

    def memset(s, eng, out, val):
        return s.op(eng, lambda: s.E[eng].memset(out.ap, val), [], [out])

    def recip(s, out, in_):
        return s.op("dve", lambda: s.nc.vector.reciprocal(out.ap, in_.ap), [in_], [out])

    def rmax(s, out, in_):
        return s.op("dve", lambda: s.nc.vector.reduce_max(out.ap, in_.ap, axis=AX.X), [in_], [out])


GROUPS = [[0, 1], [2, 3, 4, 5, 6, 7], [8, 9, 10, 11, 12, 13], [14, 15, 16, 17]]
WSPEC = {
    0: [("ret_w_in", 2048, 12288), ("ret_w_out", 4096, 2048)],
    1: [("att_w_in", 2048, 3072), ("att_w_out", 2048, 2048)],
    2: [("gdn_w_in", 2048, 12416), ("gdn_w_out", 4096, 2048)],
    3: [("sgu_w_in", 2048, 8192), ("sgu_w_out", 4096, 2048)],
}
IN_SHAPES = {
    "mod_w": [4, 2048, 12288], "mod_b": [4, 12288], "norm1_g": [4, 2048], "norm2_g": [4, 2048],
    "mlp_up": [4, 2048, 8192], "mlp_down": [4, 8192, 2048], "final_g": [2048],
    "ret_w_in": [1, 2048, 12288], "ret_gn_g": [1, 4096], "ret_w_out": [1, 4096, 2048],
    "att_w_in": [1, 2048, 3072], "att_sink": [1, 16], "att_w_out": [1, 2048, 2048],
    "gdn_w_in": [1, 2048, 12416], "gdn_conv_w": [1, 4, 8192], "gdn_a_log": [1, 2, 32],
    "gdn_dt_bias": [1, 2, 32], "gdn_norm_g": [1, 128], "gdn_w_out": [1, 4096, 2048],
    "sgu_w_in": [1, 2048, 8192], "sgu_ln_g": [1, 4096], "sgu_ln_b": [1, 4096],
    "sgu_w_s": [1, 8, 128, 128], "sgu_b_s": [1, 8, 128], "sgu_w_out": [1, 4096, 2048],
}


class Model(Bld):
    def __init__(s, layers=(0, 1, 2, 3), final=True, consts=None):
        super().__init__()
        s.layers = list(layers)
        s.I = {}
        s.I["xin"] = s.dram("xin", [NT * 128, D], F32, "ExternalInput")
        s.I["c2"] = s.dram("c2", [2, D], F32, "ExternalInput")
        for k, shp in IN_SHAPES.items():
            s.I[k] = s.dram(k, shp, F32, "ExternalInput")
        for k, arr in (consts or {}).items():
            s.I[k] = s.dram(k, list(arr.shape), F32, "ExternalInput")
        s.out = s.dram("y", [16 * 128, D], F32, "ExternalOutput")
        s.xres = s.dram("xres", [NT * 128, D], F32)
        s.adav = {li: s.dram("adav%d" % li, [2, 6 * D], F32) for li in s.layers}
        s.proj = s.dram("proj", [NT * 128, 12416], BF16)
        s.projf = s.dram("projf", [NT * 128, 128], F32)
        s.omix = s.dram("omix", [NT * 128, 4096], BF16)
        s.mixT = s.dram("mixT", [64, 128, NT * 128], BF16)
        s.wb = {}
        for li in s.layers:
            for nm, k, n in WSPEC[li]:
                s.wb[nm] = s.dram(nm + "_bf", [k, n], BF16)
            s.wb["up%d" % li] = s.dram("up%d_bf" % li, [D, HID], BF16)
            s.wb["down%d" % li] = s.dram("down%d_bf" % li, [HID, D], BF16)
        s.psf = Ring([s.ps([128, 512], F32, "psf") for _ in range(6)])
        s.psb = Ring([s.ps([128, 1024], BF16, "psb") for _ in range(2)])
        s.identf = s.sb([128, 128], F32, "identf")
        s.identb = s.sb([128, 128], BF16, "identb")
        s.dma("sp", s.identf[:, :], s.I["ident"][:, :])
        s.copy("dve", s.identb[:, :], s.identf[:, :])
        s.small = Ring([s.sb([128, 8], F32, "sm") for _ in range(12)])
        s.evi = 0
        s.xsrc = s.I["xin"]
        s.build()

    def ev_eng(s):
        s.evi += 1
        return "act" if s.evi % 2 else "dve"

    def load_bc(s, dst, src_tl, ap1d, q="sp", np_=128):
        s.dma(q, dst, View(src_tl, ap1d.partition_broadcast(np_)))

    def cast_plan(s, li):
        if li not in s.layers:
            return
        lst = [(nm, s.I[nm], s.I[nm].base[0], k, n) for nm, k, n in WSPEC[li]]
        lst.insert(1, ("up%d" % li, s.I["mlp_up"], s.I["mlp_up"].base[li], D, HID))
        lst.append(("down%d" % li, s.I["mlp_down"], s.I["mlp_down"].base[li], HID, D))
        for nm, tl, src, k, n in lst:
            for r in range(0, k, 256):
                s.cast_jobs.append((s.wb[nm][r:r + 256, :].k(r), View(tl, src[r:r + 256, :])))

    def cast_some(s, n=None):
        n = len(s.cast_jobs) if n is None else min(n, len(s.cast_jobs))
        for _ in range(n):
            dst, src = s.cast_jobs.pop(0)
            s.dma("pool", dst, src)

    def ada_phase(s):
        with contextlib.ExitStack() as st:
            c2t = s.sb([2, D], F32, "c2t", st)
            sc = s.sb([2, D], F32, "sc", st)
            scT = s.sb([128, KC, 2], F32, "scT", st)
            wr = Ring([s.sb([128, KC, 512], F32, "adw", st) for _ in range(2)])
            mbr = Ring([s.sb([2, 512], F32, "mb", st) for _ in range(2)])
            orr = Ring([s.sb([2, 512], F32, "ao", st) for _ in range(2)])
            s.dma("sp", c2t[:, :], s.I["c2"][:, :])
            s.act(sc[:, :], c2t[:, :], AF.Silu)
            p = s.psf.next()
            for kc in range(KC):
                s.tr(p[:, kc * 2:(kc + 1) * 2], sc[0:2, kc * 128:(kc + 1) * 128], s.identf[0:2, 0:2])
            s.copy("dve", scT[:, :, :], View(p, p.base[:, 0:32].rearrange("p (k r) -> p k r", r=2)))
            for li in s.layers:
                wv = s.I["mod_w"].base[li].rearrange("(kc p) n -> p kc n", p=128)
                for c0 in range(0, 6 * D, 512):
                    wt = wr.next()
                    s.dma("sp", wt[:, :, :], View(s.I["mod_w"], wv[:, :, c0:c0 + 512]))
                    mb = mbr.next()
                    s.load_bc(mb[:, :], s.I["mod_b"], s.I["mod_b"].base[li, c0:c0 + 512], np_=2)
                    p = s.psf.next()
                    for kc in range(KC):
                        s.mm(p[0:2, :], scT[:, kc, :], wt[:, kc, :], kc == 0, kc == KC - 1)
                    o = orr.next()
                    s.tt("dve", o[:, :], p[0:2, :], mb[:, :], ALU.add)
                    s.dma("pool", s.adav[li][:, c0:c0 + 512], o[:, :])
        s.barrier()

    def mod_vecs(s, li, r, which, gm, sh, gt, tmp):
        a = s.adav[li]
        o = 0 if which == 1 else 3
        ng = s.I["norm1_g" if which == 1 else "norm2_g"]
        s.load_bc(sh[:, :], a, a.base[r, (o + 0) * D:(o + 1) * D])
        s.load_bc(tmp[:, :], a, a.base[r, (o + 1) * D:(o + 2) * D])
        s.load_bc(gm[:, :], ng, ng.base[li, :])
        s.stt(gm[:, :], tmp[:, :], 1.0, gm[:, :], ALU.add, ALU.mult)
        if gt is not None:
            s.load_bc(gt[:, :], a, a.base[r, (o + 2) * D:(o + 3) * D])

    def norm_tile(s, xt, gm, sh, h32, hb, junk):
        ss = s.small.next()
        s.act(junk[:, :], xt, AF.Square, accum=ss[:, 0:1])
        s.ts("dve", ss[:, 1:2], ss[:, 0:1], 1.0 / D, EPS, ALU.mult, ALU.add)
        s.act(ss[:, 2:3], ss[:, 1:2], AF.Sqrt)
        s.recip(ss[:, 3:4], ss[:, 2:3])
        s.stt(h32[:, :], xt, ss[:, 3:4], gm, ALU.mult, ALU.mult)
        s.tt("pool", hb[:, :], h32[:, :], sh, ALU.add)

    def transpose_tile(s, src, ncol, dst, kc0, tok0):
        nb = ncol // 128
        for b0 in range(0, nb, 8):
            n = min(8, nb - b0)
            p = s.psb.next()
            for j in range(n):
                s.tr(p[:, j * 128:(j + 1) * 128], src[:, (b0 + j) * 128:(b0 + j + 1) * 128], s.identb[:, :])
            s.copy(s.ev_eng(), dst[:, kc0 + b0:kc0 + b0 + n, tok0:tok0 + 128],
                   View(p, p.base[:, 0:n * 128].rearrange("p (j c) -> p j c", c=128)))

    def wview(s, nm):
        return s.wb[nm].base.rearrange("(kc p) n -> p kc n", p=128)

    def wkeys(s, k0, k1):
        return [r for r in range(0, 8192, 256) if r < k1 and r + 256 > k0]

    def gemm_tm(s, lhsT_fn, tiles, wname, K, n0, n1, evac, wring):
        kcn = K // 128
        wv = s.wview(wname)
        for c0 in range(n0, n1, 512):
            cw = min(512, n1 - c0)
            wt = wring.next()
            s.dma("sp", wt[:, 0:kcn, 0:cw], View(s.wb[wname], wv[:, :, c0:c0 + cw], s.wkeys(0, K)))
            for t in tiles:
                p = s.psf.next()
                for kc in range(kcn):
                    s.mm(p[:, 0:cw], lhsT_fn(t, kc), wt[:, kc, 0:cw], kc == 0, kc == kcn - 1)
                evac(t, c0, cw, p)

    def gemm_acc(s, lhsT_fn, tiles, wname, K, evac, wring):
        kcn = K // 128
        wv = s.wview(wname)
        assert len(tiles) <= 6
        for c0 in range(0, D, 512):
            banks = [s.psf.next() for _ in tiles]
            for hb in range(0, kcn, 16):
                wt = wring.next()
                s.dma("sp", wt[:, 0:16, :], View(s.wb[wname], wv[:, hb:hb + 16, c0:c0 + 512],
                                                  s.wkeys(hb * 128, (hb + 16) * 128)))
                for i, t in enumerate(tiles):
                    for kc in range(16):
                        s.mm(banks[i][:, :], lhsT_fn(i, hb + kc), wt[:, kc, :], hb + kc == 0, hb + kc == kcn - 1)
            for i, t in enumerate(tiles):
                evac(t, c0, banks[i])

    def resid_evac_fn(s, gt, xr, tr_):
        def evac(t, c0, p):
            xt = xr.next()
            s.dma("sp", xt[:, :], s.xsrc[t * 128:(t + 1) * 128, c0:c0 + 512].k((t, c0)))
            s.tt("dve", p[:, :], p[:, :], gt[:, c0:c0 + 512], ALU.mult)
            s.tt("dve", xt[:, :], xt[:, :], p[:, :], ALU.add)
            s.dma("pool", s.xres[t * 128:(t + 1) * 128, c0:c0 + 512].k((t, c0)), xt[:, :])
        return evac

    def xkeys(s, t):
        return [(t, c) for c in range(0, D, 512)]

    def outproj_phase(s, li, wname, K, do_ctx):
        kcn = K // 128
        with contextlib.ExitStack() as st:
            oT = s.sb([128, kcn, 6 * 128], BF16, "oT", st)
            gt = s.sb([128, D], F32, "g1", st)
            otr = Ring([s.sb([128, K], BF16, "ot", st) for _ in range(2)])
            wring = Ring([s.sb([128, 16, 512], BF16, "wo", st) for _ in range(2)])
            xr = Ring([s.sb([128, 512], F32, "xr", st) for _ in range(3)])
            tr_ = None
            a = s.adav[li]
            cur_r = None
            for g in GROUPS:
                r = 1 if g[0] < NCTX else 0
                if r == 1 and not do_ctx:
                    continue
                if r != cur_r:
                    s.load_bc(gt[:, :], a, a.base[r, 2 * D:3 * D])
                    cur_r = r
                for i, t in enumerate(g):
                    ot = otr.next()
                    s.dma("sp", ot[:, :], s.omix[t * 128:(t + 1) * 128, 0:K].k(t))
                    s.transpose_tile(ot, K, oT, 0, i * 128)
                s.gemm_acc(lambda i, kc: oT[:, kc, i * 128:(i + 1) * 128], g, wname, K,
                           s.resid_evac_fn(gt, xr, tr_), wring)
        s.xsrc = s.xres if do_ctx or True else s.xsrc
        s.barrier()

    def mlp_phase(s, li, do_ctx):
        with contextlib.ExitStack() as st:
            hT = s.sb([128, KC, 6 * 128], BF16, "hT2", st)
            upT = s.sb([128, 64, 6 * 128], BF16, "upT", st)
            gm = s.sb([128, D], F32, "gm2", st)
            sh = s.sb([128, D], F32, "sh2", st)
            gt = s.sb([128, D], F32, "g2", st)
            xtr = Ring([s.sb([128, D], F32, "xt", st) for _ in range(2)])
            hbr = Ring([s.sb([128, D], BF16, "hb", st) for _ in range(1)])
            wring = Ring([s.sb([128, 16, 512], BF16, "wm", st) for _ in range(2)])
            xr = Ring([s.sb([128, 512], F32, "xr", st) for _ in range(3)])
            tr_ = None
            r32 = Ring([s.sb([128, 512], F32, "r32", st) for _ in range(2)])
            cur_r = None
            wv = s.wview("up%d" % li)
            for g in GROUPS:
                r = 1 if g[0] < NCTX else 0
                if r == 1 and not do_ctx:
                    continue
                if r != cur_r:
                    s.mod_vecs(li, r, 2, gm, sh, gt, xtr.next())
                    cur_r = r
                ntok = len(g) * 128
                for i, t in enumerate(g):
                    xt = xtr.next()
                    s.dma("sp", xt[:, :], s.xsrc[t * 128:(t + 1) * 128, :].k(s.xkeys(t)))
                    hb = hbr.next()
                    s.norm_tile(xt[:, :], gm[:, :], sh[:, :], xt, hb, hb)
                    s.transpose_tile(hb, D, hT, 0, i * 128)
                for c0 in range(0, HID, 512):
                    wt = wring.next()
                    s.dma("sp", wt[:, :, :], View(s.wb["up%d" % li], wv[:, :, c0:c0 + 512], s.wkeys(0, D)))
                    for j in range(4):
                        hc = c0 // 128 + j
                        for tb in range(0, ntok, 512):
                            tw = min(512, ntok - tb)
                            p = s.psf.next()
                            for kc in range(KC):
                                s.mm(p[:, 0:tw], wt[:, kc, j * 128:(j + 1) * 128], hT[:, kc, tb:tb + tw],
                                     kc == 0, kc == KC - 1)
                            rr = r32.next()
                            s.act(rr[:, 0:tw], p[:, 0:tw], AF.Relu)
                            s.tt("pool" if (hc + tb // 512) % 2 else "dve", upT[:, hc, tb:tb + tw],
                                 rr[:, 0:tw], rr[:, 0:tw], ALU.mult)
                s.gemm_acc(lambda i, kc: upT[:, kc, i * 128:(i + 1) * 128], g, "down%d" % li, HID,
                           s.resid_evac_fn(gt, xr, tr_), wring)
        s.barrier()

    def n1_phase(s, li, tiles, hT, st):
        with contextlib.ExitStack() as st2:
            gm = s.sb([128, D], F32, "gm1", st2)
            sh = s.sb([128, D], F32, "sh1", st2)
            xtr = Ring([s.sb([128, D], F32, "xt", st2) for _ in range(3)])
            h32r = Ring([s.sb([128, D], F32, "h32", st2) for _ in range(2)])
            hbr = Ring([s.sb([128, D], BF16, "hb", st2) for _ in range(3)])
            cur_r = None
            for t in tiles:
                r = 1 if t < NCTX else 0
                if r != cur_r:
                    s.mod_vecs(li, r, 1, gm, sh, None, h32r.next())
                    cur_r = r
                xt = xtr.next()
                s.dma("sp", xt[:, :], s.xsrc[t * 128:(t + 1) * 128, :].k(s.xkeys(t)))
                hb = hbr.next()
                s.norm_tile(xt[:, :], gm[:, :], sh[:, :], h32r.next(), hb, hb)
                s.transpose_tile(hb, D, hT, 0, t * 128)
            s.barrier()

    def final_phase(s):
        with contextlib.ExitStack() as st:
            gm = s.sb([128, D], F32, "fg", st)
            xtr = Ring([s.sb([128, D], F32, "xt", st) for _ in range(2)])
            otr = Ring([s.sb([128, D], F32, "ot", st) for _ in range(2)])
            junk = s.sb([128, D], BF16, "junk", st)
            s.load_bc(gm[:, :], s.I["final_g"], s.I["final_g"].base[:])
            off = NCTX if os.environ.get("DBGCTX") != "1" else 0
            for t in range(off, off + 16):
                xt = xtr.next()
                s.dma("sp", xt[:, :], s.xsrc[t * 128:(t + 1) * 128, :].k(s.xkeys(t)))
                ss = s.small.next()
                s.act(junk[:, :], xt[:, :], AF.Square, accum=ss[:, 0:1])
                s.ts("dve", ss[:, 1:2], ss[:, 0:1], 1.0 / D, EPS, ALU.mult, ALU.add)
                s.act(ss[:, 2:3], ss[:, 1:2], AF.Sqrt)
                s.recip(ss[:, 3:4], ss[:, 2:3])
                ot = otr.next()
                s.stt(ot[:, :], xt[:, :], ss[:, 3:4], gm[:, :], ALU.mult, ALU.mult)
                s.dma("pool", s.out[(t - off) * 128:(t - off + 1) * 128, :], ot[:, :])
        s.barrier()

    def sgu_mixer(s, li):
        tiles = list(range(NCTX, NT))
        with contextlib.ExitStack() as st:
            hT = s.sb([128, KC, NT * 128], BF16, "hT", st)
            s.n1_phase(li, tiles, hT, st)
            with contextlib.ExitStack() as st2:
                wring = Ring([s.sb([128, 16, 512], BF16, "wi", st2) for _ in range(2)])
                zr = Ring([s.sb([128, 512], BF16, "z", st2) for _ in range(4)])

                def evac(t, c0, cw, p):
                    z = zr.next()
                    s.act(z[:, 0:cw], p[:, 0:cw], AF.Gelu)
                    s.dma("pool", s.proj[t * 128:(t + 1) * 128, c0:c0 + cw].k((t, c0)), z[:, 0:cw])
                s.gemm_tm(lambda t, kc: hT[:, kc, t * 128:(t + 1) * 128], tiles, "sgu_w_in", D, 0, 8192, evac, wring)
            s.barrier()
        with contextlib.ExitStack() as st:
            lg = s.sb([128, 4096], F32, "lng", st)
            lb = s.sb([128, 4096], F32, "lnb", st)
            s.load_bc(lg[:, :], s.I["sgu_ln_g"], s.I["sgu_ln_g"].base[0, :])
            s.load_bc(lb[:, :], s.I["sgu_ln_b"], s.I["sgu_ln_b"].base[0, :])
            wsT = s.sb([128, 8, 128], BF16, "wsT", st)
            bs = s.sb([128, 8], F32, "bs", st)
            wsf = s.sb([128, 8, 128], F32, "wsf", st)
            wsb = s.sb([128, 8, 128], BF16, "wsb", st)
            bsr = s.sb([8, 128], F32, "bsr", st)
            s.dma("sp", wsf[:, :, :], View(s.I["sgu_w_s"], s.I["sgu_w_s"].base[0].rearrange("g p q -> p g q")))
            s.copy("dve", wsb[:, :, :], wsf[:, :, :])
            for g0 in range(0, 8, 4):
                p = s.psb.next()
                for j in range(4):
                    s.tr(p[:, j * 128:(j + 1) * 128], wsb[:, g0 + j, :], s.identb[:, :])
                s.copy("dve", wsT[:, g0:g0 + 4, :], View(p, p.base[:, 0:512].rearrange("p (j c) -> p j c", c=128)))
            s.dma("sp", bsr[:, :], s.I["sgu_b_s"][0, :, :])
            p = s.psf.next()
            s.tr(p[:, 0:8], bsr[0:8, :], s.identf[0:8, 0:8])
            s.copy("dve", bs[:, :], p[:, 0:8])
            zt = Ring([s.sb([128, 8192], BF16, "zt", st) for _ in range(2)])
            vn32 = s.sb([128, 4096], F32, "vn32", st)
            vnb = s.sb([128, 4096], BF16, "vnb", st)
            junk = s.sb([128, 4096], BF16, "junk", st)
            gor = Ring([s.sb([128, 4096], BF16, "go", st) for _ in range(2)])
            for t in tiles:
                z = zt.next()
                s.dma("sp", z[:, :], s.proj[t * 128:(t + 1) * 128, 0:8192].k([(t, c) for c in range(0, 8192, 512)]))
                ss = s.small.next()
                v = z[:, 4096:8192]
                s.act(junk[:, :], v, AF.Copy, accum=ss[:, 0:1])
                s.act(junk[:, :], v, AF.Square, accum=ss[:, 1:2])
                s.ts("dve", ss[:, 2:3], ss[:, 0:1], 1.0 / 4096, None, ALU.mult)
                s.tt("dve", ss[:, 3:4], ss[:, 2:3], ss[:, 2:3], ALU.mult)
                s.stt(ss[:, 4:5], ss[:, 1:2], 1.0 / 4096, ss[:, 3:4], ALU.mult, ALU.subtract)
                s.ts("dve", ss[:, 5:6], ss[:, 4:5], EPS, None, ALU.add)
                s.act(ss[:, 6:7], ss[:, 5:6], AF.Sqrt)
                s.recip(ss[:, 7:8], ss[:, 6:7])
                s.ts("dve", vn32[:, :], v, ss[:, 2:3], ss[:, 7:8], ALU.subtract, ALU.mult)
                s.tt("pool", vn32[:, :], vn32[:, :], lg[:, :], ALU.mult)
                s.tt("dve", vnb[:, :], vn32[:, :], lb[:, :], ALU.add)
                go = gor.next()
                for g in range(8):
                    p = s.psf.next()
                    s.mm(p[:, :], wsT[:, g, :], vnb[:, g * 512:(g + 1) * 512], True, True)
                    s.stt(go[:, g * 512:(g + 1) * 512], p[:, :], bs[:, g:g + 1], z[:, g * 512:(g + 1) * 512],
                          ALU.add, ALU.mult)
                s.dma("pool", s.omix[t * 128:(t + 1) * 128, :].k(t), go[:, :])
        s.barrier()
        s.outproj_phase(li, "sgu_w_out", 4096, False)


    def rope_evac(s, p, cw, t, scale, rope, cosT, sinT, xs, tmpr, ob):
        if not rope:
            s.act(ob[:, 0:cw], p[:, 0:cw], AF.Copy, scale=scale)
            return
        s.act(xs[:, 0:cw], p[:, 0:cw], AF.Copy, scale=scale)
        h = cw // 2
        x1 = View(xs, xs.base[:, 0:cw].rearrange("p (n two) -> p n two", two=2)[:, :, 0])
        x2 = View(xs, xs.base[:, 0:cw].rearrange("p (n two) -> p n two", two=2)[:, :, 1])
        y1 = View(ob, ob.base[:, 0:cw].rearrange("p (n two) -> p n two", two=2)[:, :, 0])
        y2 = View(ob, ob.base[:, 0:cw].rearrange("p (n two) -> p n two", two=2)[:, :, 1])
        c = cosT[:, t - NCTX, 0:h]
        sn = sinT[:, t - NCTX, 0:h]
        t1, t2, t3, t4 = [tmpr.next() for _ in range(4)]
        s.tt("dve", t1[:, 0:h], x1, c, ALU.mult)
        s.tt("pool", t2[:, 0:h], x2, sn, ALU.mult)
        s.tt("dve", y1, t1[:, 0:h], t2[:, 0:h], ALU.subtract)
        s.tt("pool", t3[:, 0:h], x1, sn, ALU.mult)
        s.tt("dve", t4[:, 0:h], x2, c, ALU.mult)
        s.tt("pool", y2, t3[:, 0:h], t4[:, 0:h], ALU.add)

    def att_mixer(s, li):
        tiles = list(range(NT))
        with contextlib.ExitStack() as st:
            hT = s.sb([128, KC, NT * 128], BF16, "hT", st)
            s.n1_phase(li, tiles, hT, st)
            with contextlib.ExitStack() as st2:
                wring = Ring([s.sb([128, 16, 512], BF16, "wi", st2) for _ in range(2)])
                cosT = s.sb([128, 16, 256], F32, "cos", st2)
                sinT = s.sb([128, 16, 256], F32, "sin", st2)
                s.dma("sp", cosT[:, :, :], View(s.I["cos_a"], s.I["cos_a"].base.rearrange("t p c -> p t c")))
                s.dma("sp", sinT[:, :, :], View(s.I["sin_a"], s.I["sin_a"].base.rearrange("t p c -> p t c")))
                xsr = Ring([s.sb([128, 512], F32, "xs", st2) for _ in range(2)])
                tmpr = Ring([s.sb([128, 256], F32, "rt", st2) for _ in range(8)])
                obr = Ring([s.sb([128, 512], BF16, "ob", st2) for _ in range(4)])

                def evac(t, c0, cw, p):
                    ob = obr.next()
                    isq = c0 < 2048
                    isv = c0 >= 2560
                    s.rope_evac(p, cw, t, (128 ** -0.5) if isq else 1.0, (not isv) and t >= NCTX,
                                cosT, sinT, xsr.next(), tmpr, ob)
                    s.dma("pool", s.proj[t * 128:(t + 1) * 128, c0:c0 + cw].k((t, c0)), ob[:, 0:cw])
                s.gemm_tm(lambda t, kc: hT[:, kc, t * 128:(t + 1) * 128], tiles, "att_w_in", D, 0, 3072, evac, wring)
            s.barrier()
        pv = s.proj.base.rearrange("(t p) c -> p t c", p=128)
        ov = s.omix.base.rearrange("(t p) c -> p t c", p=128)
        allk = lambda c0: [(t, c0) for t in range(NT)]
        with contextlib.ExitStack() as st:
            mask3 = s.sb([128, 384], F32, "mask3", st)
            s.dma("sp", mask3[:, :], s.I["mask3"][:, :])
            sinkb = s.sb([128, 16], F32, "sinkb", st)
            s.load_bc(sinkb[:, :], s.I["att_sink"], s.I["att_sink"].base[0, :])
            ktm = s.sb([128, NT, 128], BF16, "ktm", st)
            kT = s.sb([128, 1, NT * 128], BF16, "kT", st)
            vtm = Ring([s.sb([128, NT, 128], BF16, "vtm", st) for _ in range(2)])
            qtmr = Ring([s.sb([128, NT, 128], BF16, "qtm", st) for _ in range(2)])
            qTr = Ring([s.sb([128, 1, NT * 128], BF16, "qT", st) for _ in range(2)])
            ohr = Ring([s.sb([128, NT, 128], BF16, "oh", st) for _ in range(2)])
            ssb = Ring([s.sb([128, 640], F32, "ssb", st) for _ in range(3)])
            pbr = Ring([s.sb([128, 640], BF16, "pb", st) for _ in range(3)])
            ptr = Ring([s.sb([128, 5, 128], BF16, "pt", st) for _ in range(3)])
            for g in range(4):
                s.cast_some(16)
                s.dma("sp", ktm[:, :, :], View(s.proj, pv[:, :, 2048 + g * 128:2048 + (g + 1) * 128], allk(2048)))
                vt = vtm.next()
                s.dma("sp", vt[:, :, :], View(s.proj, pv[:, :, 2560 + g * 128:2560 + (g + 1) * 128], allk(2560)))
                for t in range(NT):
                    s.transpose_tile(_Sub(ktm, t), 128, kT, 0, t * 128)
                for hh in range(4):
                    h = g * 4 + hh
                    qtm = qtmr.next()
                    s.dma("sp", qtm[:, :, :], View(s.proj, pv[:, :, h * 128:(h + 1) * 128], allk((h // 4) * 512)))
                    qT = qTr.next()
                    for t in range(NT):
                        s.transpose_tile(_Sub(qtm, t), 128, qT, 0, t * 128)
                    oh = ohr.next()
                    for t in range(NT):
                        qv = qT[:, 0, t * 128:(t + 1) * 128]
                        sS = ssb.next()
                        if t < NCTX:
                            pa = s.psf.next()
                            s.mm(pa[:, 0:256], qv, kT[:, 0, 0:256], True, True)
                            s.copy("act", sS[:, 0:256], pa[:, 0:256])
                            n = 256
                            ktiles = [0, 1]
                        else:
                            lo, hi = max(NCTX, t - 1), min(NT - 1, t + 1)
                            nl = hi - lo + 1
                            pa = s.psf.next()
                            s.mm(pa[:, 0:256], qv, kT[:, 0, 0:256], True, True)
                            pb_ = s.psf.next()
                            s.mm(pb_[:, 0:nl * 128], qv, kT[:, 0, lo * 128:(hi + 1) * 128], True, True)
                            s.copy("act", sS[:, 0:256], pa[:, 0:256])
                            m0 = (lo - (t - 1)) * 128
                            s.tt("dve", sS[:, 256:256 + nl * 128], pb_[:, 0:nl * 128], mask3[:, m0:m0 + nl * 128], ALU.add)
                            n = 256 + nl * 128
                            ktiles = [0, 1] + list(range(lo, hi + 1))
                        sm = s.small.next()
                        s.rmax(sm[:, 0:1], sS[:, 0:n])
                        s.ts("dve", sm[:, 1:2], sm[:, 0:1], sinkb[:, h:h + 1], -1.0, ALU.max, ALU.mult)
                        pb = pbr.next()
                        s.act(pb[:, 0:n], sS[:, 0:n], AF.Exp, bias=sm[:, 1:2], accum=sm[:, 2:3])
                        s.act(sm[:, 3:4], sinkb[:, h:h + 1], AF.Exp, bias=sm[:, 1:2])
                        s.tt("dve", sm[:, 4:5], sm[:, 2:3], sm[:, 3:4], ALU.add)
                        s.recip(sm[:, 5:6], sm[:, 4:5])
                        nk = n // 128
                        pT = ptr.next()
                        pp = s.psb.next()
                        for j in range(nk):
                            s.tr(pp[:, j * 128:(j + 1) * 128], pb[:, j * 128:(j + 1) * 128], s.identb[:, :])
                        s.copy("act", pT[:, 0:nk, :], View(pp, pp.base[:, 0:nk * 128].rearrange("p (j c) -> p j c", c=128)))
                        po = s.psf.next()
                        for j in range(nk):
                            s.mm(po[:, 0:128], pT[:, j, :], vt[:, ktiles[j], :], j == 0, j == nk - 1)
                        s.ts("dve", oh[:, t, :], po[:, 0:128], sm[:, 5:6], None, ALU.mult)
                    s.dma("pool", View(s.omix, ov[:, :, h * 128:(h + 1) * 128], list(range(NT))), oh[:, :, :])
        s.barrier()
        s.outproj_phase(li, "att_w_out", 2048, True)


    def ret_mixer(s, li):
        tiles = list(range(NT))
        with contextlib.ExitStack() as st:
            hT = s.sb([128, KC, NT * 128], BF16, "hT", st)
            s.n1_phase(li, tiles, hT, st)
            with contextlib.ExitStack() as st2:
                wring = Ring([s.sb([128, 16, 512], BF16, "wi", st2) for _ in range(2)])
                cosT = s.sb([128, 16, 256], F32, "cos", st2)
                sinT = s.sb([128, 16, 256], F32, "sin", st2)
                s.dma("sp", cosT[:, :, :], View(s.I["cos_r"], s.I["cos_r"].base.rearrange("t p c -> p t c")))
                s.dma("sp", sinT[:, :, :], View(s.I["sin_r"], s.I["sin_r"].base.rearrange("t p c -> p t c")))
                xsr = Ring([s.sb([128, 512], F32, "xs", st2) for _ in range(2)])
                tmpr = Ring([s.sb([128, 256], F32, "rt", st2) for _ in range(8)])
                obr = Ring([s.sb([128, 512], BF16, "ob", st2) for _ in range(4)])

                def evac(t, c0, cw, p):
                    ob = obr.next()
                    if c0 >= 8192:
                        s.act(ob[:, 0:cw], p[:, 0:cw], AF.Silu)
                    else:
                        isk = 2048 <= c0 < 4096
                        s.rope_evac(p, cw, t, (256 ** -0.5) if isk else 1.0, c0 < 4096 and t >= NCTX,
                                    cosT, sinT, xsr.next(), tmpr, ob)
                    s.dma("pool", s.proj[t * 128:(t + 1) * 128, c0:c0 + cw].k((t, c0)), ob[:, 0:cw])
                s.gemm_tm(lambda t, kc: hT[:, kc, t * 128:(t + 1) * 128], tiles, "ret_w_in", D, 0, 12288, evac, wring)
            s.barrier()
        pv = s.proj.base.rearrange("(t p) c -> p t c", p=128)
        ov = s.omix.base.rearrange("(t p) c -> p t c", p=128)
        allk = lambda c0: [(t, (c0 // 512) * 512) for t in range(NT)]
        lg = np.log1p(-np.exp2(-5.0 - np.arange(8, dtype=np.float32))).astype(np.float32)
        with contextlib.ExitStack() as st:
            qtm = s.sb([128, NT, 256], BF16, "qtm", st)
            ktm = s.sb([128, NT, 256], BF16, "ktm", st)
            vtm = s.sb([128, NT, 512], BF16, "vtm", st)
            gtm = s.sb([128, NT, 512], BF16, "gtm", st)
            oh = s.sb([128, NT, 512], BF16, "oh", st)
            qT = s.sb([128, 2, NT * 128], BF16, "qT", st)
            kT = s.sb([128, 2, NT * 128], BF16, "kT", st)
            sbst = s.sb([128, NT, 2, 512], BF16, "sbst", st)
            S32 = s.sb([128, 2, 512], F32, "S32", st)
            Sfb = Ring([s.sb([128, 2, 512], BF16, "Sfb", st) for _ in range(2)])
            Dm = s.sb([128, 128], F32, "Dm", st)
            qdec = s.sb([128, 2, 128], F32, "qdec", st)
            kdec = s.sb([128, 16], F32, "kdec", st)
            gng = s.sb([128, 512], F32, "gng", st)
            s.dma("sp", kdec[:, :], s.I["ret_kdec"][:, :])
            ksr = Ring([s.sb([128, 256], BF16, "ks", st) for _ in range(3)])
            ptr = Ring([s.sb([128, 128], BF16, "PT", st) for _ in range(2)])
            qfr = Ring([s.sb([128, 2, 128], BF16, "qf", st) for _ in range(4)])
            o32 = Ring([s.sb([128, 512], F32, "o32", st) for _ in range(2)])
            junk = s.sb([128, 512], BF16, "junk", st)
            for h in range(8):
                s.cast_some(8)
                cd = float(np.exp(lg[h] * np.float32(128.0)))
                s.dma("sp", qtm[:, :, :], View(s.proj, pv[:, :, h * 256:(h + 1) * 256], allk(h * 256)))
                s.dma("sp", ktm[:, :, :], View(s.proj, pv[:, :, 2048 + h * 256:2048 + (h + 1) * 256], allk(2048 + h * 256)))
                s.dma("sp", vtm[:, :, :], View(s.proj, pv[:, :, 4096 + h * 512:4096 + (h + 1) * 512], allk(4096 + h * 512)))
                s.dma("sp", gtm[:, :, :], View(s.proj, pv[:, :, 8192 + h * 512:8192 + (h + 1) * 512], allk(8192 + h * 512)))
                s.dma("sp", Dm[:, :], s.I["ret_D"][h, :, :])
                s.load_bc(qdec[:, 0, :], s.I["ret_qdec"], s.I["ret_qdec"].base[h, 0, :])
                s.load_bc(qdec[:, 1, :], s.I["ret_qdec"], s.I["ret_qdec"].base[h, 1, :])
                s.load_bc(gng[:, :], s.I["ret_gn_g"], s.I["ret_gn_g"].base[0, h * 512:(h + 1) * 512])
                for t in range(NT):
                    s.transpose_tile(_Sub(qtm, t), 256, qT, 0, t * 128)
                    s.transpose_tile(_Sub(ktm, t), 256, kT, 0, t * 128)

                def upd(t, col):
                    ks = ksr.next()
                    s.act(ks[:, :], ktm[:, t, :], AF.Copy, scale=kdec[:, col:col + 1])
                    for dc in range(2):
                        p = s.psf.next()
                        s.mm(p[:, :], ks[:, dc * 128:(dc + 1) * 128], vtm[:, t, :], True, True)
                        s.stt(S32[:, dc, :], S32[:, dc, :], cd, p[:, :], ALU.mult, ALU.add)
                s.memset("dve", S32[:, :, :], 0.0)
                for t in [1, 0] + list(range(NT - 1, NCTX - 1, -1)):
                    s.copy("act", sbst[:, t, :, :], S32[:, :, :])
                    upd(t, h * 2 + 1)
                s.memset("dve", S32[:, :, :], 0.0)
                for t in range(NT):
                    sf = Sfb.next()
                    s.copy("act", sf[:, :, :], S32[:, :, :])
                    tk = slice(t * 128, (t + 1) * 128)
                    p1 = s.psf.next()
                    for dc in range(2):
                        s.mm(p1[:, 0:128], kT[:, dc, tk], qT[:, dc, tk], dc == 0, dc == 1)
                    PT = ptr.next()
                    s.tt("dve", PT[:, :], p1[:, 0:128], Dm[:, :], ALU.mult)
                    qf = qfr.next()
                    qb = qfr.next()
                    for dc in range(2):
                        s.tt("pool", qf[:, dc, :], qT[:, dc, tk], qdec[:, 0, :], ALU.mult)
                        s.tt("dve", qb[:, dc, :], qT[:, dc, tk], qdec[:, 1, :], ALU.mult)
                    po = s.psf.next()
                    s.mm(po[:, :], PT[:, :], vtm[:, t, :], True, False)
                    for dc in range(2):
                        s.mm(po[:, :], qf[:, dc, :], sf[:, dc, :], False, False)
                    for dc in range(2):
                        s.mm(po[:, :], qb[:, dc, :], sbst[:, t, dc, :], False, dc == 1)
                    sm = s.small.next()
                    s.act(junk[:, :], po[:, :], AF.Square, accum=sm[:, 0:1])
                    s.ts("dve", sm[:, 1:2], sm[:, 0:1], 1.0 / 512, EPS, ALU.mult, ALU.add)
                    s.act(sm[:, 2:3], sm[:, 1:2], AF.Sqrt)
                    s.recip(sm[:, 3:4], sm[:, 2:3])
                    o = o32.next()
                    s.stt(o[:, :], po[:, :], sm[:, 3:4], gng[:, :], ALU.mult, ALU.mult)
                    s.tt("pool", oh[:, t, :], o[:, :], gtm[:, t, :], ALU.mult)
                    upd(t, h * 2 + 0)
                s.dma("pool", View(s.omix, ov[:, :, h * 512:(h + 1) * 512], list(range(NT))), oh[:, :, :])
        s.barrier()
        s.outproj_phase(li, "ret_w_out", 4096, True)


    def gdn_mixer(s, li):
        tiles = list(range(NT))
        XW = 2310
        mixT = s.mixT
        with contextlib.ExitStack() as st:
            hT = s.sb([128, KC, NT * 128], BF16, "hT", st)
            s.n1_phase(li, tiles, hT, st)
            with contextlib.ExitStack() as st2:
                wring = Ring([s.sb([128, 16, 512], BF16, "wi", st2) for _ in range(2)])
                X = s.sb([128, XW], F32, "X", st2)
                Y = s.sb([128, NT * 128], F32, "Y", st2)
                Z = s.sb([128, NT * 128], F32, "Z", st2)
                SQ = s.sb([128, NT * 128], F32, "SQ", st2)
                OB = Ring([s.sb([128, NT * 128], BF16, "OB", st2) for _ in range(2)])
                rn = Ring([s.sb([128, 512], F32, "rn", st2) for _ in range(2)])
                onesf = s.sb([128, 128], F32, "onesf", st2)
                s.memset("dve", onesf[:, :], 1.0)
                s.memset("dve", X[:, :], 0.0)
                cw4 = s.sb([4, 8192], F32, "cw4", st2)
                cwT = s.sb([128, 64, 4], F32, "cwT", st2)
                s.dma("sp", cw4[:, :], s.I["gdn_conv_w"][0, :, :])
                p = s.psf.next()
                for j in range(64):
                    s.tr(p[:, j * 4:(j + 1) * 4], cw4[0:4, j * 128:(j + 1) * 128], s.identf[0:4, 0:4])
                s.copy("dve", cwT[:, :, :], View(p, p.base[:, 0:256].rearrange("p (j k) -> p j k", k=4)))
                wv = s.wview("gdn_w_in")
                blocks = [(0, 256)] + [(256 + i * 512, 512) for i in range(4)]
                xcol = lambda tok: tok + 2 if tok < 256 else tok + 5
                for c0 in range(0, 8192, 512):
                    wt = wring.next()
                    s.dma("sp", wt[:, :, :], View(s.wb["gdn_w_in"], wv[:, :, c0:c0 + 512], s.wkeys(0, D)))
                    for j in range(4):
                        ch = c0 // 128 + j
                        for tb, tw in blocks:
                            p = s.psf.next()
                            for kc in range(KC):
                                s.mm(p[:, 0:tw], wt[:, kc, j * 128:(j + 1) * 128], hT[:, kc, tb:tb + tw], kc == 0, kc == KC - 1)
                            s.copy("act", X[:, xcol(tb):xcol(tb) + tw], p[:, 0:tw])
                        for (t0, n) in [(0, 256), (256, 2048)]:
                            x0 = xcol(t0)
                            s.ts("dve", Y[:, t0:t0 + n], X[:, x0 - 2:x0 - 2 + n], cwT[:, ch, 0:1], None, ALU.mult)
                            for k in range(1, 4):
                                s.stt(Y[:, t0:t0 + n], X[:, x0 - 2 + k:x0 - 2 + k + n], cwT[:, ch, k:k + 1], Y[:, t0:t0 + n],
                                      ALU.mult, ALU.add)
                        ob = OB.next()
                        if ch >= 32:
                            s.act(ob[:, :], Y[:, :], AF.Silu)
                        else:
                            s.act(Z[:, :], Y[:, :], AF.Silu)
                            s.tt("pool", SQ[:, :], Z[:, :], Z[:, :], ALU.mult)
                            for tb, tw in blocks:
                                p = s.psf.next()
                                s.mm(p[:, 0:tw], onesf[:, :], SQ[:, tb:tb + tw], True, True)
                                r = rn.next()
                                s.ts("dve", r[:, 0:tw], p[:, 0:tw], 1e-6, None, ALU.add)
                                s.act(r[:, 0:tw], r[:, 0:tw], AF.Sqrt)
                                s.recip(r[:, 0:tw], r[:, 0:tw])
                                s.stt(ob[:, tb:tb + tw], Z[:, tb:tb + tw], (128 ** -0.5) if ch < 16 else 1.0, r[:, 0:tw],
                                      ALU.mult, ALU.mult)
                        s.dma("pool", mixT[ch, :, :].k(ch), ob[:, :])
                zr = Ring([s.sb([128, 512], BF16, "z", st2) for _ in range(3)])
                fr = Ring([s.sb([128, 128], F32, "f", st2) for _ in range(2)])

                def evac(t, c0, cw, p):
                    if c0 < 12288:
                        z = zr.next()
                        s.act(z[:, 0:cw], p[:, 0:cw], AF.Silu)
                        s.dma("pool", s.proj[t * 128:(t + 1) * 128, c0 - 8192:c0 - 8192 + cw].k((t, c0 - 8192)), z[:, 0:cw])
                    else:
                        f = fr.next()
                        s.copy("dve", f[:, 0:cw], p[:, 0:cw])
                        s.dma("pool", s.projf[t * 128:(t + 1) * 128, 0:cw].k(t), f[:, 0:cw])
                s.gemm_tm(lambda t, kc: hT[:, kc, t * 128:(t + 1) * 128], tiles, "gdn_w_in", D, 8192, 12416, evac, wring)
            s.barrier()
        if os.environ.get("GDNDBG") == "1":
            s.outproj_phase(li, "gdn_w_out", 4096, False)
            return
        pv = s.proj.base.rearrange("(t p) c -> p t c", p=128)
        ov = s.omix.base.rearrange("(t p) c -> p t c", p=128)
        with contextlib.ExitStack() as st:
            f32t = lambda nm, shp: s.sb(shp, F32, nm, st)
            beta = f32t("beta", [128, NT, 64])
            nbeta = f32t("nbeta", [128, NT, 64])
            gc = f32t("gc", [128, NT, 64])
            eg = f32t("eg", [128, NT, 64])
            egl = f32t("egl", [128, NT, 64])
            etot = f32t("etot", [128, NT, 64])
            beg = f32t("beg", [128, NT, 64])
            onesf = f32t("onesf", [128, 128])
            masks = f32t("masks", [128, 14, 128])
            ngb = f32t("ngb", [128, 128])
            st_g = contextlib.ExitStack()
            f32g = lambda nm, shp: s.sb(shp, F32, nm, st_g)
            raw = f32g("raw", [128, NT, 128])
            g = f32g("g", [128, NT, 64])
            t1 = f32g("t1", [128, NT, 64])
            t2 = f32g("t2", [128, NT, 64])
            tot = f32g("tot", [128, NT, 64])
            alb = f32g("alb", [128, 64])
            dtb = f32g("dtb", [128, 64])
            nA = f32g("nA", [128, 64])
            s.dma("sp", raw[:, :, :], View(s.projf, s.projf.base.rearrange("(t p) c -> p t c", p=128), list(range(NT))))
            s.memset("dve", onesf[:, :], 1.0)
            s.dma("sp", masks[:, :, :], View(s.I["gdn_masks"], s.I["gdn_masks"].base.rearrange("k p c -> p k c")))
            s.load_bc(alb[:, :], s.I["gdn_a_log"], s.I["gdn_a_log"].base[0].rearrange("a b -> (a b)"))
            s.load_bc(dtb[:, :], s.I["gdn_dt_bias"], s.I["gdn_dt_bias"].base[0].rearrange("a b -> (a b)"))
            s.load_bc(ngb[:, :], s.I["gdn_norm_g"], s.I["gdn_norm_g"].base[0, :])
            s.act(nA[:, :], alb[:, :], AF.Exp)
            s.ts("dve", nA[:, :], nA[:, :], -1.0, None, ALU.mult)
            s.act(beta[:, :, :], raw[:, :, 0:64], AF.Sigmoid)
            s.ts("dve", nbeta[:, :, :], beta[:, :, :], -1.0, None, ALU.mult)
            for t in range(NT):
                s.tt("dve", t1[:, t, :], raw[:, t, 64:128], dtb[:, :], ALU.add)
            s.act(t2[:, :, :], t1[:, :, :], AF.Abs)
            s.act(t2[:, :, :], t2[:, :, :], AF.Exp, scale=-1.0)
            s.act(t2[:, :, :], t2[:, :, :], AF.Ln, bias=1.0)
            s.ts("dve", t1[:, :, :], t1[:, :, :], 0.0, None, ALU.max)
            s.tt("dve", t1[:, :, :], t1[:, :, :], t2[:, :, :], ALU.add)
            for t in range(NT):
                s.tt("dve", g[:, t, :], t1[:, t, :], nA[:, :], ALU.mult)
            MU_F, MU_B, M_SL, M_SU, M_UI, M_LI = range(6)
            for t in range(NT):
                p = s.psf.next()
                s.mm(p[:, 0:32], masks[:, MU_F, :], g[:, t, 0:32], True, True)
                s.mm(p[:, 32:64], masks[:, MU_B, :], g[:, t, 32:64], True, True)
                s.mm(p[:, 64:128], onesf[:, :], g[:, t, :], True, True)
                s.copy("act", gc[:, t, :], p[:, 0:64])
                s.copy("dve", tot[:, t, :], p[:, 64:128])
            s.act(eg[:, :, :], gc[:, :, :], AF.Exp)
            s.tt("dve", t1[:, :, :], tot[:, :, :], gc[:, :, :], ALU.subtract)
            s.act(egl[:, :, :], t1[:, :, :], AF.Exp)
            s.act(etot[:, :, :], tot[:, :, :], AF.Exp)
            s.tt("dve", beg[:, :, :], beta[:, :, :], eg[:, :, :], ALU.mult)
            s.barrier()
            st_g.close()
            kT = s.sb([128, 1, NT * 128], BF16, "kT", st)
            qT = s.sb([128, 1, NT * 128], BF16, "qT", st)
            Ktm = s.sb([128, NT, 128], BF16, "Ktm", st)
            vT = [s.sb([128, 1, NT * 128], BF16, "vT", st) for _ in range(2)]
            Vtm = [s.sb([128, NT, 128], BF16, "Vtm", st) for _ in range(2)]
            ztm = [s.sb([128, NT, 128], BF16, "ztm", st) for _ in range(2)]
            acc = [f32t("acc", [128, NT, 128]) for _ in range(2)]
            oh = [s.sb([128, NT, 128], BF16, "oh", st) for _ in range(2)]
            S32 = [[f32t("S32", [128, 128]) for _ in range(2)] for _ in range(2)]
            Sb = [[Ring([s.sb([128, 128], BF16, "Sb", st) for _ in range(2)]) for _ in range(2)] for _ in range(2)]
            shr = Ring([f32t("shr", [128, 5, 128]) for _ in range(4)])
            frs = [[Ring([f32t("fr", [128, 128]) for _ in range(6)]) for _ in range(2)] for _ in range(2)]
            rrs = [[Ring([f32t("rr", [128, 128]) for _ in range(11)]) for _ in range(2)] for _ in range(2)]
            brs = [[Ring([s.sb([128, 128], BF16, "br", st) for _ in range(26)]) for _ in range(2)] for _ in range(2)]
            junk = s.sb([128, 128], BF16, "junk", st)

            def unit(hk, e, d, t, sh, sbc, accw):
                fr = frs[e][d]
                rr = rrs[e][d]
                br = brs[e][d]
                tk = slice(t * 128, (t + 1) * 128)
                hv = hk * 2 + e
                col = d * 32 + hv
                gcc = gc[:, t, col:col + 1]
                M = fr.next()
                s.act(M[:, :], onesf[:, :], AF.Copy, scale=gcc)
                pR = s.psf.next()
                s.tr(pR[:, 0:128], M[:, :], s.identf[:, :])
                A1 = fr.next()
                s.ts("dve", A1[:, :], pR[:, 0:128], gcc, 0.0, ALU.subtract, ALU.max)
                B1 = fr.next()
                s.ts("dve", B1[:, :], pR[:, 0:128], gcc, 0.0, ALU.subtract, ALU.min)
                yield
                s.act(A1[:, :], A1[:, :], AF.Exp, scale=-1.0)
                s.act(B1[:, :], B1[:, :], AF.Exp)
                yield
                R_ = lambda v: View(v.tl, v.ap.bitcast(F32R), v.key)
                P = rr.next()
                s.stt(R_(P[:, :]), A1[:, :], nbeta[:, t, col:col + 1], sh[:, 0, :], ALU.mult, ALU.mult)
                Pp = [P]
                for j in range(1, 4):
                    pj = br.next()
                    s.stt(pj[:, :], A1[:, :], nbeta[:, t, col:col + 1], sh[:, j, :], ALU.mult, ALU.mult)
                    Pp.append(pj)
                AT = br.next()
                s.tt("pool", AT[:, :], B1[:, :], sh[:, 4, :], ALU.mult)
                yield
                pt = s.psf.next()
                s.tr(pt[:, 0:128], P[:, :], s.identf[:, :])
                PT = rr.next()
                s.copy("act", R_(PT[:, :]), pt[:, 0:128])
                Xt = rr.next()
                s.tt("dve", R_(Xt[:, :]), pt[:, 0:128], s.identf[:, :], ALU.add)
                X = rr.next()
                s.tt("dve", R_(X[:, :]), P[:, :], s.identf[:, :], ALU.add)
                pp = s.psb.next()
                for j in range(1, 4):
                    s.tr(pp[:, (j - 1) * 128:j * 128], Pp[j][:, :], s.identb[:, :])
                PTb = br.next()
                PTb2 = br.next()
                PTb3 = br.next()
                PTp = [PT, PTb, PTb2, PTb3]
                for j in range(1, 4):
                    s.copy("act", PTp[j][:, :], pp[:, (j - 1) * 128:j * 128])
                yield
                for lv in range(1, 4):
                    p1 = s.psf.next()
                    s.mm(p1[:, 0:128], R_(PT[:, :]), R_(P[:, :]), True, True)
                    s.mm(p1[:, 128:256], R_(P[:, :]), R_(PT[:, :]), True, True)
                    Pn = rr.next()
                    PTn = rr.next()
                    s.copy("act", R_(Pn[:, :]), p1[:, 0:128])
                    s.copy("act", R_(PTn[:, :]), p1[:, 128:256])
                    yield
                    p3 = s.psf.next()
                    s.mm(p3[:, 0:128], R_(PTn[:, :]), R_(X[:, :]), True, True)
                    s.mm(p3[:, 128:256], R_(Pn[:, :]), R_(Xt[:, :]), True, True)
                    last = lv == 3
                    if last:
                        Xb = br.next()
                        Xtb = br.next()
                        s.tt("dve", Xb[:, :], X[:, :], p3[:, 0:128], ALU.add)
                        s.tt("dve", Xtb[:, :], Xt[:, :], p3[:, 128:256], ALU.add)
                    else:
                        s.tt("dve", R_(X[:, :]), X[:, :], p3[:, 0:128], ALU.add)
                        s.tt("dve", R_(Xt[:, :]), Xt[:, :], p3[:, 128:256], ALU.add)
                    P, PT = Pn, PTn
                    yield
                for j in range(1, 4):
                    pa = s.psf.next()
                    A1s = br.next()
                    B1s = br.next()
                    if j < 3:
                        s.mm(pa[:, 0:128], PTp[j][:, :], Xb[:, :], True, True)
                    s.mm(pa[:, 128:256], Pp[j][:, :], Xtb[:, :], True, True)
                    if j < 3:
                        s.copy("act", A1s[:, :], pa[:, 0:128])
                    s.copy("act", B1s[:, :], pa[:, 128:256])
                    yield
                    pc = s.psf.next()
                    if j < 3:
                        s.mm(pc[:, 0:128], Xtb[:, :], A1s[:, :], True, True)
                    s.mm(pc[:, 128:256], Xb[:, :], B1s[:, :], True, True)
                    Xtn = br.next()
                    if j < 3:
                        Xn = br.next()
                        s.tt("dve", Xn[:, :], Xb[:, :], pc[:, 0:128], ALU.add)
                    s.tt("dve", Xtn[:, :], Xtb[:, :], pc[:, 128:256], ALU.add)
                    if j < 3:
                        Xb = Xn
                    Xtb = Xtn
                    yield
                TTb = Xtb
                vb = br.next()
                s.act(vb[:, :], Vtm[e][:, t, :], AF.Copy, scale=beta[:, t, col:col + 1])
                kbg = br.next()
                s.act(kbg[:, :], Ktm[:, t, :], AF.Copy, scale=beg[:, t, col:col + 1])
                kdl = br.next()
                s.ts("pool", kdl[:, :], Ktm[:, t, :], egl[:, t, col:col + 1], None, ALU.mult)
                yield
                pw = s.psf.next()
                s.mm(pw[:, 0:128], kbg[:, :], TTb[:, :], True, True)
                nw = br.next()
                s.act(nw[:, :], pw[:, 0:128], AF.Copy, scale=-1.0)
                yield
                S_ = S32[e][d]
                sb_old = sbc[(e, d)]
                pvn = s.psf.next()
                s.mm(pvn[:, 0:128], TTb[:, :], vb[:, :], True, False)
                s.mm(pvn[:, 0:128], nw[:, :], sb_old[:, :], False, True)
                if t >= NCTX:
                    s.mm(pvn[:, 128:256], qT[:, 0, tk], sb_old[:, :], True, True)
                vn = br.next()
                s.copy("act", vn[:, :], pvn[:, 0:128])
                yield
                po2 = s.psf.next()
                s.mm(po2[:, 128:256], kdl[:, :], vn[:, :], True, True)
                if t >= NCTX:
                    s.mm(po2[:, 0:128], AT[:, :], vn[:, :], True, True)
                s.stt(S_[:, :], S_[:, :], etot[:, t, col:col + 1], po2[:, 128:256], ALU.mult, ALU.add)
                sbn = Sb[e][d].next()
                s.copy("act", sbn[:, :], S_[:, :])
                sbc[(e, d)] = sbn
                if t >= NCTX:
                    o1 = fr.next()
                    s.ts("dve", o1[:, :], pvn[:, 128:256], eg[:, t, col:col + 1], None, ALU.mult)
                    if (e, t) not in accw:
                        accw.add((e, t))
                        s.tt("dve", acc[e][:, t, :], o1[:, :], po2[:, 0:128], ALU.add)
                    else:
                        s.tt("dve", o1[:, :], o1[:, :], po2[:, 0:128], ALU.add)
                        s.tt("pool", acc[e][:, t, :], acc[e][:, t, :], o1[:, :], ALU.add)

            for hk in range(int(os.environ.get("GDNHK", "16"))):
                s.cast_some(4)
                s.dma("sp", kT[:, 0, :], mixT[16 + hk, :, :].k(16 + hk))
                s.dma("sp", qT[:, 0, :], mixT[hk, :, :].k(hk))
                for t in range(NT):
                    pp = s.psb.next()
                    s.tr(pp[:, 0:128], kT[:, 0, t * 128:(t + 1) * 128], s.identb[:, :])
                    s.copy("act", Ktm[:, t, :], pp[:, 0:128])
                for e in range(2):
                    hv = hk * 2 + e
                    s.dma("sp", vT[e][:, 0, :], mixT[32 + hv, :, :].k(32 + hv))
                    s.dma("sp", ztm[e][:, :, :], View(s.proj, pv[:, :, hv * 128:(hv + 1) * 128],
                                                      [(t, (hv // 4) * 512) for t in range(NT)]))
                    for t in range(NT):
                        pp = s.psb.next()
                        s.tr(pp[:, 0:128], vT[e][:, 0, t * 128:(t + 1) * 128], s.identb[:, :])
                        s.copy("dve", Vtm[e][:, t, :], pp[:, 0:128])
                orders = {0: list(range(NT)), 1: [1, 0] + list(range(NT - 1, NCTX - 1, -1))}
                sbc = {}
                accw = set()
                for e in range(2):
                    for d in range(2):
                        s.memset("dve", S32[e][d][:, :], 0.0)
                        sbc[(e, d)] = Sb[e][d].next()
                        s.memset("pool", sbc[(e, d)][:, :], 0.0)
                for step in range(NT):
                    gens = []
                    for d in range(2):
                        t = orders[d][step]
                        tk = slice(t * 128, (t + 1) * 128)
                        sh = shr.next()
                        pG = s.psf.next()
                        s.mm(pG[:, 0:128], kT[:, 0, tk], kT[:, 0, tk], True, True)
                        s.mm(pG[:, 128:256], kT[:, 0, tk], qT[:, 0, tk], True, True)
                        for j in range(4):
                            s.tt("dve", sh[:, j, :], pG[:, 0:128], masks[:, (6 if d == 0 else 10) + j, :], ALU.mult)
                        s.tt("dve", sh[:, 4, :], pG[:, 128:256], masks[:, M_UI if d == 0 else M_LI, :], ALU.mult)
                        for e in range(2):
                            gens.append(unit(hk, e, d, t, sh, sbc, accw))
                    while gens:
                        for g_ in list(gens):
                            try:
                                next(g_)
                            except StopIteration:
                                gens.remove(g_)
                for e in range(2):
                    hv = hk * 2 + e
                    for t in range(NCTX, NT):
                        sm = s.small.next()
                        s.act(junk[:, :], acc[e][:, t, :], AF.Square, accum=sm[:, 0:1])
                        s.ts("dve", sm[:, 1:2], sm[:, 0:1], 1.0 / 128, EPS, ALU.mult, ALU.add)
                        s.act(sm[:, 2:3], sm[:, 1:2], AF.Sqrt)
                        s.recip(sm[:, 3:4], sm[:, 2:3])
                        s.stt(acc[e][:, t, :], acc[e][:, t, :], sm[:, 3:4], ngb[:, :], ALU.mult, ALU.mult)
                        s.tt("pool", oh[e][:, t, :], acc[e][:, t, :], ztm[e][:, t, :], ALU.mult)
                    s.dma("pool", View(s.omix, ov[:, NCTX:NT, hv * 128:(hv + 1) * 128], list(range(NCTX, NT))),
                          oh[e][:, NCTX:NT, :])
        s.barrier()
        s.outproj_phase(li, "gdn_w_out", 4096, False)

    def build(s):
        s.cast_jobs = []
        s.cast_plan(s.layers[0])
        s.cast_some()
        s.ada_phase()
        for i, li in enumerate(s.layers):
            kind = li % 4
            want_ctx = li < 2
            s.cast_some()
            if i + 1 < len(s.layers):
                s.cast_plan(s.layers[i + 1])
            if kind == 3:
                s.sgu_mixer(li)
            elif kind == 1:
                s.att_mixer(li)
            elif kind == 0:
                s.ret_mixer(li)
            elif kind == 2:
                s.gdn_mixer(li)
            s.mlp_phase(li, want_ctx)
        s.final_phase()


def make_consts():
    c = {}
    c["ident"] = np.eye(128, dtype=np.float32)
    i = np.arange(128)[:, None]
    j = np.arange(128)[None, :]
    NEG = -30000.0
    mprev = np.where(j >= i, 0.0, NEG)
    mnext = np.where(j <= i, 0.0, NEG)
    c["mask3"] = np.concatenate([mprev, np.zeros((128, 128)), mnext], axis=1).astype(np.float32)

    def rope(hd, rep):
        rows = 2048 // 64
        row = np.repeat(np.arange(rows, dtype=np.float32), 64)
        col = np.tile(np.arange(64, dtype=np.float32), rows)
        ad = hd // 2
        inv = np.exp(np.float32(-math.log(10000.0)) * np.arange(0, ad, 2, dtype=np.float32) / np.float32(ad)).astype(np.float32)
        ang = np.concatenate([row[:, None] * inv, col[:, None] * inv], axis=-1).astype(np.float32)
        cs = np.cos(ang).astype(np.float32)
        sn = np.sin(ang).astype(np.float32)
        cs = np.tile(cs, (1, rep)).reshape(16, 128, -1)
        sn = np.tile(sn, (1, rep)).reshape(16, 128, -1)
        return np.ascontiguousarray(cs), np.ascontiguousarray(sn)
    c["cos_a"], c["sin_a"] = rope(128, 4)
    c["cos_r"], c["sin_r"] = rope(256, 2)
    tt_ = np.arange(128)[:, None]
    cc_ = np.arange(128)[None, :]
    lo = tt_ > cc_
    F = [lo & (tt_ // 16 == cc_ // 16),
         lo & (tt_ // 32 == cc_ // 32) & (tt_ // 16 != cc_ // 16),
         lo & (tt_ // 64 == cc_ // 64) & (tt_ // 32 != cc_ // 32),
         lo & (tt_ // 64 != cc_ // 64)]
    Bm = [f.T for f in F]
    c["gdn_masks"] = np.stack([tt_ <= cc_, tt_ >= cc_, tt_ > cc_, tt_ < cc_, cc_ >= tt_, cc_ <= tt_] + F + Bm).astype(np.float32)
    lg = np.log1p(-np.exp2(-5.0 - np.arange(8, dtype=np.float32))).astype(np.float32)
    pos = np.arange(128, dtype=np.float32)
    diff = np.abs(pos[:, None] - pos[None, :])
    Dm = np.exp(lg[:, None, None] * diff[None]).astype(np.float32)
    Dm[:, np.arange(128), np.arange(128)] = 2.0
    c["ret_D"] = np.ascontiguousarray(Dm)
    qd = np.stack([np.exp(lg[:, None] * (pos + 1.0)[None]), np.exp(lg[:, None] * (128.0 - pos)[None])], axis=1)
    c["ret_qdec"] = np.ascontiguousarray(qd.astype(np.float32))
    kd = np.stack([np.exp(lg[:, None] * (127.0 - pos)[None]), np.exp(lg[:, None] * pos[None])], axis=1)
    c["ret_kdec"] = np.ascontiguousarray(kd.astype(np.float32).transpose(2, 0, 1).reshape(128, 16))
    return c


_CACHE = {}


def kernel(**inputs):
    layers = inputs.pop("_layers", (0, 1, 2, 3))
    ncores = inputs.pop("_ncores", 8)
    consts = make_consts()
    key = (tuple(layers), ncores)
    if key not in _CACHE:
        _CACHE[key] = Model(layers=layers, consts=consts)
    m = _CACHE[key]
    x = np.asarray(inputs["x"], dtype=np.float32)
    ctx = np.asarray(inputs["ctx"], dtype=np.float32)
    c = np.asarray(inputs["c"], dtype=np.float32)
    cc = np.asarray(inputs["c_ctx"], dtype=np.float32)
    shared = {k: np.ascontiguousarray(np.asarray(inputs[k], dtype=np.float32)) for k in IN_SHAPES}
    shared.update(consts)
    in_maps = []
    for b in range(ncores):
        d = dict(shared)
        d["xin"] = np.ascontiguousarray(np.concatenate([ctx[b], x[b]], axis=0))
        d["c2"] = np.ascontiguousarray(np.stack([c[b], cc], axis=0))
        in_maps.append(d)
    if os.environ.get("KTRACE") == "1":
        res = run_bass_kernel_spmd(m.nc, in_maps, core_ids=list(range(ncores)), trace=True)
        print("EXEC_NS", res.exec_time_ns)
    else:
        res = run_bass_kernel_spmd(m.nc, in_maps, core_ids=list(range(ncores)))
    out = np.stack([np.asarray(r["y"], dtype=np.float32) for r in res.results], axis=0)
    return out
```
